# Optimizing a Trainium2 kernel written in Bass

```python
import jax, jax.numpy as jnp
from jax import lax
import numpy as np

D_MODEL = 1024
BATCH = 4
SEQ = 4096
DEPTH = 1

N_MEM = 256
EPS = 1e-6
NEG = -1e30
ML_HEADS = 4
ML_DH = 128
ML_WIDTH = ML_HEADS * ML_DH
ML_CHUNK = 64
ML_CONV = 4
NSA_HEADS = 8
NSA_KV = 2
NSA_DH = 64
NSA_WIDTH = NSA_HEADS * NSA_DH
NSA_KV_WIDTH = NSA_KV * NSA_DH
CMP_LEN = 32
CMP_STRIDE = 16
CMP_HIDDEN = 256
SEL_LEN = 64
SEL_TOP = 16
WINDOW = 512
Q_BLOCK = 128
FORCE_BONUS = 1e3
XA_HEADS = 4
XA_DH = 128
XA_WIDTH = XA_HEADS * XA_DH
N_BRANCH = 3
BRANCH_WIDTH = 512
D_FF = 4 * D_MODEL
IN_SPLITS = (ML_WIDTH, ML_WIDTH, ML_WIDTH, ML_WIDTH, ML_HEADS, ML_HEADS,
             NSA_WIDTH, NSA_KV_WIDTH, NSA_KV_WIDTH, NSA_KV_WIDTH, NSA_KV_WIDTH, NSA_KV_WIDTH, NSA_KV_WIDTH,
             3 * NSA_HEADS, XA_WIDTH, N_BRANCH * D_MODEL)
D_IN = 4 * ML_WIDTH + 2 * ML_HEADS + NSA_WIDTH + 6 * NSA_KV_WIDTH + 3 * NSA_HEADS + XA_WIDTH + N_BRANCH * D_MODEL
ML_F_OFF = 4 * ML_WIDTH + ML_HEADS

kernel_name = 'hybrid_mlstm_nsa_memory_block'


def rmsnorm(x, g):
    xf = x.astype(jnp.float32)
    y = xf * lax.rsqrt(jnp.mean(xf * xf, axis=-1, keepdims=True) + EPS)
    return (y * g.astype(jnp.float32)).astype(x.dtype)


def alibi_slopes(n):
    return jnp.asarray(2.0 ** (-8.0 * np.arange(1, n + 1) / n), dtype=jnp.float32)


def causal_dwconv(x, w):
    K = w.shape[0]
    T = x.shape[1]
    xp = jnp.pad(x, ((0, 0), (K - 1, 0), (0, 0)))
    y = xp[:, K - 1:K - 1 + T] * w[0]
    for j in range(1, K):
        y = y + xp[:, K - 1 - j:K - 1 - j + T] * w[j]
    return y


def mlstm_chunkwise(q, k, v, i_pre, f_pre):
    f32 = jnp.float32
    B, T, H, Dh = q.shape
    L = ML_CHUNK
    NC = T // L
    def chunks(a):
        return a.astype(f32).reshape(B, NC, L, H, Dh).transpose(0, 3, 1, 2, 4)
    qc, kc, vc = chunks(q), chunks(k) * (Dh ** -0.5), chunks(v)
    li = i_pre.astype(f32).reshape(B, NC, L, H).transpose(0, 3, 1, 2)
    lf = jax.nn.log_sigmoid(f_pre.astype(f32)).reshape(B, NC, L, H).transpose(0, 3, 1, 2)
    b = jnp.cumsum(lf, axis=-1)
    g = b[..., -1]
    causal = np.tril(np.ones((L, L), dtype=bool))
    logD = jnp.where(causal, b[..., :, None] - b[..., None, :] + li[..., None, :], -jnp.inf)
    w_end = g[..., None] - b + li
    m_loc = jnp.max(w_end, axis=-1)
    e = jnp.exp(w_end - m_loc[..., None])
    A = jnp.einsum('bhcs,bhcsk,bhcsv->bhckv', e, kc, vc)
    nA = jnp.einsum('bhcs,bhcsk->bhck', e, kc)

    def step(carry, xs):
        C, n, m = carry
        g_c, m_c, A_c, nA_c = xs
        m_new = jnp.maximum(g_c + m, m_c)
        a = jnp.exp(g_c + m - m_new)
        bb = jnp.exp(m_c - m_new)
        C_new = a[..., None, None] * C + bb[..., None, None] * A_c
        n_new = a[..., None] * n + bb[..., None] * nA_c
        return (C_new, n_new, m_new), (C, n, m)

    init = (jnp.zeros((B, H, Dh, Dh), f32), jnp.zeros((B, H, Dh), f32), jnp.zeros((B, H), f32))
    xs = (jnp.moveaxis(g, 2, 0), jnp.moveaxis(m_loc, 2, 0), jnp.moveaxis(A, 2, 0), jnp.moveaxis(nA, 2, 0))
    _, (C_prev, n_prev, m_prev) = lax.scan(step, init, xs)
    C_prev = jnp.moveaxis(C_prev, 0, 2)
    n_prev = jnp.moveaxis(n_prev, 0, 2)
    m_prev = jnp.moveaxis(m_prev, 0, 2)
    inter_log = b + m_prev[..., None]
    m_t = jnp.maximum(inter_log, jnp.max(logD, axis=-1))
    S = jnp.einsum('bhctd,bhcsd->bhcts', qc, kc) * jnp.exp(logD - m_t[..., None])
    sc = jnp.exp(inter_log - m_t)
    num = jnp.einsum('bhcts,bhcsv->bhctv', S, vc) + sc[..., None] * jnp.einsum('bhctk,bhckv->bhctv', qc, C_prev)
    den = jnp.sum(S, axis=-1) + sc * jnp.einsum('bhctk,bhck->bhct', qc, n_prev)
    h = num / jnp.maximum(jnp.abs(den), jnp.exp(-m_t))[..., None]
    return h.transpose(0, 2, 3, 1, 4).reshape(B, T, H, Dh)


def compress_blocks(kv, pe, w1, w2):
    B, T, G, dh = kv.shape
    nb = (T - CMP_LEN) // CMP_STRIDE + 1
    idx = np.arange(nb)[:, None] * CMP_STRIDE + np.arange(CMP_LEN)[None, :]
    blk = kv[:, idx] + pe[:, None, :]
    blk = blk.transpose(0, 1, 3, 2, 4).reshape(B, nb, G, CMP_LEN * dh)
    return jax.nn.gelu(blk @ w1) @ w2


def nsa_attention(q, kc, vc, ks, vs, kw, vw, gates):
    f32 = jnp.float32
    B, T, H, dh = q.shape
    G = kc.shape[2]
    R = H // G
    nbc = kc.shape[1]
    nbs = T // SEL_LEN
    topn = min(SEL_TOP, nbs)
    scale = dh ** -0.5
    slopes = alibi_slopes(H).reshape(G, R)
    cmp_end = jnp.asarray(np.arange(nbc) * CMP_STRIDE + CMP_LEN - 1, jnp.int32)
    cs = np.arange(nbc) * CMP_STRIDE
    js = np.arange(nbs) * SEL_LEN
    overlap = jnp.asarray(((cs[:, None] < js[None, :] + SEL_LEN) & (cs[:, None] + CMP_LEN > js[None, :])).astype(np.float32))
    qg = q.astype(f32).reshape(B, T, G, R, dh)
    gg = gates.astype(f32).reshape(B, T, G, R, 3)
    kc = kc.astype(f32)
    vc = vc.astype(f32)
    ks_blk = ks.astype(f32).reshape(B, nbs, SEL_LEN, G, dh).transpose(0, 3, 1, 2, 4)
    vs_blk = vs.astype(f32).reshape(B, nbs, SEL_LEN, G, dh).transpose(0, 3, 1, 2, 4)
    kw_pad = jnp.pad(kw.astype(f32), ((0, 0), (WINDOW, 0), (0, 0), (0, 0)))
    vw_pad = jnp.pad(vw.astype(f32), ((0, 0), (WINDOW, 0), (0, 0), (0, 0)))
    b_ix = jnp.arange(B)[:, None, None, None]
    g_ix = jnp.arange(G)[None, None, :, None]
    blk_ids = jnp.arange(nbs)

    def one_block(c):
        t0 = c * Q_BLOCK
        qb = lax.dynamic_slice_in_dim(qg, t0, Q_BLOCK, axis=1) * scale
        gb = lax.dynamic_slice_in_dim(gg, t0, Q_BLOCK, axis=1)
        tpos = t0 + jnp.arange(Q_BLOCK)
        dist_c = tpos[:, None] - cmp_end[None, :]
        s_c = jnp.einsum('bqgrd,bngd->bqgrn', qb, kc) - slopes[:, :, None] * dist_c[:, None, None, :]
        ok_c = (dist_c >= 0)[:, None, None, :]
        p_c = jax.nn.softmax(jnp.where(ok_c, s_c, NEG), axis=-1) * ok_c
        o_c = jnp.einsum('bqgrn,bngd->bqgrd', p_c, vc)
        imp = jnp.einsum('bqgrn,nj->bqgj', p_c, overlap)
        cur = tpos // SEL_LEN
        sel_ok = (blk_ids[None, :] <= cur[:, None])[:, None, :]
        forced = ((blk_ids[None, :] == 0) | (blk_ids[None, :] == cur[:, None]) | (blk_ids[None, :] == cur[:, None] - 1))[:, None, :]
        score = jnp.where(sel_ok, imp + jnp.where(forced, FORCE_BONUS, 0.0), NEG)
        _, idx = lax.top_k(score, topn)
        k_sel = ks_blk[b_ix, g_ix, idx].reshape(B, Q_BLOCK, G, topn * SEL_LEN, dh)
        v_sel = vs_blk[b_ix, g_ix, idx].reshape(B, Q_BLOCK, G, topn * SEL_LEN, dh)
        kpos = (idx[..., None] * SEL_LEN + jnp.arange(SEL_LEN)).reshape(B, Q_BLOCK, G, topn * SEL_LEN)
        dist_s = tpos[None, :, None, None] - kpos
        s_s = jnp.einsum('bqgrd,bqgkd->bqgrk', qb, k_sel) - slopes[None, None, :, :, None] * dist_s[:, :, :, None, :]
        ok_s = (dist_s >= 0)[:, :, :, None, :]
        p_s = jax.nn.softmax(jnp.where(ok_s, s_s, NEG), axis=-1)
        o_s = jnp.einsum('bqgrk,bqgkd->bqgrd', p_s, v_sel)
        k_win = lax.dynamic_slice_in_dim(kw_pad, t0, Q_BLOCK + WINDOW, axis=1)
        v_win = lax.dynamic_slice_in_dim(vw_pad, t0, Q_BLOCK + WINDOW, axis=1)
        wpos = t0 - WINDOW + jnp.arange(Q_BLOCK + WINDOW)
        dist_w = tpos[:, None] - wpos[None, :]
        ok_w = ((dist_w >= 0) & (dist_w < WINDOW) & (wpos[None, :] >= 0))[:, None, None, :]
        s_w = jnp.einsum('bqgrd,bkgd->bqgrk', qb, k_win) - slopes[:, :, None] * dist_w[:, None, None, :]
        p_w = jax.nn.softmax(jnp.where(ok_w, s_w, NEG), axis=-1)
        o_w = jnp.einsum('bqgrk,bkgd->bqgrd', p_w, v_win)
        return gb[..., 0:1] * o_c + gb[..., 1:2] * o_s + gb[..., 2:3] * o_w

    out = lax.map(one_block, jnp.arange(T // Q_BLOCK))
    return jnp.moveaxis(out, 0, 1).reshape(B, T, H * dh)


def memory_cross_attention(q, mem_n, w_mem_kv):
    f32 = jnp.float32
    B, T, _ = q.shape
    M = mem_n.shape[1]
    kv = mem_n @ w_mem_kv
    mk, mv = jnp.split(kv, 2, axis=-1)
    qh = q.astype(f32).reshape(B, T, XA_HEADS, XA_DH) * (XA_DH ** -0.5)
    mk = mk.astype(f32).reshape(B, M, XA_HEADS, XA_DH)
    mv = mv.astype(f32).reshape(B, M, XA_HEADS, XA_DH)
    p = jax.nn.softmax(jnp.einsum('bthd,bmhd->bhtm', qh, mk), axis=-1)
    return jnp.einsum('bhtm,bmhd->bthd', p, mv).reshape(B, T, XA_WIDTH)


def hybrid_layer(h, mem, g_mix, w_in, b_in, ml_conv, ml_norm_g, cmp_pe, cmp_w1, cmp_w2,
                 g_mem, w_mem_kv, w_branch, w_out, g_ffn, w_ff1, w_ff2):
    f32 = jnp.float32
    B, T, _ = h.shape
    dt = h.dtype
    u = rmsnorm(h, g_mix)
    proj = u @ w_in + b_in
    splits = [int(s) for s in np.cumsum(IN_SPLITS)[:-1]]
    (ml_q, ml_k, ml_v, ml_o, ml_i, ml_f, ns_q, ns_kc, ns_vc, ns_ks, ns_vs, ns_kw, ns_vw,
     ns_g, xa_q, mg) = jnp.split(proj, splits, axis=-1)
    qk = jax.nn.silu(causal_dwconv(jnp.concatenate([ml_q, ml_k], axis=-1), ml_conv))
    mq, mk = jnp.split(qk, 2, axis=-1)
    hml = mlstm_chunkwise(mq.reshape(B, T, ML_HEADS, ML_DH), mk.reshape(B, T, ML_HEADS, ML_DH),
                          ml_v.reshape(B, T, ML_HEADS, ML_DH), ml_i, ml_f)
    hml = (hml * lax.rsqrt(jnp.mean(hml * hml, axis=-1, keepdims=True) + EPS)).reshape(B, T, ML_WIDTH)
    y_ml = (jax.nn.sigmoid(ml_o.astype(f32)) * hml * ml_norm_g.astype(f32)).astype(dt)
    kc = compress_blocks(ns_kc.reshape(B, T, NSA_KV, NSA_DH), cmp_pe[0], cmp_w1[0], cmp_w2[0])
    vc = compress_blocks(ns_vc.reshape(B, T, NSA_KV, NSA_DH), cmp_pe[1], cmp_w1[1], cmp_w2[1])
    y_nsa = nsa_attention(ns_q.reshape(B, T, NSA_HEADS, NSA_DH), kc, vc,
                          ns_ks.reshape(B, T, NSA_KV, NSA_DH), ns_vs.reshape(B, T, NSA_KV, NSA_DH),
                          ns_kw.reshape(B, T, NSA_KV, NSA_DH), ns_vw.reshape(B, T, NSA_KV, NSA_DH),
                          jax.nn.sigmoid(ns_g)).astype(dt)
    y_xa = memory_cross_attention(xa_q, rmsnorm(mem, g_mem), w_mem_kv).astype(dt)
    ys = jnp.stack([y_ml, y_nsa, y_xa], axis=2)
    ups = jnp.einsum('btjc,jcd->btjd', ys, w_branch)
    gates = jax.nn.sigmoid(mg.reshape(B, T, N_BRANCH, D_MODEL))
    merged = jnp.sum(gates * ups, axis=2)
    h = h + merged @ w_out
    a = rmsnorm(h, g_ffn) @ w_ff1
    return h + jnp.square(jax.nn.relu(a)) @ w_ff2


def setup_inputs(seed: int = 0) -> dict:
    key = jax.random.key(seed)
    ks = jax.random.split(key, 20)
    f32 = jnp.float32

    def nrm(k, shape, scale):
        return jax.random.normal(k, shape, f32) * scale

    def gain(k, n):
        return 1.0 + 0.02 * jax.random.normal(k, (DEPTH, n), f32)

    b_in = nrm(ks[4], (DEPTH, D_IN), 0.01)
    b_in = b_in.at[:, ML_F_OFF:ML_F_OFF + ML_HEADS].add(jnp.linspace(3.0, 6.0, ML_HEADS, dtype=f32))
    return {
        'x': nrm(ks[0], (BATCH, SEQ, D_MODEL), 1.0),
        'mem': nrm(ks[1], (BATCH, N_MEM, D_MODEL), 1.0),
        'g_mix': gain(ks[2], D_MODEL),
        'w_in': nrm(ks[3], (DEPTH, D_MODEL, D_IN), D_MODEL ** -0.5),
        'b_in': b_in,
        'ml_conv': nrm(ks[5], (DEPTH, ML_CONV, 2 * ML_WIDTH), ML_CONV ** -0.5),
        'ml_norm_g': gain(ks[6], ML_WIDTH),
        'cmp_pe': nrm(ks[7], (DEPTH, 2, CMP_LEN, NSA_DH), 0.1),
        'cmp_w1': nrm(ks[8], (DEPTH, 2, CMP_LEN * NSA_DH, CMP_HIDDEN), (CMP_LEN * NSA_DH) ** -0.5),
        'cmp_w2': nrm(ks[9], (DEPTH, 2, CMP_HIDDEN, NSA_DH), CMP_HIDDEN ** -0.5),
        'g_mem': gain(ks[10], D_MODEL),
        'w_mem_kv': nrm(ks[11], (DEPTH, D_MODEL, 2 * XA_WIDTH), D_MODEL ** -0.5),
        'w_branch': nrm(ks[12], (DEPTH, N_BRANCH, BRANCH_WIDTH, D_MODEL), BRANCH_WIDTH ** -0.5),
        'w_out': nrm(ks[13], (DEPTH, D_MODEL, D_MODEL), D_MODEL ** -0.5),
        'g_ffn': gain(ks[14], D_MODEL),
        'w_ff1': nrm(ks[15], (DEPTH, D_MODEL, D_FF), D_MODEL ** -0.5),
        'w_ff2': nrm(ks[16], (DEPTH, D_FF, D_MODEL), D_FF ** -0.5),
        'g_final': 1.0 + 0.02 * jax.random.normal(ks[17], (D_MODEL,), f32),
    }


def reference(x, mem, g_mix, w_in, b_in, ml_conv, ml_norm_g, cmp_pe, cmp_w1, cmp_w2,
              g_mem, w_mem_kv, w_branch, w_out, g_ffn, w_ff1, w_ff2, g_final):
    h = x
    for l in range(DEPTH):
        h = hybrid_layer(h, mem, g_mix[l], w_in[l], b_in[l], ml_conv[l], ml_norm_g[l], cmp_pe[l],
                         cmp_w1[l], cmp_w2[l], g_mem[l], w_mem_kv[l], w_branch[l], w_out[l],
                         g_ffn[l], w_ff1[l], w_ff2[l])
    return rmsnorm(h, g_final)
```

```python
import os
import numpy as np
import ml_dtypes
from contextlib import ExitStack
import concourse.bass as bass
import concourse.mybir as mybir
from concourse.bass_utils import run_bass_kernel_spmd

F32 = mybir.dt.float32
BF16 = mybir.dt.bfloat16
AF = mybir.ActivationFunctionType
ALU = mybir.AluOpType

D = 1024
SEQ = 4096
HALF = 2048
NT = 2
L = 128 * NT
NST = SEQ // L
NSTP = NST // 2
NTILE = SEQ // 128
D_IN = 6944
EPS = 1e-6
NEGM = -30000.0
WB = 4352
NWBUF = int(os.environ.get("K_NW", "4"))

C_MLQ, C_MLK, C_MLV, C_MLO, C_MLI, C_MLF = 0, 512, 1024, 1536, 2048, 2052
C_NSQ, C_KC, C_VC, C_KS, C_VS, C_KW, C_VW, C_NSG, C_XAQ, C_MG = 2056, 2568, 2696, 2824, 2952, 3080, 3208, 3336, 3360, 3872

DEBUG = {}


class Sched:
    ENG = ["pe", "act", "dve", "pool", "sp"]

    def __init__(self, nc, same_engine_sync=True, reorder=True):
        self.nc = nc
        self.ops = []
        self.same = same_engine_sync
        self.reorder = reorder
        self.tag = "setup"

    def add(self, eng, fn, reads=(), writes=(), dma=None, cost=0.3, nbytes=0):
        self.ops.append(dict(eng=eng, fn=fn, reads=tuple(reads), writes=tuple(writes), dma=dma, tag=self.tag, cost=cost, nbytes=nbytes))

    def _schedule(self, ops):
        import heapq
        n = len(ops)
        succ = [[] for _ in range(n)]
        indeg = [0] * n
        for i, op in enumerate(ops):
            indeg[i] = len(op["alldeps"])
            for d in op["alldeps"]:
                succ[d].append(i)
        finish = [0.0] * n
        ready_t = [0.0] * n
        PRIO = os.environ.get("K_PRIO", "1") == "1"
        blevel = [0.0] * n
        if PRIO:
            for i in range(n - 1, -1, -1):
                c = ops[i]["cost"] if ops[i]["dma"] is None else (ops[i]["nbytes"] / 230e3 + 2.0)
                m = 0.0
                for s_ in succ[i]:
                    if blevel[s_] > m:
                        m = blevel[s_]
                blevel[i] = c + m
        eng_free = {e: 0.0 for e in self.ENG}
        future = {e: [] for e in self.ENG}
        avail = {e: [] for e in self.ENG}
        for i, op in enumerate(ops):
            if indeg[i] == 0:
                heapq.heappush(future[op["eng"]], (0.0, i))
        dma_free = 0.0
        order = []
        BW = float(os.environ.get("K_BW", "230")) * 1e3
        LAT = float(os.environ.get("K_LAT", "0.3"))
        while len(order) < n:
            best = None
            for e in self.ENG:
                f, a = future[e], avail[e]
                while f and f[0][0] <= eng_free[e]:
                    t_, i_ = heapq.heappop(f)
                    heapq.heappush(a, (-blevel[i_], i_) if PRIO else i_)
                if a:
                    cand = (eng_free[e], (a[0][1] if PRIO else a[0]), e, True)
                elif f:
                    cand = (f[0][0], f[0][1], e, False)
                else:
                    continue
                if best is None or cand[:2] < best[:2]:
                    best = cand
            start, i, e, from_avail = best
            if from_avail:
                heapq.heappop(avail[e])
            else:
                heapq.heappop(future[e])
            op = ops[i]
            if op["dma"] is not None:
                eng_free[e] = start + 0.08
                t0 = max(start, dma_free)
                dma_free = t0 + op["nbytes"] / BW
                finish[i] = dma_free + float(os.environ.get('K_DLAT', '2.0'))
            else:
                finish[i] = start + op["cost"]
                eng_free[e] = finish[i]
            order.append(i)
            for s_ in succ[i]:
                if op["dma"] is not None and ops[s_]["dma"] == op["dma"]:
                    ready_t[s_] = max(ready_t[s_], start)
                else:
                    lat = 0.0 if (ops[s_]["eng"] == e and op["dma"] is None) else LAT
                    ready_t[s_] = max(ready_t[s_], finish[i] + lat)
                indeg[s_] -= 1
                if indeg[s_] == 0:
                    heapq.heappush(future[ops[s_]["eng"]], (ready_t[s_], s_))
        self.est_total = max(finish) if n else 0.0
        return order

    def finalize(self, es):
        nc, ops = self.nc, self.ops
        last_w, readers = {}, {}
        for i, op in enumerate(ops):
            deps = set()
            for b in op["reads"]:
                if b in last_w:
                    deps.add(last_w[b])
            for b in op["writes"]:
                if b in last_w:
                    deps.add(last_w[b])
                deps.update(readers.get(b, ()))
            deps.discard(i)
            nd = set()
            for d in deps:
                dop = ops[d]
                if dop["dma"] is not None and op["dma"] is not None and dop["dma"] == op["dma"]:
                    nd |= dop["alldeps"]
                    nd.add(d) if dop["eng"] == op["eng"] else None
                else:
                    nd.add(d)
            op["alldeps"] = nd
            for b in op["reads"]:
                readers.setdefault(b, []).append(i)
            for b in op["writes"]:
                last_w[b] = i
                readers[b] = []
        order = self._schedule(ops) if (self.reorder and os.environ.get("K_REORDER", "1") == "1") else list(range(len(ops)))
        pos = {i: p for p, i in enumerate(order)}
        for i, op in enumerate(ops):
            assert all(pos[d] < pos[i] for d in op["alldeps"])
        needed = set()
        for i, op in enumerate(ops):
            nd = set()
            for d in op["alldeps"]:
                dop = ops[d]
                if dop["dma"] is None and op["dma"] is None and dop["eng"] == op["eng"]:
                    if op["eng"] == "pe" or not self.same:
                        continue
                if dop["dma"] is not None and op["dma"] is not None and dop["dma"] == op["dma"]:
                    continue
                nd.add(d)
            op["deps"] = nd
            needed |= nd
        sems, cnt = {}, {}

        def getsem(key):
            if key not in sems:
                sems[key] = es.enter_context(nc.semaphore("s%d" % len(sems)))
                cnt[key] = 0
            return sems[key]

        for i in order:
            op = ops[i]
            if op["dma"] is not None:
                key = ("d", op["dma"])
                getsem(key)
                cnt[key] += 16
                op["sig"] = (key, cnt[key], 16)
            elif i in needed:
                key = ("e", op["eng"])
                getsem(key)
                cnt[key] += 1
                op["sig"] = (key, cnt[key], 1)
            else:
                op["sig"] = None
        per = {e: [] for e in self.ENG}
        for i in order:
            per[ops[i]["eng"]].append(i)
        self.order = order
        self.per = per
        self.nwaits = 0
        self.nsems = len(sems)

        def run(eng_name, e):
            w = {}
            for i in per[eng_name]:
                op = ops[i]
                need = {}
                for d in op["deps"]:
                    key, val, _ = ops[d]["sig"]
                    if need.get(key, 0) < val:
                        need[key] = val
                for key, val in need.items():
                    if w.get(key, 0) < val:
                        e.wait_ge(sems[key], val)
                        w[key] = val
                        self.nwaits += 1
                ins = op["fn"](e)
                if op["sig"] is not None:
                    key, val, inc = op["sig"]
                    ins.then_inc(sems[key], inc)
            if eng_name == "sp":
                for key in sems:
                    if key[0] == "d":
                        e.wait_ge(sems[key], cnt[key])

        block = es.enter_context(nc.Block())

        @block.sync
        def _(e):
            run("sp", e)

        @block.scalar
        def _(e):
            run("act", e)

        @block.vector
        def _(e):
            run("dve", e)

        @block.gpsimd
        def _(e):
            run("pool", e)

        @block.tensor
        def _(e):
            run("pe", e)


def build(debug=False, stop_after=None):
    nc = bass.Bass("TRN2", target_bir_lowering=False)
    es = ExitStack()
    S = Sched(nc)

    def fsz(ap):
        n = 1
        for d in ap.shape[1:]:
            n *= d
        return n

    def MM(out, lhsT, rhs, start, stop, reads, writes):
        n = max(fsz(rhs), 64)
        c = n / 2400.0 * (4.0 if rhs.dtype == F32 else 1.0) + 0.035
        S.add("pe", lambda e: e.matmul(out, lhsT=lhsT, rhs=rhs, start=start, stop=stop), reads, writes, cost=c)

    def TR(out, in_, ident, reads, writes):
        S.add("pe", lambda e: e.transpose(out=out, in_=in_, identity=ident), reads, writes, cost=0.09)

    def ACT(out, in_, func, reads, writes, bias=None, scale=1.0, accum_out=None):
        kw = {}
        if bias is not None:
            kw["bias"] = bias
        if accum_out is not None:
            kw["accum_out"] = accum_out
        S.add("act", lambda e: e.activation(out=out, in_=in_, func=func, scale=scale, **kw), reads, writes, cost=fsz(in_) / 1200.0 + 0.2)

    def dvecost(eng, ap):
        return fsz(ap) / (900.0 if eng == "dve" else 400.0) + (0.07 if eng == "dve" else 0.3)

    def TT(eng, out, in0, in1, op, reads, writes):
        S.add(eng, lambda e: e.tensor_tensor(out=out, in0=in0, in1=in1, op=op), reads, writes, cost=dvecost(eng, in0))

    def TS(eng, out, in0, s1, op0, reads, writes, s2=None, op1=None):
        if op1 is None:
            S.add(eng, lambda e: e.tensor_scalar(out=out, in0=in0, scalar1=s1, scalar2=None, op0=op0), reads, writes, cost=dvecost(eng, in0))
        else:
            S.add(eng, lambda e: e.tensor_scalar(out=out, in0=in0, scalar1=s1, scalar2=s2, op0=op0, op1=op1), reads, writes, cost=dvecost(eng, in0))

    def STT(out, in0, scalar, in1, op0, op1, reads, writes):
        S.add("dve", lambda e: e.scalar_tensor_tensor(out=out, in0=in0, scalar=scalar, in1=in1, op0=op0, op1=op1), reads, writes, cost=dvecost("dve", in0) * 1.3)

    def CP(eng, out, in_, reads, writes):
        if eng == "act":
            S.add("act", lambda e: e.copy(out=out, in_=in_), reads, writes, cost=fsz(in_) / 1200.0 + 0.2)
        else:
            S.add(eng, lambda e: e.tensor_copy(out=out, in_=in_), reads, writes, cost=dvecost(eng, in_))

    def EVB(out, in_, bias_col, reads, writes, scale=None):
        if os.environ.get("K_EVB", "0") == "0":
            ACT(out, in_, AF.Identity, reads, writes, bias=bias_col, scale=(1.0 if scale is None else scale))
            return
        if scale is None:
            S.add("dve", lambda e: e.tensor_scalar(out=out, in0=in_, scalar1=bias_col, scalar2=None, op0=ALU.add), reads, writes, cost=dvecost("dve", in_))
        else:
            S.add("dve", lambda e: e.tensor_scalar(out=out, in0=in_, scalar1=scale, scalar2=bias_col, op0=ALU.mult, op1=ALU.add), reads, writes, cost=dvecost("dve", in_))

    def SIGT(out, in_, reads, writes, hbias=None, scale=1.0):
        ACT(out, in_, AF.Tanh, reads, writes, bias=hbias, scale=0.5 * scale)
        TS("dve", out, out, 0.5, ALU.mult, writes, writes, s2=0.5, op1=ALU.add)

    def RCP(out, in_, reads, writes):
        S.add("dve", lambda e: e.reciprocal(out=out, in_=in_), reads, writes, cost=dvecost("dve", in_) + 0.1)

    def SQRT(out, in_, reads, writes):
        S.add("act", lambda e: e.sqrt(out=out, in_=in_), reads, writes, cost=fsz(in_) / 1200.0 + 0.2)

    def MSET(eng, ap, val, writes, reads=()):
        S.add(eng, lambda e: e.memset(ap, val), reads, writes, cost=dvecost(eng, ap))

    def DMA(eng, out, in_, reads, writes, key, slow=False, nbm=1):
        nb = out.shape[0] * fsz(out) * (4 if out.dtype == F32 else 2) * nbm
        if slow:
            S.add(eng, lambda e: e.dma_start(out=out, in_=in_, allow_slow_non_contiguous=True), reads, writes, dma=key, nbytes=nb)
        else:
            S.add(eng, lambda e: e.dma_start(out=out, in_=in_), reads, writes, dma=key, nbytes=nb)

    def din(name, shape, dt=F32):
        return nc.dram_tensor(name, list(shape), dt, kind="ExternalInput").ap()

    xc = din("xc", [SEQ, D])
    mem = din("mem", [256, D])
    w_in = din("w_in", [D, D_IN])
    b_in = din("b_in", [1, D_IN])
    ml_conv = din("ml_conv", [4, 1024])
    ml_norm_g = din("ml_norm_g", [1, 512])
    cmp_pe = din("cmp_pe", [2, 32, 64])
    cmp_w1 = din("cmp_w1", [2, 2048, 256])
    cmp_w2 = din("cmp_w2", [2, 256, 64])
    g_mix = din("g_mix", [1, D])
    g_mem = din("g_mem", [1, D])
    g_ffn = din("g_ffn", [1, D])
    g_final = din("g_final", [1, D])
    w_mem_kv = din("w_mem_kv", [D, D])
    w_branch = din("w_branch", [3, 512, D])
    w_out = din("w_out", [D, D])
    w_ff1 = din("w_ff1", [D, 4096])
    w_ff2 = din("w_ff2", [4096, D])
    t_flag = din("t_flag", [128, 2])
    t_identf = din("t_identf", [128, 128])
    t_U = din("t_U", [128, 128])
    t_okc = din("t_okc", [128, 2, HALF], BF16)
    t_bsel = din("t_bsel", [128, 16, 64], BF16)
    t_EA = din("t_EA", [68, NTILE, 128], BF16)
    t_CA = din("t_CA", [68, 256], BF16)
    t_qaug = din("t_qaug", [4, 16, 2, 512], BF16)
    t_caus = din("t_caus", [128, 512], BF16)
    t_win = din("t_win", [128, 512], BF16)
    t_ovl = din("t_ovl", [128, 2, 63], BF16)
    y_out = nc.dram_tensor("y", [HALF, D], F32, kind="ExternalOutput").ap()

    def sb(name, shape, dt=F32):
        return es.enter_context(nc.sbuf_tensor(name, list(shape), dt))

    def ps(name, shape, dt=F32):
        return es.enter_context(nc.psum_tensor(name, list(shape), dt))

    def dbg(name, tile_ap, reads, shape):
        if not debug:
            return
        o = nc.dram_tensor("dbg_" + name, list(shape), F32, kind="ExternalOutput").ap()
        DEBUG[name] = tuple(shape)
        DMA("sp" if tile_ap.dtype == F32 else "pool", o, tile_ap, reads, [], ("dbg", name))

    psT = ps("psT", [128, 1024], BF16)

    def nextT():
        i = rot["T"] % 2 if os.environ.get("K_T2", "0") == "1" else 0
        rot["T"] += 1
        return psT[:, i * 512:(i + 1) * 512], ("psT", i)
    NPSG = int(os.environ.get("K_PSG", "4"))
    NPSC = int(os.environ.get("K_PSC", "3"))
    psG = [ps("psG%d" % i, [128, 512]) for i in range(NPSG)]
    psC = [ps("psC%d" % i, [128, 512]) for i in range(NPSC)]
    rot = {"G": 0, "C": 0, "P": 0, "W": 0, "X": 0, "S": 0, "T": 0, "N": 0, "W2": 0, "V": 0}

    resG = set()

    def nextG():
        while True:
            i = rot["G"] % NPSG
            rot["G"] += 1
            if i not in resG:
                return psG[i], ("psG", i)

    def nextC():
        i = rot["C"] % NPSC
        rot["C"] += 1
        return psC[i], ("psC", i)

    identf = sb("identf", [128, 128])
    identb = sb("identb", [128, 128], BF16)
    Uf = sb("Uf", [128, 128])
    onesf = sb("onesf", [128, 128])
    flag2 = sb("flag2", [128, 2])
    ones2 = sb("ones2", [128, 2])
    mhalf = sb("mhalf", [128, 4])
    okc = sb("okc", [128, 2, HALF], BF16)
    bsel = sb("bsel", [128, 16, 64], BF16)
    EA = sb("EA", [68, NTILE, 128], BF16)
    CA = sb("CA", [68, 256], BF16)
    caus1 = sb("caus", [128, 128], BF16)
    winm1 = sb("winm", [128, 128], BF16)
    gmix_r = sb("gmix_r", [128, D], BF16)
    gffn_r = sb("gffn_r", [128, D], BF16)
    gfin_r = sb("gfin_r", [128, D])
    hres = sb("hres", [128, NT, D])
    gmem_r = hres[:, 0, :]
    convrow = hres[0:4, 1, :]
    gml_r = sb("gml_r", [128, 512])
    biasTM = sb("biasTM", [128, 1312])
    biasTMf = sb("biasTMf", [128, 256])
    brow = sb("brow", [64, 128])
    biasFM = sb("biasFM", [128, 64])
    hbiasFM = sb("hbiasFM", [128, 64])
    convw = sb("convw", [128, 8, 4])

    def load(out_ap, in_ap, tok, eng="sp"):
        DMA(eng, out_ap, in_ap, [], [tok], tok)

    load(identf[:], t_identf, "identf")
    load(Uf[:], t_U, "Uf")
    load(flag2[:], t_flag, "flag2")
    caus = caus1[:].unsqueeze(1).to_broadcast([128, 4, 128])
    winm = winm1[:].unsqueeze(1).to_broadcast([128, 4, 128])
    load(gmix_r[:], g_mix.partition_broadcast(128).squeeze(1), "gmix_r", eng="pool")
    HR = [("hres", t, dh) for t in range(NT) for dh in range(2)]
    DMA("sp", convrow, ml_conv, [], HR, "convrow")

    def late_tables():
        load(okc[:], t_okc, "okc")
        load(bsel[:], t_bsel, "bsel")
        load(EA[:], t_EA, "EA")
        load(CA[:], t_CA, "CA")
        load(caus1[:], t_caus[:, 0:128], "caus")
        load(winm1[:], t_win[:, 0:128], "winm")
        load(gffn_r[:], g_ffn.partition_broadcast(128).squeeze(1), "gffn_r", eng="pool")
        load(gfin_r[:], g_final.partition_broadcast(128).squeeze(1), "gfin_r")
        load(gml_r[:], ml_norm_g.partition_broadcast(128).squeeze(1), "gml_r")
    bpb = b_in.partition_broadcast(128).squeeze(1)
    load(biasTM[:, 0:512], bpb[:, C_MLV:C_MLV + 512], "biasTM")
    load(biasTM[:, 512:640], bpb[:, C_VS:C_VS + 128], "biasTM")
    load(biasTM[:, 640:768], bpb[:, C_VW:C_VW + 128], "biasTM")
    load(biasTM[:, 768:776], bpb[:, C_MLI:C_MLI + 8], "biasTM")
    load(biasTM[:, 776:800], bpb[:, C_NSG:C_NSG + 24], "biasTM")
    load(biasTM[:, 800:1312], bpb[:, C_MLO:C_MLO + 512], "biasTM")
    MSET("dve", onesf[:], 1.0, ["onesf"])
    MSET("dve", ones2[:], 1.0, ["ones2"])
    MSET("dve", mhalf[:], -0.5, ["mhalf"])
    CP("dve", identb[:], identf[:], ["identf"], ["identb"])
    TS("dve", biasTMf[:], biasTM[:, 512:768], flag2[:, 0:1], ALU.mult, ["biasTM", "flag2"], ["biasTMf"])
    MSET("dve", brow[32:40, :], 0.0, ["brow0"])
    FMROW = {}
    rows_free = [i for i in range(64) if not (32 <= i < 40)]

    def newrow():
        return rows_free.pop(0)

    def brow_load(r, c0, n, dst0=0, full=True):
        DMA("sp", brow[r:r + 1, dst0:dst0 + n], b_in[0:1, c0:c0 + n], ([] if full else ["brow0"]), ["brow"], "brow")

    for c in range(4):
        r = newrow(); FMROW[("mlk", c)] = r; brow_load(r, C_MLK + c * 128, 128)
    for c in range(4):
        r = newrow(); FMROW[("mlq", c)] = r; brow_load(r, C_MLQ + c * 128, 128)
    r = newrow(); FMROW["ks"] = r; brow_load(r, C_KS, 128)
    r = newrow(); FMROW["kw"] = r; brow_load(r, C_KW, 128)
    for kv, cbase in ((0, C_KC), (1, C_VC)):
        for g in range(2):
            r = newrow(); FMROW[("kc", kv, g)] = r
            brow_load(r, cbase + g * 64, 64, 0)
            brow_load(r, cbase + g * 64, 64, 64)
    for p_ in range(4):
        r = 32 + p_
        FMROW[("nsq", p_)] = r
        brow_load(r, C_NSQ + p_ * 64, 64, 0, full=False)
        brow_load(r, C_NSQ + (p_ + 4) * 64, 64, 64, full=False)
    for c in range(4):
        r = newrow(); FMROW[("xaq", c)] = r; brow_load(r, C_XAQ + c * 128, 128)
    for c in range(24):
        r = newrow(); FMROW[("mg", c)] = r; brow_load(r, C_MG + c * 128, 128)
    r = 0
    assert r <= 64
    pg, pgt = nextG()
    TR(pg[:, 0:64], brow[:, :], identf[0:64, 0:64], ["brow0", "brow", "identf"], [pgt])
    CP("dve", biasFM[:], pg[:, 0:64], [pgt], ["biasFM"])
    nq0 = FMROW[("nsq", 0)]
    TS("dve", biasFM[:, nq0:nq0 + 8], biasFM[:, nq0:nq0 + 8], 0.125, ALU.mult, ["biasFM"], ["biasFM"])
    TS("dve", hbiasFM[:], biasFM[:], 0.5, ALU.mult, ["biasFM"], ["hbiasFM"])
    pg2, pgt2 = nextG()
    for c in range(8):
        TR(pg2[:, c * 4:(c + 1) * 4], convrow[:, c * 128:(c + 1) * 128], identf[0:4, 0:4], HR + ["identf"], [pgt2])
    CP("dve", convw[:].rearrange("p c j -> p (c j)"), pg2[:, 0:32], [pgt2], ["convw"])

    wbufs = [sb("wbuf%d" % i, [128, WB], BF16) for i in range(NWBUF)]
    WSPEC = []
    WSLOT = {}

    def wreg(key, K, pieces, zero=False):
        WSLOT[key] = len(WSPEC)
        WSPEC.append((key, K, pieces, zero))

    for kv in range(2):
        WSLOT[("w1", kv)] = len(WSPEC)
        WSPEC.append((("w1", kv), 16, "w1", kv))
    wreg("memk", 8, [(0, 512, w_mem_kv[:, 0:512])])
    wreg("memv", 8, [(0, 512, w_mem_kv[:, 512:1024])])
    wreg("mlk", 8, [(0, 512, w_in[:, C_MLK:C_MLK + 512])])
    wreg("mlv", 8, [(0, 512, w_in[:, C_MLV:C_MLV + 512])])
    wreg("cbuf", 8, [(0, 128, w_in[:, C_KS:C_KS + 128]), (128, 128, w_in[:, C_KW:C_KW + 128]),
                     (256, 64, w_in[:, C_KC:C_KC + 64]), (320, 64, w_in[:, C_KC:C_KC + 64]),
                     (384, 64, w_in[:, C_KC + 64:C_KC + 128]), (448, 64, w_in[:, C_KC + 64:C_KC + 128])])
    wreg("dbuf", 8, [(0, 64, w_in[:, C_VC:C_VC + 64]), (64, 64, w_in[:, C_VC:C_VC + 64]),
                     (128, 64, w_in[:, C_VC + 64:C_VC + 128]), (192, 64, w_in[:, C_VC + 64:C_VC + 128]),
                     (256, 128, w_in[:, C_VS:C_VS + 128]), (384, 128, w_in[:, C_VW:C_VW + 128]),
                     (512, 8, w_in[:, C_MLI:C_MLI + 8]), (520, 24, w_in[:, C_NSG:C_NSG + 24])])
    wreg("mlq", 8, [(0, 512, w_in[:, C_MLQ:C_MLQ + 512])])
    wreg("mlo", 8, [(0, 512, w_in[:, C_MLO:C_MLO + 512])])
    pcs = []
    for p_ in range(4):
        pcs.append((p_ * 128, 64, w_in[:, C_NSQ + p_ * 64:C_NSQ + (p_ + 1) * 64]))
        pcs.append((p_ * 128 + 64, 64, w_in[:, C_NSQ + (p_ + 4) * 64:C_NSQ + (p_ + 5) * 64]))
    wreg("nsq", 8, pcs)
    wreg("xaq", 8, [(0, 512, w_in[:, C_XAQ:C_XAQ + 512])])
    for j in range(3):
        for hf in range(2):
            wreg(("mg", j, hf), 8, [(0, 512, w_in[:, C_MG + j * 1024 + hf * 512:C_MG + j * 1024 + (hf + 1) * 512])])
        wreg(("br", j), 4, [(0, 1024, w_branch[j])])
    for dh in range(2):
        wreg(("out", dh), 8, [(0, 512, w_out[:, dh * 512:(dh + 1) * 512])])
    for fb in range(8):
        wreg(("ff1", fb), 8, [(0, 512, w_ff1[:, fb * 512:(fb + 1) * 512])])
    for dh in range(2):
        for fg in range(4):
            wreg(("ff2", dh, fg), 8, [(0, 512, w_ff2[fg * 1024:(fg + 1) * 1024, dh * 512:(dh + 1) * 512])])
    NSLOT = len(WSPEC)
    wscr = nc.dram_tensor("wscr", [NSLOT, 128, WB], BF16, kind="Internal").ap()
    MSET("pool", wbufs[0][:], 0.0, [("wbuf", 0)])

    def emit_conv(idx, extra_reads=()):
        key, K, pieces, zero = WSPEC[idx]
        if key in FT_KEYS:
            return
        er = list(extra_reads)
        if pieces == "w1":
            kv = zero
            src = cmp_w1[kv].rearrange("(lp two d) h -> two d lp h", two=2, d=64)
            v = wscr[idx][:, 0:4096].rearrange("p (k w) -> p k w", k=16)
            DMA("pool", v[0:64], src[1], er, [("wscr", idx)], ("wscr", idx))
            DMA("pool", v[64:128], src[0], er, [("wscr", idx)], ("wscr", idx))
            return
        W = 512 if zero else sum(n for _, n, _ in pieces)
        v = wscr[idx][:, 0:K * W].rearrange("p (k w) -> p k w", k=K)
        rd = []
        if zero:
            DMA("pool", wscr[idx][:, 0:K * W], wbufs[0][:, 0:K * W], [("wbuf", 0)] + er, [("wscrz", idx)], ("wscrz", idx))
            rd = [("wscrz", idx)]
        for off, n, src_ap in pieces:
            DMA("pool", v[:, :, off:off + n], src_ap.rearrange("(k p) n -> p k n", p=128), rd + er, [("wscr", idx)], ("wscr", idx))

    converted = set()
    FT_KEYS = set(["mlk", "mlv", "cbuf", "dbuf", ("w1", 0), ("w1", 1)])

    def wfetch(key, donetok=None):
        idx = WSLOT[key]
        _, K, pieces, zero = WSPEC[idx]
        W = 256 if pieces == "w1" else (512 if zero else sum(n for _, n, _ in pieces))
        nhead = int(os.environ.get("K_NHEAD", "0"))
        if nhead == 0:
            i = rot["W"] % NWBUF
            rot["W"] += 1
        elif S.tag in ("H_merge", "I_outproj", "J_ffn", "K_final"):
            i = nhead + rot["W2"] % (NWBUF - nhead)
            rot["W2"] += 1
        else:
            i = rot["W"] % nhead
            rot["W"] += 1
        tok = ("wbuf", i)
        extra = [donetok] if donetok else []
        if key in FT_KEYS and idx not in converted:
            converted.add(idx)
            bv = wbufs[i][:, 0:K * W].rearrange("p (k w) -> p k w", k=K)
            if pieces == "w1":
                kv = zero
                src = cmp_w1[kv].rearrange("(lp two d) h -> two d lp h", two=2, d=64)
                DMA("pool", bv[0:64], src[1], [], [tok], tok, nbm=2)
                DMA("pool", bv[64:128], src[0], [], [tok] + extra, tok, nbm=2)
            else:
                if zero:
                    MSET("pool", wbufs[i][:, 0:K * W], 0.0, [tok])
                for pi, (off, n, src_ap) in enumerate(pieces):
                    last = pi == len(pieces) - 1
                    DMA("pool", bv[:, :, off:off + n], src_ap.rearrange("(k p) n -> p k n", p=128), [], [tok] + (extra if last else []), tok, nbm=2)
            DMA("sp", wscr[idx][:, 0:K * W], wbufs[i][:, 0:K * W], [tok], [("wscr", idx)], ("wscr", idx))
        else:
            DMA("sp", wbufs[i][:, 0:K * W], wscr[idx][:, 0:K * W], [("wscr", idx)], [tok] + extra, tok)
        return wbufs[i][:, 0:K * W].rearrange("p (k w) -> p k w", k=K), tok

    ksT = sb("ksT", [128, SEQ], BF16)
    kwT = sb("kwT", [128, 1024], BF16)
    vs_aug = sb("vs_aug", [128, NTILE, 2, 65], BF16)
    vw_aug = sb("vw_aug", [128, 8, 2, 65], BF16)
    kcT = sb("kcT", [128, 256], BF16)
    vcT = sb("vcT", [128, 256], BF16)
    vcx = sb("vcx", [128, 2, 2, 128], BF16)
    mkT = sb("mkT", [128, 4, 256], BF16)
    mv_aug = sb("mv_aug", [128, 2, 4, 129], BF16)
    W2p = sb("W2p", [128, 2, 2, 2, 128], BF16)
    pep = sb("pep", [128, 2, 16], BF16)
    hb = sb("hb", [128, 2, 2])
    kcin2 = sb("kcin2", [128, 4, L + 17], BF16)
    Cst = sb("Cst", [128, 4, 129])
    Cb = sb("Cb", [128, 4, 129], BF16)
    prek = sb("prek", [128, 4, 3])
    preq = sb("preq", [128, 4, 3])

    MSET("dve", vs_aug[:], 1.0, [("vs_aug", i) for i in range(NST)])
    MSET("pool", vw_aug[:], 1.0, [("vw_aug", i) for i in range(8 // NT)])
    MSET("dve", kcT[:], 0.0, ["kcT"])
    MSET("dve", vcT[:], 0.0, ["vcT"])
    MSET("pool", kwT[:], 0.0, [("kwT", i) for i in range(8 // NT)])
    MSET("dve", mv_aug[:], 1.0, ["mv_aug"])
    MSET("dve", kcin2[:], 0.0, ["kcin2"])
    MSET("dve", Cst[:], 0.0, ["Cst"])
    MSET("dve", Cb[:], 0.0, ["Cb"])
    MSET("dve", prek[:], 0.0, [("prek", c) for c in range(4)])
    MSET("dve", preq[:], 0.0, [("preq", c) for c in range(4)])
    MSET("pool", W2p[:], 0.0, ["W2p0"])
    MSET("dve", vcx[:], 0.0, ["vcx"])
    MSET("dve", vcx[:, :, :, 64:65], 1.0, ["vcx"], ["vcx"])
    for g in range(2):
        DMA("sp", vcx[:, :, g, 65:128], t_ovl, ["vcx"], ["vcx_ovl"], "vcx_ovl")
    VCX = ["vcx", "vcx_ovl"]
    for kv in range(2):
        srcp = cmp_pe[kv].rearrange("(lp two) d -> two d lp", two=2)
        DMA("pool", pep[0:64, kv, :], srcp[1], [], ["pep"], "pep", slow=True)
        DMA("pool", pep[64:128, kv, :], srcp[0], [], ["pep"], "pep", slow=True)
        for half in range(2):
            for g in range(2):
                DMA("pool", W2p[:, kv, half, g, g * 64:(g + 1) * 64], cmp_w2[kv, half * 128:(half + 1) * 128, :], ["W2p0"], ["W2p"], "W2p")
    W2T = ["W2p0", "W2p"]
    NCONV0 = WSLOT["mlq"]
    first = [WSLOT[k] for k in ("mlk", "mlv", "cbuf", "dbuf", ("w1", 0), ("w1", 1))]
    for idx in first:
        emit_conv(idx)
    for idx in range(NSLOT):
        if WSPEC[idx][3] is True:
            emit_conv(idx)
    conv_rest = [WSLOT["memk"], WSLOT["memv"]] + [idx for idx in range(NCONV0, WSLOT[("ff1", 0)]) if WSPEC[idx][3] is not True]
    conv_ffn = list(range(WSLOT[("ff1", 0)], NSLOT))

    for kv in range(2):
        w1v, w1t = wfetch(("w1", kv))
        for half in range(2):
            pgx, pgxt = nextG()
            for lp in range(16):
                MM(pgx[:, 0:1], w1v[:, lp, half * 128:(half + 1) * 128], pep[:, kv, lp:lp + 1], lp == 0, lp == 15, [w1t, "pep"], [pgxt])
            CP("dve", hb[:, kv, half:half + 1], pgx[:, 0:1], [pgxt], ["hb"])

    stat = sb("stat", [128, 16])
    xbuf = [sb("xbuf%d" % i, [128, D]) for i in range(2)]
    xnbs = [sb("xnb%d" % i, [128, D], BF16) for i in range(2)]

    def nextX():
        i = rot["X"] % 2
        rot["X"] += 1
        return i

    def rms_scale(src_ap, src_toks, junk=None, junktok=None):
        if junk is None:
            junk, junktok = xnbs[0], ("xnb", 0)
        si = rot["S"] % 8
        rot["S"] += 1
        tok = ("stat", si)
        col = stat[:, si:si + 1]
        ACT(junk[:], src_ap, AF.Square, src_toks, [junktok, tok], accum_out=col)
        TS("dve", col, col, 1.0 / D, ALU.mult, [tok], [tok], s2=EPS, op1=ALU.add)
        TT("pool", col, col, mhalf[:, 0:1], ALU.pow, [tok, "mhalf"], [tok])
        return col, tok

    def rmsnorm_to_T(src_ap, src_toks, grep, grep_tok, dstT, dst_cols, dst_tok):
        gt_ = list(grep_tok) if isinstance(grep_tok, list) else [grep_tok]
        dst_tok = list(dst_tok) if isinstance(dst_tok, list) else [dst_tok]
        ni = rot["N"] % 2 if os.environ.get("K_XNB2", "1") == "1" else 0
        rot["N"] += 1
        xn_, xtok = xnbs[ni], ("xnb", ni)
        col, tok = rms_scale(src_ap, src_toks, xn_, xtok)
        STT(xn_[:], src_ap, col, grep[:], ALU.mult, ALU.mult, list(src_toks) + [tok] + gt_, [xtok])
        for hf in range(2):
            pt, ptt = nextT()
            for k in range(4):
                kk = hf * 4 + k
                TR(pt[:, k * 128:(k + 1) * 128], xn_[:, kk * 128:(kk + 1) * 128], identb[:], [xtok, "identb"], [ptt])
            CP("act", dstT[:, hf * 4:(hf + 1) * 4, dst_cols], pt.rearrange("p (k n) -> p k n", k=4), [ptt], dst_tok)

    aTraw = sb("aTraw", [128, 16 * L])
    aT = aTraw[:].bitcast(BF16).rearrange("p (f l) -> p f l", f=32)
    macc = aTraw[:, 8 * L:16 * L].rearrange("p (f l) -> p f l", f=8)
    memT = aT[:, 0:8, :]
    AT8 = [("aT", fc) for fc in range(8)]

    def mem_setup():
        DMA("sp", gmem_r, g_mem.partition_broadcast(128).squeeze(1), [], HR, "gmem_r")
        for mt in range(2):
            xi = nextX()
            DMA("sp", xbuf[xi][:], mem[mt * 128:(mt + 1) * 128, :], [], [("xbuf", xi)], ("xbuf", xi))
            rmsnorm_to_T(xbuf[xi][:], [("xbuf", xi)], gmem_r, HR, memT, slice(mt * 128, (mt + 1) * 128), AT8)
        wk, wt_ = wfetch("memk")
        for h in range(4):
            pgx, pgxt = nextG()
            for k in range(8):
                MM(pgx[:, 0:256], wk[:, k, h * 128:(h + 1) * 128], memT[:, k, :], k == 0, k == 7, [wt_] + AT8, [pgxt])
            CP("act", mkT[:, h, :], pgx[:, 0:256], [pgxt], ["mkT"])
        wv, wt_ = wfetch("memv")
        for mt in range(2):
            pgx, pgxt = nextG()
            for k in range(8):
                MM(pgx[:, 0:512], memT[:, k, mt * 128:(mt + 1) * 128], wv[:, k, :], k == 0, k == 7, [wt_] + AT8, [pgxt])
            CP("act", mv_aug[:, mt, :, 0:128], pgx[:, 0:512].rearrange("p (h d) -> p h d", h=4), [pgxt, "mv_aug"], ["mv_aug"])
    dbg("hb", hb[:], ["hb"], [128, 2, 2])
    dbg("biasFM", biasFM[:], ["biasFM"], [128, 64])
    dbg("convw", convw[:], ["convw"], [128, 8, 4])

    NXN = int(os.environ.get("K_NXN", "1"))
    xnTs = [sb("xnT%d" % i, [128, 8, L], BF16) for i in range(NXN)]
    XN = [xnTs[0]]
    XT = [("xnT", 0)]
    kT = sb("kT", [128, 4, L], BF16)
    qT = sb("qT", [128, 4, L], BF16)
    ktm = sb("ktm", [128, NT, 512], BF16)
    v_aug = sb("v_aug", [128, NT, 4, 129], BF16)
    og = sb("og", [128, NT, 512], BF16)
    Rms = [sb("Rm_%d" % i, [128, 4, 128]) for i in range(2)]
    otmp = Rms[0][:].rearrange("p h t -> p (h t)")
    gate = sb("gate", [128, NT, 8])
    gsig = sb("gsig", [128, NT, 24])
    nqT = sb("nqT", [128, NT, 8, 128], BF16)
    MSET("pool", nqT[:], 0.0, ["nqT"])
    xqT = sb("xqT", [128, 4, L], BF16)
    NA = sb("NA", [68, NT, 2, 512], BF16)
    yT = sb("yT", [128, 12, L], BF16)
    MSET("dve", v_aug[:], 1.0, [("v_aug", t) for t in range(NT)])

    def fm_group(wv_, wtok, col0, M=128):
        pgx, pgxt = nextG()
        for k in range(8):
            MM(pgx[0:M, 0:L], wv_[:, k, col0:col0 + M], XN[0][:, k, :], k == 0, k == 7, [wtok, XT[0]], [pgxt])
        return pgx, pgxt

    NCV = int(os.environ.get("K_NCV", "2"))
    pretmps = [sb("pretmp%d" % i, [128, L + 3]) for i in range(NCV)]
    convaccs = [sb("convacc%d" % i, [128, L]) for i in range(NCV)]

    def conv_silu(pre, pretok, pgx, pgxt, c, brow_i, wchunk, dst, dsttok):
        ci = rot["V"] % NCV
        rot["V"] += 1
        pretmp, convacc = pretmps[ci], convaccs[ci]
        ptk, cak = ("pretmp", ci), ("convacc", ci)
        pt = (pretok, c)
        CP("dve", pretmp[:, 0:3], pre[:, c, :], [pt, ptk], [ptk])
        ACT(pretmp[:, 3:3 + L], pgx[:, 0:L], AF.Identity, [pgxt, "biasFM", ptk], [ptk], bias=biasFM[:, brow_i:brow_i + 1])
        TS("dve", convacc[:], pretmp[:, 3:3 + L], convw[:, wchunk, 0:1], ALU.mult, [ptk, "convw"], [cak])
        for j in range(1, 4):
            STT(convacc[:], pretmp[:, 3 - j:3 - j + L], convw[:, wchunk, j:j + 1], convacc[:], ALU.mult, ALU.add, [ptk, "convw", cak], [cak])
        ACT(pretmp[:, 0:L], convacc[:], AF.Tanh, [cak, ptk], [ptk], scale=0.5)
        TS("dve", pretmp[:, 0:L], pretmp[:, 0:L], 0.5, ALU.mult, [ptk], [ptk], s2=0.5, op1=ALU.add)
        TT("dve", dst[:, c, :], pretmp[:, 0:L], convacc[:], ALU.mult, [ptk, cak], [dsttok])
        CP("dve", pre[:, c, :], pretmp[:, L:L + 3], [ptk], [pt])

    l1s = [sb("l1_%d" % i, [128, 4]) for i in range(2)]
    lfms = [sb("lfm_%d" % i, [128, 4]) for i in range(2)]
    expBs = [sb("expB_%d" % i, [128, 4, 128]) for i in range(2)]
    wcols = [sb("wcol_%d" % i, [128, 4]) for i in range(2)]
    ecols = [sb("ecol_%d" % i, [128, 4]) for i in range(2)]
    DT = sb("DT", [128, 4, 128], BF16)
    PT = sb("PT", [128, 512], BF16)
    qbT = sb("qbT", [128, 4, 128], BF16)
    ves = [sb("ve_%d" % i, [128, 4, 129], BF16) for i in range(2)]
    mst = sb("mst", [128, 16])
    yml = sb("yml", [128, 512], BF16)
    LNS = float(np.log(128.0 ** -0.5))

    l1_all = [sb("l1all%d" % i, [128, NT, 4]) for i in range(2)]
    lfm_all = [sb("lfmall%d" % i, [128, NT, 4]) for i in range(2)]

    def mlstm_gates(st):
        p = st % 2
        GT = [("gate", t) for t in range(NT)]
        ACT(l1_all[p][:], gate[:, :, 4:8], AF.Exp, GT, [("l1_all", p)], scale=-1.0)
        ACT(l1_all[p][:], l1_all[p][:], AF.Ln, [("l1_all", p)], [("l1_all", p)], bias=1.0)
        TS("dve", lfm_all[p][:], l1_all[p][:], -1.0, ALU.mult, [("l1_all", p)], [("lfm_all", p)])

    def mlstm_tile(st, t, own):
        gt = st * NT + t
        par = gt % 2
        l1, lfm, Rm, expB, wcol, ecol, ve = l1s[par], lfms[par], Rms[par], expBs[par], wcols[par], ecols[par], ves[par]
        T_ = lambda n: (n, par)
        cols = slice(t * 128, (t + 1) * 128)
        gtok = ("gate", t)
        lfm = lfm_all[st % 2][:, t, :]
        LFT = ("lfm_all", st % 2)
        for h in range(4):
            TS("dve", Rm[:, h, :], Uf[:], lfm[:, h:h + 1], ALU.mult, ["Uf", LFT], [T_("Rm")])
        pB, pBt = nextG()
        MM(pB[:, 0:512], onesf[:], Rm[:].rearrange("p h t -> p (h t)"), True, True, ["onesf", T_("Rm")], [pBt])
        pb2, pb2t = nextG()
        MM(pb2[:, 0:4], Uf[:], lfm, True, True, ["Uf", LFT], [pb2t])
        ACT(expB[:].rearrange("p h t -> p (h t)"), pB[:, 0:512], AF.Exp, [pBt], [T_("expB")])
        TT("dve", wcol[:], gate[:, t, 0:4], pb2[:, 0:4], ALU.subtract, [gtok, pb2t], [T_("wcol")])
        ACT(wcol[:], wcol[:], AF.Exp, [T_("wcol")], [T_("wcol")], bias=LNS)
        TT("dve", ecol[:], wcol[:], expB[:, :, 127], ALU.mult, [T_("wcol"), T_("expB")], [T_("ecol")])
        if own:
            for h in range(4):
                STT(DT[:, h, :], expB[:, h, :], wcol[:, h:h + 1], Uf[:], ALU.mult, ALU.mult, [T_("expB"), T_("wcol"), "Uf"], ["DT"])
            pS, pSt = nextG()
            for h in range(4):
                MM(pS[:, h * 128:(h + 1) * 128], kT[:, h, cols], qT[:, h, cols], True, True, ["kT", "qT"], [pSt])
            TT("dve", PT[:], pS[:, 0:512], DT[:].rearrange("p h t -> p (h t)"), ALU.mult, [pSt, "DT"], ["PT"])
            TT("pool", qbT[:], qT[:, :, cols], expB[:], ALU.mult, ["qT", T_("expB")], ["qbT"])
            pO = []
            for hp in range(2):
                po, pot = nextC()
                pO.append((po, pot))
                for hh in range(2):
                    h = hp * 2 + hh
                    MM(po[:, hh * 129:(hh + 1) * 129], PT[:, h * 128:(h + 1) * 128], v_aug[:, t, h, :], True, False, ["PT", ("v_aug", t)], [pot])
                    MM(po[:, hh * 129:(hh + 1) * 129], qbT[:, h, :], Cb[:, h, :], False, True, ["qbT", "Cb"], [pot])
            for hp in range(2):
                po, pot = pO[hp]
                pv = po[:, 0:258].rearrange("p (h c) -> p h c", h=2)
                ACT(mst[:, hp * 2:hp * 2 + 2], pv[:, :, 128], AF.Abs, [pot, "mstA"], ["mstA"])
            TS("dve", mst[:, 0:4], mst[:, 0:4], 1.0, ALU.max, ["mstA"], ["mstA"])
            RCP(mst[:, 0:4], mst[:, 0:4], ["mstA"], ["mstA"])
            for h in range(4):
                po, pot = pO[h // 2]
                hh = h % 2
                ACT(PT[:, 0:128], po[:, hh * 129:hh * 129 + 128], AF.Square, [pot, "mstA", "mstB"], ["PT", "mstB"], scale=mst[:, h:h + 1], accum_out=mst[:, 4 + h:5 + h])
            TS("dve", mst[:, 4:8], mst[:, 4:8], 1.0 / 128, ALU.mult, ["mstB"], ["mstB"], s2=EPS, op1=ALU.add)
            TT("pool", mst[:, 4:8], mst[:, 4:8], mhalf[:, 0:4], ALU.pow, ["mstB", "mhalf"], ["mstB"])
            TT("dve", mst[:, 8:12], mst[:, 0:4], mst[:, 4:8], ALU.mult, ["mstA", "mstB"], ["mstC"])
            for h in range(4):
                po, pot = pO[h // 2]
                hh = h % 2
                STT(yml[:, h * 128:(h + 1) * 128], po[:, hh * 129:hh * 129 + 128], mst[:, 8 + h:9 + h], og[:, t, h * 128:(h + 1) * 128], ALU.mult, ALU.mult,
                    [pot, "mstC", ("og", t)], ["yml"])
            pt, ptt = nextT()
            for h in range(4):
                TR(pt[:, h * 128:(h + 1) * 128], yml[:, h * 128:(h + 1) * 128], identb[:], ["yml", "identb"], [ptt])
            CP("act", yT[:, 0:4, cols], pt.rearrange("p (k n) -> p k n", k=4), [ptt], [("yT", 0, t)])
        TT("dve", ve[:], v_aug[:, t, :, :], ecol[:].unsqueeze(2).to_broadcast([128, 4, 129]), ALU.mult, [("v_aug", t), T_("ecol")], [T_("ve")])
        for hp in range(2):
            pa, pat = nextC()
            for hh in range(2):
                h = hp * 2 + hh
                MM(pa[:, hh * 129:(hh + 1) * 129], ktm[:, t, h * 128:(h + 1) * 128], ve[:, h, :], True, True, [("ktm", t), T_("ve")], [pat])
            for hh in range(2):
                h = hp * 2 + hh
                STT(Cst[:, h, :], Cst[:, h, :], expB[:, h, 127:128], pa[:, hh * 129:(hh + 1) * 129], ALU.mult, ALU.add, [pat, T_("expB"), "Cst"], ["Cst"])
        if gt == NTILE // 2 - 1:
            TS("dve", Cst[:], Cst[:], flag2[:, 0:1], ALU.mult, ["Cst", "flag2"], ["Cst"])
        CP("act", Cb[:], Cst[:], ["Cst"], ["Cb"])

    NPB = int(os.environ.get("K_NP", "3"))
    PcT = [sb("PcT%d" % i, [128, 512], BF16) for i in range(NPB)]

    def nextP():
        i = rot["P"] % NPB
        rot["P"] += 1
        return PcT[i], ("PcT", i)

    nsts = [sb("nst%d" % i, [128, 32]) for i in range(2)]
    impaccs = [sb("impacc%d" % i, [128, 64]) for i in range(2)]
    scores = [sb("score%d" % i, [128, 64]) for i in range(2)]
    score2s = [sb("score2_%d" % i, [128, 64]) for i in range(2)]
    mx8s = [sb("mx8_%d" % i, [128, 16]) for i in range(2)]
    negms = [sb("negm%d" % i, [128, 64], BF16) for i in range(2)]
    nst, score, negm = nsts[0], scores[0], negms[0]
    ynsa = sb("ynsa", [128, 512], BF16)
    ytmps = [sb("ytmp%d" % i, [128, 64]) for i in range(2)]
    yxa = sb("yxa", [128, 512], BF16)

    def attn_scores(pS, pSt, mms):
        n = len(mms)
        for i, (l_ap, r_ap, rd) in enumerate(mms):
            MM(pS[:, 0:512], l_ap, r_ap, i == 0, i == n - 1, rd, [pSt])

    def run_jobs(jobs):
        issued = []

        def issue_scores(j):
            if j.get("pre") is not None:
                j["pre"]()
            pS, pSt = nextG()
            attn_scores(pS, pSt, j["mms"])
            issued.append((pS, pSt))

        if not jobs:
            return
        issue_scores(jobs[0])
        for i, j in enumerate(jobs):
            if i + 1 < len(jobs):
                issue_scores(jobs[i + 1])
            pS, pSt = issued[i]
            P, Pt = nextP()
            ACT(P[:], pS[:, 0:512], AF.Exp, [pSt], [Pt], scale=j.get("scale", 1.0))
            j["pv"](P, Pt)
            if j.get("post") is not None:
                j["post"]()

    def nsa_group(st, t, g):
        gt = st * NT + t
        nst, impacc, score, score2, mx8, negm, ytmp = nsts[g], impaccs[g], scores[g], score2s[g], mx8s[g], negms[g], ytmps[g]
        G_ = lambda n: (n, g)
        ti = gt - NTILE // 2
        cols = slice(t * 128, (t + 1) * 128)
        q4 = nqT[:, t, 4 * g:4 * g + 4, :]
        na_full = NA[0:68, t, g, :]
        na_aug = NA[64:68, t, g, :]
        natok = ("NA", t, g)
        pOC, pOCt = nextC()
        oc3 = pOC[:, 0:512].rearrange("p (r c) -> p r c", r=4)
        acc = {}
        jobs = []
        for nt in range(2):
            mms = [(kcT[:, nt * 128:(nt + 1) * 128], q4, ["kcT", "nqT"]),
                   (CA[64:68, nt * 128:(nt + 1) * 128], na_aug, ["CA", "NAaug"]),
                   (identb[:], okc[:, nt, ti * 128:(ti + 1) * 128].unsqueeze(1).to_broadcast([128, 4, 128]), ["identb", "okc"])]

            def pv_c(P, Pt, nt=nt):
                for r_ in range(4):
                    MM(pOC[:, r_ * 128:(r_ + 1) * 128], P[:, r_ * 128:(r_ + 1) * 128], vcx[:, nt, g, :], nt == 0 and r_ == 0, nt == 1, [Pt] + VCX, [pOCt])
            jobs.append(dict(mms=mms, pv=pv_c))

        def topk_dve():
            TS("dve", nst[:, 0:4], oc3[:, :, 64], 1e-30, ALU.max, [pOCt, G_("nstA")], [G_("nstA")])
            RCP(nst[:, 0:4], nst[:, 0:4], [G_("nstA")], [G_("nstA")])
            TS("dve", impacc[:, 0:63], oc3[:, 0, 65:128], nst[:, 0:1], ALU.mult, [pOCt, G_("nstA")], [G_("impacc")])
            for r_ in range(1, 4):
                STT(impacc[:, 0:63], oc3[:, r_, 65:128], nst[:, r_:r_ + 1], impacc[:, 0:63], ALU.mult, ALU.add, [pOCt, G_("nstA"), G_("impacc")], [G_("impacc")])
            CP("dve", score[:], bsel[:, ti, :], ["bsel"], [G_("score")])
            TT("dve", score[:, 0:63], score[:, 0:63], impacc[:, 0:63], ALU.add, [G_("score"), G_("impacc")], [G_("score")])
            S.add("dve", lambda e: e.max(out=mx8[:, 0:8], in_=score[:]), [G_("score"), G_("mx8")], [G_("mx8")], cost=0.15)
            S.add("dve", lambda e: e.match_replace(out=score2[:], in_to_replace=mx8[:, 0:8], in_values=score[:], imm_value=-3.0e38), [G_("score"), G_("mx8")], [G_("score2")], cost=0.25)
            S.add("dve", lambda e: e.max(out=mx8[:, 8:16], in_=score2[:]), [G_("score2"), G_("mx8")], [G_("mx8")], cost=0.15)
            TS("dve", negm[:], score[:], mx8[:, 15:16], ALU.is_ge, [G_("score"), G_("mx8")], [G_("negm")], s2=1.0, op1=ALU.subtract)
        jobs[-1]["post"] = topk_dve

        def mask_to_NA():
            pt, ptt = nextT()
            TR(pt[0:64, 0:128], negm[:], identb[:], [G_("negm"), "identb"], [ptt])
            ACT(NA[0:64, t, g, :].rearrange("p (r q) -> p r q", r=4), pt[0:64, 0:128].unsqueeze(1).to_broadcast([64, 4, 128]), AF.Copy, [ptt], [natok], scale=-NEGM)

        cjobs = jobs

        def make_wjobs():
          pOW, pOWt = nextC()
          acc["OW"] = (pOW, pOWt)
          jobs = []
          for kt in range(gt - 4, gt + 1):
              slot = kt % 8
              mms = [(kwT[:, slot * 128:(slot + 1) * 128], q4, [("kwT", slot // NT), "nqT"]),
                     (EA[64:68, kt, :], na_aug, ["EA", "NAaug"])]
              if kt == gt:
                  mms.append((identb[:], caus, ["identb", "caus"]))
              if kt == gt - 4:
                  mms.append((identb[:], winm, ["identb", "winm"]))

              def pv_w(P, Pt, kt=kt, slot=slot):
                  for r_ in range(4):
                      MM(pOW[:, r_ * 65:(r_ + 1) * 65], P[:, r_ * 128:(r_ + 1) * 128], vw_aug[:, slot, g, :], kt == gt - 4 and r_ == 0, kt == gt, [Pt, ("vw_aug", slot // NT)], [pOWt])
              jobs.append(dict(mms=mms, pv=pv_w))
          return jobs

        kt_lo = max(0, gt - 16) if g == 0 else 0

        def make_sjobs():
          i_ = rot["G"] % NPSG
          while i_ in resG:
              rot["G"] += 1
              i_ = rot["G"] % NPSG
          rot["G"] += 1
          resG.add(i_)
          pOS, pOSt = psG[i_], ("psG", i_)
          acc["OS"] = (pOS, pOSt, i_)
          jobs = []
          for kt in range(kt_lo, gt + 1):
              mms = [(ksT[:, kt * 128:(kt + 1) * 128], q4, [("ksT", kt // NT), "nqT"]),
                     (EA[0:68, kt, :], na_full, ["EA", natok, "NAaug"])]
              if kt == gt:
                  mms.append((identb[:], caus, ["identb", "caus"]))

              def pv_s(P, Pt, kt=kt):
                  for r_ in range(4):
                      MM(pOS[:, r_ * 65:(r_ + 1) * 65], P[:, r_ * 128:(r_ + 1) * 128], vs_aug[:, kt, g, :], kt == kt_lo and r_ == 0, kt == gt, [Pt, ("vs_aug", kt // NT)], [pOSt])
              jobs.append(dict(mms=mms, pv=pv_s, pre=(mask_to_NA if kt == kt_lo else None)))
          return jobs

        def combine():
          pOS, pOSt, i_ = acc["OS"]
          pOW, pOWt = acc["OW"]
          os3 = pOS[:, 0:260].rearrange("p (r c) -> p r c", r=4)
          ow3 = pOW[:, 0:260].rearrange("p (r c) -> p r c", r=4)
          combine_body(os3, ow3, pOSt, pOWt)
          resG.discard(i_)

        def combine_body(os3, ow3, pOSt, pOWt):
          TS("dve", nst[:, 4:8], os3[:, :, 64], 1e-30, ALU.max, [pOSt, G_("nstB")], [G_("nstB")])
          TS("dve", nst[:, 8:12], ow3[:, :, 64], 1e-30, ALU.max, [pOWt, G_("nstB")], [G_("nstB")])
          RCP(nst[:, 4:12], nst[:, 4:12], [G_("nstB")], [G_("nstB")])
          gv = gsig[:, t, g * 12:(g + 1) * 12].rearrange("p (r b) -> p r b", b=3)
          for bi in range(3):
              TT("dve", nst[:, 12 + bi * 4:16 + bi * 4], nst[:, bi * 4:bi * 4 + 4], gv[:, :, bi], ALU.mult, [G_("nstA"), G_("nstB"), ("gsig", t), G_("nstK")], [G_("nstK")])
          for r_ in range(4):
              hcol = (4 * g + r_) * 64
              TS("dve", ytmp[:], oc3[:, r_, 0:64], nst[:, 12 + r_:13 + r_], ALU.mult, [pOCt, G_("nstK")], [G_("ytmp")])
              STT(ytmp[:], os3[:, r_, 0:64], nst[:, 16 + r_:17 + r_], ytmp[:], ALU.mult, ALU.add, [pOSt, G_("ytmp"), G_("nstK")], [G_("ytmp")])
              STT(ynsa[:, hcol:hcol + 64], ow3[:, r_, 0:64], nst[:, 20 + r_:21 + r_], ytmp[:], ALU.mult, ALU.add, [pOWt, G_("ytmp"), G_("nstK")], ["ynsa"])

        return cjobs, make_wjobs, make_sjobs, combine

    def nsa_tile(st, t):
        cols = slice(t * 128, (t + 1) * 128)
        c0, w0, s0, fin0 = nsa_group(st, t, 0)
        c1, w1, s1, fin1 = nsa_group(st, t, 1)
        run_jobs(c0 + c1 + w0() + s0())
        fin0()
        run_jobs(w1() + s1())
        fin1()
        pt, ptt = nextT()
        for h in range(4):
            TR(pt[:, h * 128:(h + 1) * 128], ynsa[:, h * 128:(h + 1) * 128], identb[:], ["ynsa", "identb"], [ptt])
        CP("act", yT[:, 4:8, cols], pt.rearrange("p (k n) -> p k n", k=4), [ptt], [("yT", 1, t)])

    def xa_tile(st, t):
        cols = slice(t * 128, (t + 1) * 128)
        pO = [nextC(), nextC()]
        jobs = []
        for mt in range(2):
            def scores_x(mt=mt):
                pass
            mms = None
            jobs.append(mt)
        held = []
        for mt in range(2):
            pS, pSt = nextG()
            for h in range(4):
                MM(pS[:, h * 128:(h + 1) * 128], mkT[:, h, mt * 128:(mt + 1) * 128], xqT[:, h, cols], True, True, ["mkT", "xqT"], [pSt])
            held.append((pS, pSt))
        for mt in range(2):
            pS, pSt = held[mt]
            P, Pt = nextP()
            ACT(P[:], pS[:, 0:512], AF.Exp, [pSt], [Pt], scale=float(128.0 ** -0.5))
            for h in range(4):
                po, pot = pO[h // 2]
                hh = h % 2
                MM(po[:, hh * 129:(hh + 1) * 129], P[:, h * 128:(h + 1) * 128], mv_aug[:, mt, h, :], mt == 0 and hh == 0, mt == 1, [Pt, "mv_aug"], [pot])
        for hp in range(2):
            po, pot = pO[hp]
            pv = po[:, 0:258].rearrange("p (h c) -> p h c", h=2)
            TS("dve", nst[:, 24 + hp * 2:26 + hp * 2], pv[:, :, 128], 1e-30, ALU.max, [pot, "nstX"], ["nstX"])
        RCP(nst[:, 24:28], nst[:, 24:28], ["nstX"], ["nstX"])
        for h in range(4):
            po, pot = pO[h // 2]
            hh = h % 2
            TS("dve", yxa[:, h * 128:(h + 1) * 128], po[:, hh * 129:hh * 129 + 128], nst[:, 24 + h:25 + h], ALU.mult, [pot, "nstX"], ["yxa"])
        pt, ptt = nextT()
        for h in range(4):
            TR(pt[:, h * 128:(h + 1) * 128], yxa[:, h * 128:(h + 1) * 128], identb[:], ["yxa", "identb"], [ptt])
        CP("act", yT[:, 8:12, cols], pt.rearrange("p (k n) -> p k n", k=4), [ptt], [("yT", 2, t)])

    gT = sb("gT", [128, 2, 2, 2, 32], BF16)
    gtmp = sb("gtmp", [128, 128])
    gtmp2 = sb("gtmp2", [128, 128])
    NB = L // 16

    def compress_st(st):
        T0 = st * L
        n0 = T0 // 16 - 1
        W1 = [wfetch(("w1", kv), donetok=(("w1done", st) if kv == 1 else None)) for kv in range(2)]
        if os.environ.get("K_CB", "1") == "1":
            for kv in range(2):
                w1v, w1t = W1[kv]
                pgx, pgxt = nextG()
                for g in range(2):
                    for half in range(2):
                        c0 = (g * 2 + half) * NB
                        for lp in range(16):
                            MM(pgx[:, c0:c0 + NB], w1v[:, lp, half * 128:(half + 1) * 128], kcin2[:, kv * 2 + g, 2 * lp + 1:2 * lp + 1 + 16 * (NB - 1) + 1:16],
                               lp == 0, lp == 15, [w1t, "kcin2"], [pgxt])
                NC4 = 4 * NB
                gx = gtmp[:, 0:NC4].rearrange("p (g h n) -> p g h n", g=2, h=2)
                px = pgx[:, 0:NC4].rearrange("p (g h n) -> p g h n", g=2, h=2)
                for half in range(2):
                    ACT(gx[:, :, half, :], px[:, :, half, :], AF.Identity, [pgxt, "hb", "gtmp"], ["gtmp"], bias=hb[:, kv, half:half + 1])
                TT("dve", gtmp2[:, 0:NC4], gtmp[:, 0:NC4], gtmp[:, 0:NC4], ALU.mult, ["gtmp"], ["gtmp2"])
                TS("dve", gtmp2[:, 0:NC4], gtmp2[:, 0:NC4], 0.044715, ALU.mult, ["gtmp2"], ["gtmp2"], s2=1.0, op1=ALU.add)
                TT("dve", gtmp2[:, 0:NC4], gtmp2[:, 0:NC4], gtmp[:, 0:NC4], ALU.mult, ["gtmp", "gtmp2"], ["gtmp2"])
                SIGT(gtmp2[:, 0:NC4], gtmp2[:, 0:NC4], ["gtmp2"], ["gtmp2"], scale=1.5957691216057308)
                TT("dve", gT[:, kv, :, :, 0:NB], gtmp2[:, 0:NC4].rearrange("p (g h n) -> p g h n", g=2, h=2), gx, ALU.mult, ["gtmp", "gtmp2"], ["gT"])
                pgy, pgyt = nextG()
                i = 0
                for g in range(2):
                    for half in range(2):
                        MM(pgy[:, 0:NB], W2p[:, kv, half, g, :], gT[:, kv, g, half, 0:NB], i == 0, i == 3, W2T + ["gT"], [pgyt])
                        i += 1
                dstc = kcT if kv == 0 else vcT
                dtok = "kcT" if kv == 0 else "vcT"
                lo = 1 if st == 0 else 0
                CP("act", dstc[:, n0 + lo:n0 + NB], pgy[:, lo:NB], [pgyt], [dtok])
        else:
            for kv in range(2):
                w1v, w1t = W1[kv]
                for g in range(2):
                    for half in range(2):
                        pgx, pgxt = nextG()
                        for lp in range(16):
                            MM(pgx[:, 0:NB], w1v[:, lp, half * 128:(half + 1) * 128], kcin2[:, kv * 2 + g, 2 * lp + 1:2 * lp + 1 + 16 * (NB - 1) + 1:16],
                               lp == 0, lp == 15, [w1t, "kcin2"], [pgxt])
                        ACT(gtmp[:, 0:NB], pgx[:, 0:NB], AF.Identity, [pgxt, "hb"], ["gtmp"], bias=hb[:, kv, half:half + 1])
                        TT("dve", gtmp2[:, 0:NB], gtmp[:, 0:NB], gtmp[:, 0:NB], ALU.mult, ["gtmp"], ["gtmp2"])
                        TS("dve", gtmp2[:, 0:NB], gtmp2[:, 0:NB], 0.044715, ALU.mult, ["gtmp2"], ["gtmp2"], s2=1.0, op1=ALU.add)
                        TT("dve", gtmp2[:, 0:NB], gtmp2[:, 0:NB], gtmp[:, 0:NB], ALU.mult, ["gtmp", "gtmp2"], ["gtmp2"])
                        SIGT(gtmp2[:, 0:NB], gtmp2[:, 0:NB], ["gtmp2"], ["gtmp2"], scale=1.5957691216057308)
                        TT("dve", gT[:, kv, g, half, 0:NB], gtmp2[:, 0:NB], gtmp[:, 0:NB], ALU.mult, ["gtmp", "gtmp2"], ["gT"])
                pgy, pgyt = nextG()
                i = 0
                for g in range(2):
                    for half in range(2):
                        MM(pgy[:, 0:NB], W2p[:, kv, half, g, :], gT[:, kv, g, half, 0:NB], i == 0, i == 3, W2T + ["gT"], [pgyt])
                        i += 1
                dstc = kcT if kv == 0 else vcT
                dtok = "kcT" if kv == 0 else "vcT"
                lo = 1 if st == 0 else 0
                CP("act", dstc[:, n0 + lo:n0 + NB], pgy[:, lo:NB], [pgyt], [dtok])
        nts = sorted(set([max(n0, 0) // 128, (n0 + NB - 1) // 128]))
        for nt in nts:
            pt, ptt = nextT()
            TR(pt[:, 0:128], vcT[:, nt * 128:(nt + 1) * 128], identb[:], ["vcT", "identb"], [ptt])
            CP("act", vcx[:, nt, :, 0:64], pt[:, 0:128].rearrange("p (g d) -> p g d", g=2), [ptt] + VCX, ["vcx"])
        CP("dve", kcin2[:, :, 0:17], kcin2[:, :, L:L + 17], ["kcin2"], ["kcin2"])

    mT = sb("mT", [128, 8, L], BF16)
    USE_HNT = os.environ.get("K_HNT", "0") == "1"
    hnT = sb("hnT", [128, 8, L], BF16) if USE_HNT else None
    gsb = [sb("gsb%d" % i, [128, L]) for i in range(2)]
    mtmp = sb("mtmp", [128, L])
    rtmp = [sb("rtmp%d" % i, [128, L], BF16) for i in range(2)]
    cnt2 = {"g": 0, "r": 0}

    def own_tail(st):
        S.tag = "H_merge"
        YT = lambda j: [("yT", j, t) for t in range(NT)]
        for j in range(3):
            wbm = [wfetch(("mg", j, hf)) for hf in range(2)]
            wbr, wbrt = wfetch(("br", j))
            for dc in range(8):
                wg, wgt = wbm[dc // 4]
                co = (dc % 4) * 128
                pg_, pgt_ = nextG()
                for k in range(8):
                    MM(pg_[:, 0:L], wg[:, k, co:co + 128], XN[0][:, k, :], k == 0, k == 7, [wgt, XT[0]], [pgt_])
                gi = cnt2["g"] % 2
                cnt2["g"] += 1
                gb, gbt = gsb[gi], ("gsb", gi)
                bi = FMROW[("mg", j * 8 + dc)]
                SIGT(gb[:], pg_[:, 0:L], [pgt_, "hbiasFM"], [gbt], hbias=hbiasFM[:, bi:bi + 1])
                pu, put = nextG()
                for k in range(4):
                    MM(pu[:, 0:L], wbr[:, k, dc * 128:(dc + 1) * 128], yT[:, j * 4 + k, :], k == 0, k == 3, [wbrt] + YT(j), [put])
                if j == 0:
                    TT("dve", macc[:, dc, :], pu[:, 0:L], gb[:], ALU.mult, [put, gbt], [("macc", dc), ("aT", 16 + 2 * dc), ("aT", 17 + 2 * dc)])
                else:
                    TT("dve", mtmp[:], pu[:, 0:L], gb[:], ALU.mult, [put, gbt], ["mtmp"])
                    if j == 1:
                        TT("pool", macc[:, dc, :], macc[:, dc, :], mtmp[:], ALU.add, ["mtmp", ("macc", dc)], [("macc", dc), ("aT", 16 + 2 * dc), ("aT", 17 + 2 * dc)])
                    else:
                        TT("pool", mT[:, dc, :], macc[:, dc, :], mtmp[:], ALU.add, ["mtmp", ("macc", dc)], [("mT", dc)])
        MT = [("mT", dc) for dc in range(8)]
        S.tag = "I_outproj"
        for t in range(NT):
            gt = st * NT + t
            DMA("sp", hres[:, t, :], xc[gt * 128:(gt + 1) * 128, :], [], [("hres", t, 0), ("hres", t, 1)], ("hresx", t))
        for dh in range(2):
            wo, wt_ = wfetch(("out", dh))
            for t in range(NT):
                ph, pht = nextG()
                for k in range(8):
                    MM(ph[:, 0:512], mT[:, k, t * 128:(t + 1) * 128], wo[:, k, :], k == 0, k == 7, [wt_] + MT, [pht])
                TT("dve", hres[:, t, dh * 512:(dh + 1) * 512], ph[:, 0:512], hres[:, t, dh * 512:(dh + 1) * 512], ALU.add, [pht, ("hres", t, dh)], [("hres", t, dh)])
        S.tag = "J_ffn"
        for t in range(NT):
            rmsnorm_to_T(hres[:, t, :], [("hres", t, 0), ("hres", t, 1)], gffn_r, "gffn_r", (hnT if USE_HNT else XN[0]), slice(t * 128, (t + 1) * 128), ("hnT" if USE_HNT else XT[0]))
        for fb in range(8):
            w1, wt_ = wfetch(("ff1", fb))
            for fcl in range(4):
                fc = fb * 4 + fcl
                pa_, pat_ = nextG()
                for k in range(8):
                    MM(pa_[:, 0:L], w1[:, k, fcl * 128:(fcl + 1) * 128], (hnT if USE_HNT else XN[0])[:, k, :], k == 0, k == 7, [wt_, ("hnT" if USE_HNT else XT[0])], [pat_])
                ri = cnt2["r"] % 2
                cnt2["r"] += 1
                rb, rbt = rtmp[ri], ("rtmp", ri)
                if os.environ.get("K_RELU", "1") == "1":
                    TS("dve", rb[:], pa_[:, 0:L], 0.0, ALU.max, [pat_], [rbt])
                else:
                    ACT(rb[:], pa_[:, 0:L], AF.Relu, [pat_], [rbt])
                TT("pool", aT[:, fc, :], rb[:], rb[:], ALU.mult, [rbt], [("aT", fc)])
        for dh in range(2):
            pf = [nextC() for t in range(NT)]
            for fg in range(4):
                w2, wt_ = wfetch(("ff2", dh, fg))
                for t in range(NT):
                    po, pot = pf[t]
                    for k in range(8):
                        fc = fg * 8 + k
                        MM(po[:, 0:512], aT[:, fc, t * 128:(t + 1) * 128], w2[:, k, :], fc == 0, fc == 31, [wt_, ("aT", fc)], [pot])
            for t in range(NT):
                po, pot = pf[t]
                TT("dve", hres[:, t, dh * 512:(dh + 1) * 512], po[:, 0:512], hres[:, t, dh * 512:(dh + 1) * 512], ALU.add, [pot, ("hres", t, dh)], [("hres", t, dh)])
        S.tag = "K_final"
        for t in range(NT):
            gt = st * NT + t
            ht = [("hres", t, 0), ("hres", t, 1)]
            col, tok = rms_scale(hres[:, t, :], ht)
            STT(hres[:, t, :], hres[:, t, :], col, gfin_r[:], ALU.mult, ALU.mult, ht + [tok, "gfin_r"], ht)
            orow = (gt - NTILE // 2) * 128
            DMA("sp", y_out[orow:orow + 128, :], hres[:, t, :], ht, [], ("yout", t))

    for st in range(NST if os.environ.get('K_NOLOOP', '0') == '0' else 0):
        XN[0] = xnTs[st % NXN]
        XT[0] = ("xnT", st % NXN)
        own = st >= NSTP
        do_q = own or st == NSTP - 1
        pfx = not own
        fcol = flag2[:, 0:1] if pfx else ones2[:, 0:1]
        fsrc2 = flag2 if pfx else ones2
        ftok = "flag2" if pfx else "ones2"
        btile = biasTMf if pfx else biasTM
        btok = "biasTMf" if pfx else "biasTM"
        S.tag = "A_norm"
        for t in range(NT):
            gt = st * NT + t
            xi = nextX()
            DMA("sp", xbuf[xi][:], xc[gt * 128:(gt + 1) * 128, :], [], [("xbuf", xi)], ("xbuf", xi))
            rmsnorm_to_T(xbuf[xi][:], [("xbuf", xi)], gmix_r, "gmix_r", XN[0], slice(t * 128, (t + 1) * 128), XT[0])
        if st == 1:
            late_tables()
        if st == 2:
            S.tag = "setup"
            mem_setup()
            dbg("mkT", mkT[:], ["mkT"], [128, 4, 256])
            dbg("mv_aug", mv_aug[:], ["mv_aug"], [128, 2, 4, 129])
            S.tag = "A_norm"
        if st == NSTP:
            dbg(XT[0], XN[0][:], [XT[0]], [128, 8, L])
        S.tag = "B_ctxproj"
        wk_, wt_ = wfetch("mlk")
        for c in range(4):
            pgx, pgxt = fm_group(wk_, wt_, c * 128)
            conv_silu(prek, "prek", pgx, pgxt, c, FMROW[("mlk", c)], 4 + c, kT, "kT")
        for t in range(NT):
            pt, ptt = nextT()
            for c in range(4):
                TR(pt[:, c * 128:(c + 1) * 128], kT[:, c, t * 128:(t + 1) * 128], identb[:], ["kT", "identb"], [ptt])
            CP("dve" if os.environ.get("K_KTM", "1") == "1" else "act", ktm[:, t, :], pt, [ptt], [("ktm", t)])
        wv_, wt_ = wfetch("mlv")
        for t in range(NT):
            pgx, pgxt = nextG()
            for k in range(8):
                MM(pgx[:, 0:512], XN[0][:, k, t * 128:(t + 1) * 128], wv_[:, k, :], k == 0, k == 7, [wt_, XT[0]], [pgxt])
            TT("dve", v_aug[:, t, :, 0:128], pgx[:, 0:512].rearrange("p (h d) -> p h d", h=4), biasTM[:, 0:512].rearrange("p (h d) -> p h d", h=4), ALU.add,
               [pgxt, "biasTM", ("v_aug", t)], [("v_aug", t)])
        wc_, wt_ = wfetch("cbuf")
        pgx, pgxt = fm_group(wc_, wt_, 0)
        bi = FMROW["ks"]
        EVB(ksT[:, st * L:(st + 1) * L], pgx[:, 0:L], biasFM[:, bi:bi + 1], [pgxt, "biasFM"], [("ksT", st)])
        pgx, pgxt = fm_group(wc_, wt_, 128)
        bi = FMROW["kw"]
        ro = (st * L) % 1024
        EVB(kwT[:, ro:ro + L], pgx[:, 0:L], biasFM[:, bi:bi + 1], [pgxt, "biasFM"], [("kwT", (ro // 128) // NT)])

        def kc_evac(pgx, pgxt, kv, g):
            bi3 = FMROW[("kc", kv, g)]
            idx = kv * 2 + g
            ACT(kcin2[0:64, idx, 16:16 + L], pgx[0:64, 0:L], AF.Identity, [pgxt, "biasFM", "kcin2"], ["kcin2"], bias=biasFM[0:64, bi3:bi3 + 1])
            ACT(kcin2[64:128, idx, 17:17 + L], pgx[64:128, 0:L], AF.Identity, [pgxt, "biasFM", "kcin2"], ["kcin2"], bias=biasFM[64:128, bi3:bi3 + 1])
        for g in range(2):
            pgx, pgxt = fm_group(wc_, wt_, 256 + g * 128)
            kc_evac(pgx, pgxt, 0, g)
        wd_, wt_ = wfetch("dbuf")
        for g in range(2):
            pgx, pgxt = fm_group(wd_, wt_, g * 128)
            kc_evac(pgx, pgxt, 1, g)
        for t in range(NT):
            gt = st * NT + t
            slot = gt % 8
            vst = ("vs_aug", gt // NT)
            vwt = ("vw_aug", slot // NT)
            pgx, pgxt = nextG()
            for k in range(8):
                MM(pgx[:, 0:288], XN[0][:, k, t * 128:(t + 1) * 128], wd_[:, k, 256:544], k == 0, k == 7, [wt_, XT[0]], [pgxt])
            STT(vs_aug[:, gt, :, 0:64], pgx[:, 0:128].rearrange("p (g d) -> p g d", g=2), fcol, (btile[:, 0:128] if pfx else btile[:, 512:640]).rearrange("p (g d) -> p g d", g=2), ALU.mult, ALU.add,
                [pgxt, ftok, btok, vst], [vst])
            CP("dve", vs_aug[:, gt, :, 64], fsrc2[:, 0:2], [ftok, vst], [vst])
            STT(vw_aug[:, slot, :, 0:64], pgx[:, 128:256].rearrange("p (g d) -> p g d", g=2), fcol, (btile[:, 128:256] if pfx else btile[:, 640:768]).rearrange("p (g d) -> p g d", g=2), ALU.mult, ALU.add,
                [pgxt, ftok, btok, vwt], [vwt])
            CP("dve", vw_aug[:, slot, :, 64], fsrc2[:, 0:2], [ftok, vwt], [vwt])
            TT("dve", gate[:, t, :], pgx[:, 256:264], biasTM[:, 768:776], ALU.add, [pgxt, "biasTM"], [("gate", t)])
            if own:
                TT("dve", gsig[:, t, :], pgx[:, 264:288], biasTM[:, 776:800], ALU.add, [pgxt, "biasTM"], [("gsig", t)])
                SIGT(gsig[:, t, :], gsig[:, t, :], [("gsig", t)], [("gsig", t)])
        S.tag = "C_compress"
        compress_st(st)
        if st < NSTP - 1:
            per_st = (len(conv_rest) + NSTP - 2) // (NSTP - 1)
            for idx in conv_rest[st * per_st:(st + 1) * per_st]:
                emit_conv(idx, [("w1done", st)])
        if st == NSTP - 1:
            for idx in conv_ffn[0:8]:
                emit_conv(idx, [("w1done", st)])
        if st == (NSTP if os.environ.get("K_CS", "1") == "1" else NSTP - 1):
            for idx in conv_ffn[8:16]:
                emit_conv(idx, [("w1done", st)])
        S.tag = "D_qproj"
        if do_q:
            wq_, wt_ = wfetch("mlq")
            for c in range(4):
                pgx, pgxt = fm_group(wq_, wt_, c * 128)
                conv_silu(preq, "preq", pgx, pgxt, c, FMROW[("mlq", c)], c, qT, "qT")
        if st == NSTP - 1:
            for c in range(4):
                TS("dve", prek[:, c, :], prek[:, c, :], flag2[:, 0:1], ALU.mult, [("prek", c), "flag2"], [("prek", c)])
                TS("dve", preq[:, c, :], preq[:, c, :], flag2[:, 0:1], ALU.mult, [("preq", c), "flag2"], [("preq", c)])
        if own:
            wo_, wt_ = wfetch("mlo")
            for t in range(NT):
                pgx, pgxt = nextG()
                for k in range(8):
                    MM(pgx[:, 0:512], XN[0][:, k, t * 128:(t + 1) * 128], wo_[:, k, :], k == 0, k == 7, [wt_, XT[0]], [pgxt])
                TT("dve", otmp, pgx[:, 0:512], biasTM[:, 800:1312], ALU.add, [pgxt, "biasTM"], [("Rm", 0)])
                SIGT(otmp, otmp, [("Rm", 0)], [("Rm", 0)])
                TT("dve", og[:, t, :], otmp, gml_r[:], ALU.mult, [("Rm", 0), "gml_r"], [("og", t)])
            v4, wt_ = wfetch("nsq")
            for p_ in range(4):
                pgx, pgxt = nextG()
                for k in range(8):
                    MM(pgx[:, 0:L], v4[:, k, p_ * 128:(p_ + 1) * 128], XN[0][:, k, :], k == 0, k == 7, [wt_, XT[0]], [pgxt])
                bq = FMROW[("nsq", p_)]
                ACT(nqT[0:64, :, p_, :], pgx[0:64, 0:L].rearrange("p (t q) -> p t q", t=NT), AF.Identity, [pgxt, "biasFM", "nqT"], ["nqT"],
                    bias=biasFM[0:64, bq:bq + 1], scale=0.125)
                ACT(nqT[64:128, :, p_ + 4, :], pgx[64:128, 0:L].rearrange("p (t q) -> p t q", t=NT), AF.Identity, [pgxt, "biasFM", "nqT"], ["nqT"],
                    bias=biasFM[64:128, bq:bq + 1], scale=0.125)
            wx_, wt_ = wfetch("xaq")
            for c in range(4):
                pgx, pgxt = fm_group(wx_, wt_, c * 128)
                bx = FMROW[("xaq", c)]
                EVB(xqT[:, c, :], pgx[:, 0:L], biasFM[:, bx:bx + 1], [pgxt, "biasFM"], ["xqT"])
            oi = st - NSTP
            for t in range(NT):
                ti = oi * NT + t
                DMA("sp", NA[64:68, t, :, :], t_qaug[:, ti, :, :], [], ["NAaug"], "NAaug")
        S.tag = "E_mlstm"
        mlstm_gates(st)
        for t in range(NT):
            S.tag = "E_mlstm"
            mlstm_tile(st, t, own)
            if own:
                S.tag = "F_nsa"
                nsa_tile(st, t)
                S.tag = "G_xa"
                xa_tile(st, t)
        if own:
            if st == NSTP:
                dbg("yT", yT[:], [("yT", j, t) for j in range(3) for t in range(NT)], [128, 12, L])
                dbg("Cst", Cst[:], ["Cst"], [128, 4, 129])
                dbg("kcT", kcT[:], ["kcT"], [128, 256])
                dbg("vcT", vcT[:], ["vcT"], [128, 256])
                dbg("kT", kT[:], ["kT"], [128, 4, L])
                dbg("qT", qT[:], ["qT"], [128, 4, L])
                dbg("nqT", nqT[:], ["nqT"], [128, NT, 8, 128])
                dbg("gsig", gsig[:], [("gsig", t) for t in range(NT)], [128, NT, 24])
                dbg("vs_aug", vs_aug[:], [("vs_aug", i) for i in range(NST)], [128, NTILE, 2, 65])
                dbg("ksT", ksT[:], [("ksT", i) for i in range(NST)], [128, SEQ])
                dbg("score", score[:], ["score"], [128, 64])
                dbg("negm", negm[:], ["negm"], [128, 64])
            own_tail(st)
            if st == NSTP:
                dbg("mT", mT[:], [("mT", dc) for dc in range(8)], [128, 8, L])
                dbg("hres", hres[:], [("hres", t, dh) for t in range(NT) for dh in range(2)], [128, NT, D])
        if stop_after is not None and st == stop_after:
            break

    S.finalize(es)
    build.stats = dict(ops=len(S.ops), waits=S.nwaits, sems=S.nsems, est_us=getattr(S, "est_total", None))
    build.sched = S
    return nc, es


def _bf(a):
    return np.asarray(a, dtype=np.float32).astype(ml_dtypes.bfloat16)


def make_tables(s):
    T = {}
    T["t_flag"] = np.zeros((128, 2), np.float32) + (1.0 if s == 1 else 0.0)
    T["t_identf"] = np.eye(128, dtype=np.float32)
    ii = np.arange(128)
    T["t_U"] = (ii[:, None] <= ii[None, :]).astype(np.float32)
    first_valid_tok = 0 if s == 1 else HALF
    n = np.arange(256)
    q = HALF + np.arange(HALF)
    vis = (16 * n[:, None] + 31 <= q[None, :]) & (16 * n[:, None] >= first_valid_tok) & (n[:, None] < 255)
    okc = np.where(vis, 0.0, NEGM).astype(np.float32)
    T["t_okc"] = _bf(okc.reshape(2, 128, HALF).transpose(1, 0, 2))
    j = np.arange(64)
    cur = q // 64
    fb = first_valid_tok // 64
    ok = (j[None, :] <= cur[:, None]) & (j[None, :] >= fb)
    forced = (j[None, :] == fb) | (j[None, :] == cur[:, None]) | ((j[None, :] == cur[:, None] - 1) & (j[None, :] >= fb))
    bs = np.where(ok, np.where(forced, 1000.0, 0.0), -1e30).astype(np.float32)
    T["t_bsel"] = _bf(np.ascontiguousarray(bs.reshape(16, 128, 64).transpose(1, 0, 2)))
    EA = np.zeros((68, NTILE, 128), np.float32)
    for kt in range(NTILE):
        EA[2 * kt, kt, 0:64] = 1.0
        EA[2 * kt + 1, kt, 64:128] = 1.0
        EA[64, kt, :] = kt
        EA[65, kt, :] = ii
        EA[66, kt, :] = 1.0
        EA[67, kt, :] = 1.0
    T["t_EA"] = _bf(EA)
    CA = np.zeros((68, 256), np.float32)
    pos = 16 * n + 31
    CA[64] = pos // 128
    CA[65] = pos % 128
    CA[66] = 1.0
    CA[67] = 1.0
    T["t_CA"] = _bf(CA)
    slopes = 2.0 ** (-8.0 * np.arange(1, 9) / 8)
    qa = np.zeros((4, 16, 2, 4, 128), np.float32)
    for ti in range(16):
        tq0 = HALF + ti * 128
        for g in range(2):
            for r in range(4):
                sl = slopes[4 * g + r]
                qa[0, ti, g, r, :] = 128.0 * sl
                qa[1, ti, g, r, :] = sl
                qa[2, ti, g, r, :] = -sl * tq0
                qa[3, ti, g, r, :] = -sl * ii
    T["t_qaug"] = _bf(qa.reshape(4, 16, 2, 512))
    caus = np.where(ii[:, None] <= ii[None, :], 0.0, NEGM).astype(np.float32)
    T["t_caus"] = _bf(np.tile(caus, (1, 4)))
    win = np.where(ii[:, None] > ii[None, :], 0.0, NEGM).astype(np.float32)
    T["t_win"] = _bf(np.tile(win, (1, 4)))
    cs = n * 16
    js = np.arange(63) * 64
    ov = ((cs[:, None] < js[None, :] + 64) & (cs[:, None] + 32 > js[None, :]) & (n[:, None] < 255)).astype(np.float32)
    T["t_ovl"] = _bf(ov.reshape(2, 128, 63).transpose(1, 0, 2))
    return T


_CACHE = {}


def kernel(x, mem, g_mix, w_in, b_in, ml_conv, ml_norm_g, cmp_pe, cmp_w1, cmp_w2, g_mem, w_mem_kv,
           w_branch, w_out, g_ffn, w_ff1, w_ff2, g_final, _debug=False, _stop_after=None, _cores=8):
    f = lambda a: np.ascontiguousarray(np.asarray(a, dtype=np.float32))
    x = f(x)
    mem = f(mem)
    shared = {
        "w_in": f(w_in)[0], "b_in": f(b_in).reshape(1, D_IN), "ml_conv": f(ml_conv)[0], "ml_norm_g": f(ml_norm_g).reshape(1, 512),
        "cmp_pe": f(cmp_pe)[0], "cmp_w1": f(cmp_w1)[0], "cmp_w2": f(cmp_w2)[0],
        "g_mix": f(g_mix).reshape(1, D), "g_mem": f(g_mem).reshape(1, D), "g_ffn": f(g_ffn).reshape(1, D), "g_final": f(g_final).reshape(1, D),
        "w_mem_kv": f(w_mem_kv)[0], "w_branch": f(w_branch)[0], "w_out": f(w_out)[0], "w_ff1": f(w_ff1)[0], "w_ff2": f(w_ff2)[0],
    }
    key = (_debug, _stop_after)
    if key not in _CACHE:
        _CACHE[key] = build(debug=_debug, stop_after=_stop_after)
    nc, _es = _CACHE[key]
    tabs = [make_tables(0), make_tables(1)]
    in_maps = []
    for core in range(_cores):
        b, s = core // 2, core % 2
        if s == 1:
            xcore = x[b]
        else:
            xcore = np.concatenate([x[b, :HALF], x[b, :HALF]], axis=0)
        m = dict(shared)
        m["xc"] = np.ascontiguousarray(xcore)
        m["mem"] = mem[b]
        m.update(tabs[s])
        in_maps.append(m)
    res = run_bass_kernel_spmd(nc, in_maps, core_ids=list(range(_cores)))
    out = np.zeros((4, SEQ, D), np.float32)
    for core in range(_cores):
        b, s = core // 2, core % 2
        out[b, s * HALF:(s + 1) * HALF] = res.results[core]["y"]
    if _debug:
        kernel.last = res.results
    return out
```

```python
import os
import numpy as np
import ml_dtypes
from contextlib import ExitStack
import concourse.bass as bass
import concourse.mybir as mybir
from concourse.bass_utils import run_bass_kernel_spmd

F32 = mybir.dt.float32
BF16 = mybir.dt.bfloat16
AF = mybir.ActivationFunctionType
ALU = mybir.AluOpType

D = 1024
SEQ = 4096
HALF = 2048
NT = 2
L = 128 * NT
NST = SEQ // L
NSTP = NST // 2
NTILE = SEQ // 128
D_IN = 6944
EPS = 1e-6
NEGM = -30000.0
WB = 4352
NWBUF = int(os.environ.get("K_NW", "4"))

C_MLQ, C_MLK, C_MLV, C_MLO, C_MLI, C_MLF = 0, 512, 1024, 1536, 2048, 2052
C_NSQ, C_KC, C_VC, C_KS, C_VS, C_KW, C_VW, C_NSG, C_XAQ, C_MG = 2056, 2568, 2696, 2824, 2952, 3080, 3208, 3336, 3360, 3872

DEBUG = {}


class Sched:
    ENG = ["pe", "act", "dve", "pool", "sp"]

    def __init__(self, nc, same_engine_sync=True, reorder=True):
        self.nc = nc
        self.ops = []
        self.same = same_engine_sync
        self.reorder = reorder
        self.tag = "setup"

    def add(self, eng, fn, reads=(), writes=(), dma=None, cost=0.3, nbytes=0):
        self.ops.append(dict(eng=eng, fn=fn, reads=tuple(reads), writes=tuple(writes), dma=dma, tag=self.tag, cost=cost, nbytes=nbytes))

    def _schedule(self, ops):
        import heapq
        n = len(ops)
        succ = [[] for _ in range(n)]
        indeg = [0] * n
        for i, op in enumerate(ops):
            indeg[i] = len(op["alldeps"])
            for d in op["alldeps"]:
                succ[d].append(i)
        finish = [0.0] * n
        ready_t = [0.0] * n
        PRIO = os.environ.get("K_PRIO", "1") == "1"
        blevel = [0.0] * n
        if PRIO:
            for i in range(n - 1, -1, -1):
                c = ops[i]["cost"] if ops[i]["dma"] is None else (ops[i]["nbytes"] / 230e3 + 2.0)
                m = 0.0
                for s_ in succ[i]:
                    if blevel[s_] > m:
                        m = blevel[s_]
                blevel[i] = c + m
        eng_free = {e: 0.0 for e in self.ENG}
        future = {e: [] for e in self.ENG}
        avail = {e: [] for e in self.ENG}
        for i, op in enumerate(ops):
            if indeg[i] == 0:
                heapq.heappush(future[op["eng"]], (0.0, i))
        dma_free = 0.0
        order = []
        BW = float(os.environ.get("K_BW", "230")) * 1e3
        LAT = float(os.environ.get("K_LAT", "0.3"))
        while len(order) < n:
            best = None
            for e in self.ENG:
                f, a = future[e], avail[e]
                while f and f[0][0] <= eng_free[e]:
                    t_, i_ = heapq.heappop(f)
                    heapq.heappush(a, (-blevel[i_], i_) if PRIO else i_)
                if a:
                    cand = (eng_free[e], (a[0][1] if PRIO else a[0]), e, True)
                elif f:
                    cand = (f[0][0], f[0][1], e, False)
                else:
                    continue
                if best is None or cand[:2] < best[:2]:
                    best = cand
            start, i, e, from_avail = best
            if from_avail:
                heapq.heappop(avail[e])
            else:
                heapq.heappop(future[e])
            op = ops[i]
            if op["dma"] is not None:
                eng_free[e] = start + 0.08
                t0 = max(start, dma_free)
                dma_free = t0 + op["nbytes"] / BW
                finish[i] = dma_free + float(os.environ.get('K_DLAT', '2.0'))
            else:
                finish[i] = start + op["cost"]
                eng_free[e] = finish[i]
            order.append(i)
            for s_ in succ[i]:
                if op["dma"] is not None and ops[s_]["dma"] == op["dma"]:
                    ready_t[s_] = max(ready_t[s_], start)
                else:
                    lat = 0.0 if (ops[s_]["eng"] == e and op["dma"] is None) else LAT
                    ready_t[s_] = max(ready_t[s_], finish[i] + lat)
                indeg[s_] -= 1
                if indeg[s_] == 0:
                    heapq.heappush(future[ops[s_]["eng"]], (ready_t[s_], s_))
        self.est_total = max(finish) if n else 0.0
        return order

    def finalize(self, es):
        nc, ops = self.nc, self.ops
        last_w, readers = {}, {}
        for i, op in enumerate(ops):
            deps = set()
            for b in op["reads"]:
                if b in last_w:
                    deps.add(last_w[b])
            for b in op["writes"]:
                if b in last_w:
                    deps.add(last_w[b])
                deps.update(readers.get(b, ()))
            deps.discard(i)
            nd = set()
            for d in deps:
                dop = ops[d]
                if dop["dma"] is not None and op["dma"] is not None and dop["dma"] == op["dma"]:
                    nd |= dop["alldeps"]
                    nd.add(d) if dop["eng"] == op["eng"] else None
                else:
                    nd.add(d)
            op["alldeps"] = nd
            for b in op["reads"]:
                readers.setdefault(b, []).append(i)
            for b in op["writes"]:
                last_w[b] = i
                readers[b] = []
        order = self._schedule(ops) if (self.reorder and os.environ.get("K_REORDER", "1") == "1") else list(range(len(ops)))
        pos = {i: p for p, i in enumerate(order)}
        for i, op in enumerate(ops):
            assert all(pos[d] < pos[i] for d in op["alldeps"])
        needed = set()
        for i, op in enumerate(ops):
            nd = set()
            for d in op["alldeps"]:
                dop = ops[d]
                if dop["dma"] is None and op["dma"] is None and dop["eng"] == op["eng"]:
                    if op["eng"] == "pe" or not self.same:
                        continue
                if dop["dma"] is not None and op["dma"] is not None and dop["dma"] == op["dma"]:
                    continue
                nd.add(d)
            op["deps"] = nd
            needed |= nd
        sems, cnt = {}, {}

        def getsem(key):
            if key not in sems:
                sems[key] = es.enter_context(nc.semaphore("s%d" % len(sems)))
                cnt[key] = 0
            return sems[key]

        for i in order:
            op = ops[i]
            if op["dma"] is not None:
                key = ("d", op["dma"])
                getsem(key)
                cnt[key] += 16
                op["sig"] = (key, cnt[key], 16)
            elif i in needed:
                key = ("e", op["eng"])
                getsem(key)
                cnt[key] += 1
                op["sig"] = (key, cnt[key], 1)
            else:
                op["sig"] = None
        per = {e: [] for e in self.ENG}
        for i in order:
            per[ops[i]["eng"]].append(i)
        self.order = order
        self.per = per
        self.nwaits = 0
        self.nsems = len(sems)

        def run(eng_name, e):
            w = {}
            for i in per[eng_name]:
                op = ops[i]
                need = {}
                for d in op["deps"]:
                    key, val, _ = ops[d]["sig"]
                    if need.get(key, 0) < val:
                        need[key] = val
                for key, val in need.items():
                    if w.get(key, 0) < val:
                        e.wait_ge(sems[key], val)
                        w[key] = val
                        self.nwaits += 1
                ins = op["fn"](e)
                if op["sig"] is not None:
                    key, val, inc = op["sig"]
                    ins.then_inc(sems[key], inc)
            if eng_name == "sp":
                for key in sems:
                    if key[0] == "d":
                        e.wait_ge(sems[key], cnt[key])

        block = es.enter_context(nc.Block())

        @block.sync
        def _(e):
            run("sp", e)

        @block.scalar
        def _(e):
            run("act", e)

        @block.vector
        def _(e):
            run("dve", e)

        @block.gpsimd
        def _(e):
            run("pool", e)

        @block.tensor
        def _(e):
            run("pe", e)


def build(debug=False, stop_after=None):
    nc = bass.Bass("TRN2", target_bir_lowering=False)
    es = ExitStack()
    S = Sched(nc)

    def fsz(ap):
        n = 1
        for d in ap.shape[1:]:
            n *= d
        return n

    def MM(out, lhsT, rhs, start, stop, reads, writes):
        n = max(fsz(rhs), 64)
        c = n / 2400.0 * (4.0 if rhs.dtype == F32 else 1.0) + 0.035
        S.add("pe", lambda e: e.matmul(out, lhsT=lhsT, rhs=rhs, start=start, stop=stop), reads, writes, cost=c)

    def TR(out, in_, ident, reads, writes):
        S.add("pe", lambda e: e.transpose(out=out, in_=in_, identity=ident), reads, writes, cost=0.09)

    def ACT(out, in_, func, reads, writes, bias=None, scale=1.0, accum_out=None):
        kw = {}
        if bias is not None:
            kw["bias"] = bias
        if accum_out is not None:
            kw["accum_out"] = accum_out
        S.add("act", lambda e: e.activation(out=out, in_=in_, func=func, scale=scale, **kw), reads, writes, cost=fsz(in_) / 1200.0 + 0.2)

    def dvecost(eng, ap):
        return fsz(ap) / (900.0 if eng == "dve" else 400.0) + (0.07 if eng == "dve" else 0.3)

    def TT(eng, out, in0, in1, op, reads, writes):
        S.add(eng, lambda e: e.tensor_tensor(out=out, in0=in0, in1=in1, op=op), reads, writes, cost=dvecost(eng, in0))

    def TS(eng, out, in0, s1, op0, reads, writes, s2=None, op1=None):
        if op1 is None:
            S.add(eng, lambda e: e.tensor_scalar(out=out, in0=in0, scalar1=s1, scalar2=None, op0=op0), reads, writes, cost=dvecost(eng, in0))
        else:
            S.add(eng, lambda e: e.tensor_scalar(out=out, in0=in0, scalar1=s1, scalar2=s2, op0=op0, op1=op1), reads, writes, cost=dvecost(eng, in0))

    def STT(out, in0, scalar, in1, op0, op1, reads, writes):
        S.add("dve", lambda e: e.scalar_tensor_tensor(out=out, in0=in0, scalar=scalar, in1=in1, op0=op0, op1=op1), reads, writes, cost=dvecost("dve", in0) * 1.3)

    def CP(eng, out, in_, reads, writes):
        if eng == "act":
            S.add("act", lambda e: e.copy(out=out, in_=in_), reads, writes, cost=fsz(in_) / 1200.0 + 0.2)
        else:
            S.add(eng, lambda e: e.tensor_copy(out=out, in_=in_), reads, writes, cost=dvecost(eng, in_))

    def EVB(out, in_, bias_col, reads, writes, scale=None):
        if os.environ.get("K_EVB", "0") == "0":
            ACT(out, in_, AF.Identity, reads, writes, bias=bias_col, scale=(1.0 if scale is None else scale))
            return
        if scale is None:
            S.add("dve", lambda e: e.tensor_scalar(out=out, in0=in_, scalar1=bias_col, scalar2=None, op0=ALU.add), reads, writes, cost=dvecost("dve", in_))
        else:
            S.add("dve", lambda e: e.tensor_scalar(out=out, in0=in_, scalar1=scale, scalar2=bias_col, op0=ALU.mult, op1=ALU.add), reads, writes, cost=dvecost("dve", in_))

    def SIGT(out, in_, reads, writes, hbias=None, scale=1.0):
        ACT(out, in_, AF.Tanh, reads, writes, bias=hbias, scale=0.5 * scale)
        TS("dve", out, out, 0.5, ALU.mult, writes, writes, s2=0.5, op1=ALU.add)

    def RCP(out, in_, reads, writes):
        S.add("dve", lambda e: e.reciprocal(out=out, in_=in_), reads, writes, cost=dvecost("dve", in_) + 0.1)

    def SQRT(out, in_, reads, writes):
        S.add("act", lambda e: e.sqrt(out=out, in_=in_), reads, writes, cost=fsz(in_) / 1200.0 + 0.2)

    def MSET(eng, ap, val, writes, reads=()):
        S.add(eng, lambda e: e.memset(ap, val), reads, writes, cost=dvecost(eng, ap))

    def DMA(eng, out, in_, reads, writes, key, slow=False, nbm=1):
        nb = out.shape[0] * fsz(out) * (4 if out.dtype == F32 else 2) * nbm
        if slow:
            S.add(eng, lambda e: e.dma_start(out=out, in_=in_, allow_slow_non_contiguous=True), reads, writes, dma=key, nbytes=nb)
        else:
            S.add(eng, lambda e: e.dma_start(out=out, in_=in_), reads, writes, dma=key, nbytes=nb)

    def din(name, shape, dt=F32):
        return nc.dram_tensor(name, list(shape), dt, kind="ExternalInput").ap()

    xc = din("xc", [SEQ, D])
    mem = din("mem", [256, D])
    w_in = din("w_in", [D, D_IN])
    b_in = din("b_in", [1, D_IN])
    ml_conv = din("ml_conv", [4, 1024])
    ml_norm_g = din("ml_norm_g", [1, 512])
    cmp_pe = din("cmp_pe", [2, 32, 64])
    cmp_w1 = din("cmp_w1", [2, 2048, 256])
    cmp_w2 = din("cmp_w2", [2, 256, 64])
    g_mix = din("g_mix", [1, D])
    g_mem = din("g_mem", [1, D])
    g_ffn = din("g_ffn", [1, D])
    g_final = din("g_final", [1, D])
    w_mem_kv = din("w_mem_kv", [D, D])
    w_branch = din("w_branch", [3, 512, D])
    w_out = din("w_out", [D, D])
    w_ff1 = din("w_ff1", [D, 4096])
    w_ff2 = din("w_ff2", [4096, D])
    t_flag = din("t_flag", [128, 2])
    t_identf = din("t_identf", [128, 128])
    t_U = din("t_U", [128, 128])
    t_okc = din("t_okc", [128, 2, HALF], BF16)
    t_bsel = din("t_bsel", [128, 16, 64], BF16)
    t_EA = din("t_EA", [68, NTILE, 128], BF16)
    t_CA = din("t_CA", [68, 256], BF16)
    t_qaug = din("t_qaug", [4, 16, 2, 512], BF16)
    t_caus = din("t_caus", [128, 512], BF16)
    t_win = din("t_win", [128, 512], BF16)
    t_ovl = din("t_ovl", [128, 2, 63], BF16)
    y_out = nc.dram_tensor("y", [HALF, D], F32, kind="ExternalOutput").ap()

    def sb(name, shape, dt=F32):
        return es.enter_context(nc.sbuf_tensor(name, list(shape), dt))

    def ps(name, shape, dt=F32):
        return es.enter_context(nc.psum_tensor(name, list(shape), dt))

    def dbg(name, tile_ap, reads, shape):
        if not debug:
            return
        o = nc.dram_tensor("dbg_" + name, list(shape), F32, kind="ExternalOutput").ap()
        DEBUG[name] = tuple(shape)
        DMA("sp" if tile_ap.dtype == F32 else "pool", o, tile_ap, reads, [], ("dbg", name))

    psT = ps("psT", [128, 1024], BF16)

    def nextT():
        i = rot["T"] % 2 if os.environ.get("K_T2", "0") == "1" else 0
        rot["T"] += 1
        return psT[:, i * 512:(i + 1) * 512], ("psT", i)
    NPSG = int(os.environ.get("K_PSG", "4"))
    NPSC = int(os.environ.get("K_PSC", "3"))
    psG = [ps("psG%d" % i, [128, 512]) for i in range(NPSG)]
    psC = [ps("psC%d" % i, [128, 512]) for i in range(NPSC)]
    rot = {"G": 0, "C": 0, "P": 0, "W": 0, "X": 0, "S": 0, "T": 0, "N": 0, "W2": 0, "V": 0}

    def nextG():
        i = rot["G"] % NPSG
        rot["G"] += 1
        return psG[i], ("psG", i)

    def nextC():
        i = rot["C"] % NPSC
        rot["C"] += 1
        return psC[i], ("psC", i)

    identf = sb("identf", [128, 128])
    identb = sb("identb", [128, 128], BF16)
    Uf = sb("Uf", [128, 128])
    onesf = sb("onesf", [128, 128])
    flag2 = sb("flag2", [128, 2])
    ones2 = sb("ones2", [128, 2])
    mhalf = sb("mhalf", [128, 4])
    okc = sb("okc", [128, 2, HALF], BF16)
    bsel = sb("bsel", [128, 16, 64], BF16)
    EA = sb("EA", [68, NTILE, 128], BF16)
    CA = sb("CA", [68, 256], BF16)
    caus1 = sb("caus", [128, 128], BF16)
    winm1 = sb("winm", [128, 128], BF16)
    gmix_r = sb("gmix_r", [128, D], BF16)
    gffn_r = sb("gffn_r", [128, D], BF16)
    gfin_r = sb("gfin_r", [128, D])
    hres = sb("hres", [128, NT, D])
    gmem_r = hres[:, 0, :]
    convrow = hres[0:4, 1, :]
    gml_r = sb("gml_r", [128, 512])
    biasTM = sb("biasTM", [128, 1312])
    biasTMf = sb("biasTMf", [128, 256])
    brow = sb("brow", [64, 128])
    biasFM = sb("biasFM", [128, 64])
    hbiasFM = sb("hbiasFM", [128, 64])
    convw = sb("convw", [128, 8, 4])

    def load(out_ap, in_ap, tok, eng="sp"):
        DMA(eng, out_ap, in_ap, [], [tok], tok)

    load(identf[:], t_identf, "identf")
    load(Uf[:], t_U, "Uf")
    load(flag2[:], t_flag, "flag2")
    caus = caus1[:].unsqueeze(1).to_broadcast([128, 4, 128])
    winm = winm1[:].unsqueeze(1).to_broadcast([128, 4, 128])
    load(gmix_r[:], g_mix.partition_broadcast(128).squeeze(1), "gmix_r", eng="pool")
    HR = [("hres", t, dh) for t in range(NT) for dh in range(2)]
    DMA("sp", convrow, ml_conv, [], HR, "convrow")

    def late_tables():
        load(okc[:], t_okc, "okc")
        load(bsel[:], t_bsel, "bsel")
        load(EA[:], t_EA, "EA")
        load(CA[:], t_CA, "CA")
        load(caus1[:], t_caus[:, 0:128], "caus")
        load(winm1[:], t_win[:, 0:128], "winm")
        load(gffn_r[:], g_ffn.partition_broadcast(128).squeeze(1), "gffn_r", eng="pool")
        load(gfin_r[:], g_final.partition_broadcast(128).squeeze(1), "gfin_r")
        load(gml_r[:], ml_norm_g.partition_broadcast(128).squeeze(1), "gml_r")
    bpb = b_in.partition_broadcast(128).squeeze(1)
    load(biasTM[:, 0:512], bpb[:, C_MLV:C_MLV + 512], "biasTM")
    load(biasTM[:, 512:640], bpb[:, C_VS:C_VS + 128], "biasTM")
    load(biasTM[:, 640:768], bpb[:, C_VW:C_VW + 128], "biasTM")
    load(biasTM[:, 768:776], bpb[:, C_MLI:C_MLI + 8], "biasTM")
    load(biasTM[:, 776:800], bpb[:, C_NSG:C_NSG + 24], "biasTM")
    load(biasTM[:, 800:1312], bpb[:, C_MLO:C_MLO + 512], "biasTM")
    MSET("dve", onesf[:], 1.0, ["onesf"])
    MSET("dve", ones2[:], 1.0, ["ones2"])
    MSET("dve", mhalf[:], -0.5, ["mhalf"])
    CP("dve", identb[:], identf[:], ["identf"], ["identb"])
    TS("dve", biasTMf[:], biasTM[:, 512:768], flag2[:, 0:1], ALU.mult, ["biasTM", "flag2"], ["biasTMf"])
    MSET("dve", brow[32:40, :], 0.0, ["brow0"])
    FMROW = {}
    rows_free = [i for i in range(64) if not (32 <= i < 40)]

    def newrow():
        return rows_free.pop(0)

    def brow_load(r, c0, n, dst0=0, full=True):
        DMA("sp", brow[r:r + 1, dst0:dst0 + n], b_in[0:1, c0:c0 + n], ([] if full else ["brow0"]), ["brow"], "brow")

    for c in range(4):
        r = newrow(); FMROW[("mlk", c)] = r; brow_load(r, C_MLK + c * 128, 128)
    for c in range(4):
        r = newrow(); FMROW[("mlq", c)] = r; brow_load(r, C_MLQ + c * 128, 128)
    r = newrow(); FMROW["ks"] = r; brow_load(r, C_KS, 128)
    r = newrow(); FMROW["kw"] = r; brow_load(r, C_KW, 128)
    for kv, cbase in ((0, C_KC), (1, C_VC)):
        for g in range(2):
            r = newrow(); FMROW[("kc", kv, g)] = r
            brow_load(r, cbase + g * 64, 64, 0)
            brow_load(r, cbase + g * 64, 64, 64)
    for p_ in range(4):
        r = 32 + p_
        FMROW[("nsq", p_)] = r
        brow_load(r, C_NSQ + p_ * 64, 64, 0, full=False)
        brow_load(r, C_NSQ + (p_ + 4) * 64, 64, 64, full=False)
    for c in range(4):
        r = newrow(); FMROW[("xaq", c)] = r; brow_load(r, C_XAQ + c * 128, 128)
    for c in range(24):
        r = newrow(); FMROW[("mg", c)] = r; brow_load(r, C_MG + c * 128, 128)
    r = 0
    assert r <= 64
    pg, pgt = nextG()
    TR(pg[:, 0:64], brow[:, :], identf[0:64, 0:64], ["brow0", "brow", "identf"], [pgt])
    CP("dve", biasFM[:], pg[:, 0:64], [pgt], ["biasFM"])
    nq0 = FMROW[("nsq", 0)]
    TS("dve", biasFM[:, nq0:nq0 + 8], biasFM[:, nq0:nq0 + 8], 0.125, ALU.mult, ["biasFM"], ["biasFM"])
    TS("dve", hbiasFM[:], biasFM[:], 0.5, ALU.mult, ["biasFM"], ["hbiasFM"])
    pg2, pgt2 = nextG()
    for c in range(8):
        TR(pg2[:, c * 4:(c + 1) * 4], convrow[:, c * 128:(c + 1) * 128], identf[0:4, 0:4], HR + ["identf"], [pgt2])
    CP("dve", convw[:].rearrange("p c j -> p (c j)"), pg2[:, 0:32], [pgt2], ["convw"])

    wbufs = [sb("wbuf%d" % i, [128, WB], BF16) for i in range(NWBUF)]
    WSPEC = []
    WSLOT = {}

    def wreg(key, K, pieces, zero=False):
        WSLOT[key] = len(WSPEC)
        WSPEC.append((key, K, pieces, zero))

    for kv in range(2):
        WSLOT[("w1", kv)] = len(WSPEC)
        WSPEC.append((("w1", kv), 16, "w1", kv))
    wreg("memk", 8, [(0, 512, w_mem_kv[:, 0:512])])
    wreg("memv", 8, [(0, 512, w_mem_kv[:, 512:1024])])
    wreg("mlk", 8, [(0, 512, w_in[:, C_MLK:C_MLK + 512])])
    wreg("mlv", 8, [(0, 512, w_in[:, C_MLV:C_MLV + 512])])
    wreg("cbuf", 8, [(0, 128, w_in[:, C_KS:C_KS + 128]), (128, 128, w_in[:, C_KW:C_KW + 128]),
                     (256, 64, w_in[:, C_KC:C_KC + 64]), (320, 64, w_in[:, C_KC:C_KC + 64]),
                     (384, 64, w_in[:, C_KC + 64:C_KC + 128]), (448, 64, w_in[:, C_KC + 64:C_KC + 128])])
    wreg("dbuf", 8, [(0, 64, w_in[:, C_VC:C_VC + 64]), (64, 64, w_in[:, C_VC:C_VC + 64]),
                     (128, 64, w_in[:, C_VC + 64:C_VC + 128]), (192, 64, w_in[:, C_VC + 64:C_VC + 128]),
                     (256, 128, w_in[:, C_VS:C_VS + 128]), (384, 128, w_in[:, C_VW:C_VW + 128]),
                     (512, 8, w_in[:, C_MLI:C_MLI + 8]), (520, 24, w_in[:, C_NSG:C_NSG + 24])])
    wreg("mlq", 8, [(0, 512, w_in[:, C_MLQ:C_MLQ + 512])])
    wreg("mlo", 8, [(0, 512, w_in[:, C_MLO:C_MLO + 512])])
    pcs = []
    for p_ in range(4):
        pcs.append((p_ * 128, 64, w_in[:, C_NSQ + p_ * 64:C_NSQ + (p_ + 1) * 64]))
        pcs.append((p_ * 128 + 64, 64, w_in[:, C_NSQ + (p_ + 4) * 64:C_NSQ + (p_ + 5) * 64]))
    wreg("nsq", 8, pcs)
    wreg("xaq", 8, [(0, 512, w_in[:, C_XAQ:C_XAQ + 512])])
    for j in range(3):
        for hf in range(2):
            wreg(("mg", j, hf), 8, [(0, 512, w_in[:, C_MG + j * 1024 + hf * 512:C_MG + j * 1024 + (hf + 1) * 512])])
        wreg(("br", j), 4, [(0, 1024, w_branch[j])])
    for dh in range(2):
        wreg(("out", dh), 8, [(0, 512, w_out[:, dh * 512:(dh + 1) * 512])])
    for fb in range(8):
        wreg(("ff1", fb), 8, [(0, 512, w_ff1[:, fb * 512:(fb + 1) * 512])])
    for dh in range(2):
        for fg in range(4):
            wreg(("ff2", dh, fg), 8, [(0, 512, w_ff2[fg * 1024:(fg + 1) * 1024, dh * 512:(dh + 1) * 512])])
    NSLOT = len(WSPEC)
    wscr = nc.dram_tensor("wscr", [NSLOT, 128, WB], BF16, kind="Internal").ap()
    MSET("pool", wbufs[0][:], 0.0, [("wbuf", 0)])

    def emit_conv(idx, extra_reads=()):
        key, K, pieces, zero = WSPEC[idx]
        if key in FT_KEYS:
            return
        er = list(extra_reads)
        if pieces == "w1":
            kv = zero
            src = cmp_w1[kv].rearrange("(lp two d) h -> two d lp h", two=2, d=64)
            v = wscr[idx][:, 0:4096].rearrange("p (k w) -> p k w", k=16)
            DMA("pool", v[0:64], src[1], er, [("wscr", idx)], ("wscr", idx))
            DMA("pool", v[64:128], src[0], er, [("wscr", idx)], ("wscr", idx))
            return
        W = 512 if zero else sum(n for _, n, _ in pieces)
        v = wscr[idx][:, 0:K * W].rearrange("p (k w) -> p k w", k=K)
        rd = []
        if zero:
            DMA("pool", wscr[idx][:, 0:K * W], wbufs[0][:, 0:K * W], [("wbuf", 0)] + er, [("wscrz", idx)], ("wscrz", idx))
            rd = [("wscrz", idx)]
        for off, n, src_ap in pieces:
            DMA("pool", v[:, :, off:off + n], src_ap.rearrange("(k p) n -> p k n", p=128), rd + er, [("wscr", idx)], ("wscr", idx))

    converted = set()
    FT_KEYS = set(["mlk", "mlv", "cbuf", "dbuf", ("w1", 0), ("w1", 1)])

    def wfetch(key, donetok=None):
        idx = WSLOT[key]
        _, K, pieces, zero = WSPEC[idx]
        W = 256 if pieces == "w1" else (512 if zero else sum(n for _, n, _ in pieces))
        nhead = int(os.environ.get("K_NHEAD", "0"))
        if nhead == 0:
            i = rot["W"] % NWBUF
            rot["W"] += 1
        elif S.tag in ("H_merge", "I_outproj", "J_ffn", "K_final"):
            i = nhead + rot["W2"] % (NWBUF - nhead)
            rot["W2"] += 1
        else:
            i = rot["W"] % nhead
            rot["W"] += 1
        tok = ("wbuf", i)
        extra = [donetok] if donetok else []
        if key in FT_KEYS and idx not in converted:
            converted.add(idx)
            bv = wbufs[i][:, 0:K * W].rearrange("p (k w) -> p k w", k=K)
            if pieces == "w1":
                kv = zero
                src = cmp_w1[kv].rearrange("(lp two d) h -> two d lp h", two=2, d=64)
                DMA("pool", bv[0:64], src[1], [], [tok], tok, nbm=2)
                DMA("pool", bv[64:128], src[0], [], [tok] + extra, tok, nbm=2)
            else:
                if zero:
                    MSET("pool", wbufs[i][:, 0:K * W], 0.0, [tok])
                for pi, (off, n, src_ap) in enumerate(pieces):
                    last = pi == len(pieces) - 1
                    DMA("pool", bv[:, :, off:off + n], src_ap.rearrange("(k p) n -> p k n", p=128), [], [tok] + (extra if last else []), tok, nbm=2)
            DMA("sp", wscr[idx][:, 0:K * W], wbufs[i][:, 0:K * W], [tok], [("wscr", idx)], ("wscr", idx))
        else:
            DMA("sp", wbufs[i][:, 0:K * W], wscr[idx][:, 0:K * W], [("wscr", idx)], [tok] + extra, tok)
        return wbufs[i][:, 0:K * W].rearrange("p (k w) -> p k w", k=K), tok

    ksT = sb("ksT", [128, SEQ], BF16)
    kwT = sb("kwT", [128, 1024], BF16)
    vs_aug = sb("vs_aug", [128, NTILE, 2, 65], BF16)
    vw_aug = sb("vw_aug", [128, 8, 2, 65], BF16)
    kcT = sb("kcT", [128, 256], BF16)
    vcT = sb("vcT", [128, 256], BF16)
    vcx = sb("vcx", [128, 2, 2, 128], BF16)
    mkT = sb("mkT", [128, 4, 256], BF16)
    mv_aug = sb("mv_aug", [128, 2, 4, 129], BF16)
    W2p = sb("W2p", [128, 2, 2, 2, 128], BF16)
    pep = sb("pep", [128, 2, 16], BF16)
    hb = sb("hb", [128, 2, 2])
    kcin2 = sb("kcin2", [128, 4, L + 17], BF16)
    Cst = sb("Cst", [128, 4, 129])
    Cb = sb("Cb", [128, 4, 129], BF16)
    prek = sb("prek", [128, 4, 3])
    preq = sb("preq", [128, 4, 3])

    MSET("dve", vs_aug[:], 1.0, [("vs_aug", i) for i in range(NST)])
    MSET("pool", vw_aug[:], 1.0, [("vw_aug", i) for i in range(8 // NT)])
    MSET("dve", kcT[:], 0.0, ["kcT"])
    MSET("dve", vcT[:], 0.0, ["vcT"])
    MSET("pool", kwT[:], 0.0, [("kwT", i) for i in range(8 // NT)])
    MSET("dve", mv_aug[:], 1.0, ["mv_aug"])
    MSET("dve", kcin2[:], 0.0, ["kcin2"])
    MSET("dve", Cst[:], 0.0, ["Cst"])
    MSET("dve", Cb[:], 0.0, ["Cb"])
    MSET("dve", prek[:], 0.0, [("prek", c) for c in range(4)])
    MSET("dve", preq[:], 0.0, [("preq", c) for c in range(4)])
    MSET("pool", W2p[:], 0.0, ["W2p0"])
    MSET("dve", vcx[:], 0.0, ["vcx"])
    MSET("dve", vcx[:, :, :, 64:65], 1.0, ["vcx"], ["vcx"])
    for g in range(2):
        DMA("sp", vcx[:, :, g, 65:128], t_ovl, ["vcx"], ["vcx_ovl"], "vcx_ovl")
    VCX = ["vcx", "vcx_ovl"]
    for kv in range(2):
        srcp = cmp_pe[kv].rearrange("(lp two) d -> two d lp", two=2)
        DMA("pool", pep[0:64, kv, :], srcp[1], [], ["pep"], "pep", slow=True)
        DMA("pool", pep[64:128, kv, :], srcp[0], [], ["pep"], "pep", slow=True)
        for half in range(2):
            for g in range(2):
                DMA("pool", W2p[:, kv, half, g, g * 64:(g + 1) * 64], cmp_w2[kv, half * 128:(half + 1) * 128, :], ["W2p0"], ["W2p"], "W2p")
    W2T = ["W2p0", "W2p"]
    NCONV0 = WSLOT["mlq"]
    first = [WSLOT[k] for k in ("mlk", "mlv", "cbuf", "dbuf", ("w1", 0), ("w1", 1))]
    for idx in first:
        emit_conv(idx)
    for idx in range(NSLOT):
        if WSPEC[idx][3] is True:
            emit_conv(idx)
    conv_rest = [WSLOT["memk"], WSLOT["memv"]] + [idx for idx in range(NCONV0, WSLOT[("ff1", 0)]) if WSPEC[idx][3] is not True]
    conv_ffn = list(range(WSLOT[("ff1", 0)], NSLOT))

    for kv in range(2):
        w1v, w1t = wfetch(("w1", kv))
        for half in range(2):
            pgx, pgxt = nextG()
            for lp in range(16):
                MM(pgx[:, 0:1], w1v[:, lp, half * 128:(half + 1) * 128], pep[:, kv, lp:lp + 1], lp == 0, lp == 15, [w1t, "pep"], [pgxt])
            CP("dve", hb[:, kv, half:half + 1], pgx[:, 0:1], [pgxt], ["hb"])

    stat = sb("stat", [128, 16])
    xbuf = [sb("xbuf%d" % i, [128, D]) for i in range(2)]
    xnbs = [sb("xnb%d" % i, [128, D], BF16) for i in range(2)]

    def nextX():
        i = rot["X"] % 2
        rot["X"] += 1
        return i

    def rms_scale(src_ap, src_toks, junk=None, junktok=None):
        if junk is None:
            junk, junktok = xnbs[0], ("xnb", 0)
        si = rot["S"] % 8
        rot["S"] += 1
        tok = ("stat", si)
        col = stat[:, si:si + 1]
        ACT(junk[:], src_ap, AF.Square, src_toks, [junktok, tok], accum_out=col)
        TS("dve", col, col, 1.0 / D, ALU.mult, [tok], [tok], s2=EPS, op1=ALU.add)
        TT("pool", col, col, mhalf[:, 0:1], ALU.pow, [tok, "mhalf"], [tok])
        return col, tok

    def rmsnorm_to_T(src_ap, src_toks, grep, grep_tok, dstT, dst_cols, dst_tok):
        gt_ = list(grep_tok) if isinstance(grep_tok, list) else [grep_tok]
        dst_tok = list(dst_tok) if isinstance(dst_tok, list) else [dst_tok]
        ni = rot["N"] % 2 if os.environ.get("K_XNB2", "1") == "1" else 0
        rot["N"] += 1
        xn_, xtok = xnbs[ni], ("xnb", ni)
        col, tok = rms_scale(src_ap, src_toks, xn_, xtok)
        STT(xn_[:], src_ap, col, grep[:], ALU.mult, ALU.mult, list(src_toks) + [tok] + gt_, [xtok])
        for hf in range(2):
            pt, ptt = nextT()
            for k in range(4):
                kk = hf * 4 + k
                TR(pt[:, k * 128:(k + 1) * 128], xn_[:, kk * 128:(kk + 1) * 128], identb[:], [xtok, "identb"], [ptt])
            CP("act", dstT[:, hf * 4:(hf + 1) * 4, dst_cols], pt.rearrange("p (k n) -> p k n", k=4), [ptt], dst_tok)

    aTraw = sb("aTraw", [128, 16 * L])
    aT = aTraw[:].bitcast(BF16).rearrange("p (f l) -> p f l", f=32)
    macc = aTraw[:, 8 * L:16 * L].rearrange("p (f l) -> p f l", f=8)
    memT = aT[:, 0:8, :]
    AT8 = [("aT", fc) for fc in range(8)]

    def mem_setup():
        DMA("sp", gmem_r, g_mem.partition_broadcast(128).squeeze(1), [], HR, "gmem_r")
        for mt in range(2):
            xi = nextX()
            DMA("sp", xbuf[xi][:], mem[mt * 128:(mt + 1) * 128, :], [], [("xbuf", xi)], ("xbuf", xi))
            rmsnorm_to_T(xbuf[xi][:], [("xbuf", xi)], gmem_r, HR, memT, slice(mt * 128, (mt + 1) * 128), AT8)
        wk, wt_ = wfetch("memk")
        for h in range(4):
            pgx, pgxt = nextG()
            for k in range(8):
                MM(pgx[:, 0:256], wk[:, k, h * 128:(h + 1) * 128], memT[:, k, :], k == 0, k == 7, [wt_] + AT8, [pgxt])
            CP("act", mkT[:, h, :], pgx[:, 0:256], [pgxt], ["mkT"])
        wv, wt_ = wfetch("memv")
        for mt in range(2):
            pgx, pgxt = nextG()
            for k in range(8):
                MM(pgx[:, 0:512], memT[:, k, mt * 128:(mt + 1) * 128], wv[:, k, :], k == 0, k == 7, [wt_] + AT8, [pgxt])
            CP("act", mv_aug[:, mt, :, 0:128], pgx[:, 0:512].rearrange("p (h d) -> p h d", h=4), [pgxt, "mv_aug"], ["mv_aug"])
    dbg("hb", hb[:], ["hb"], [128, 2, 2])
    dbg("biasFM", biasFM[:], ["biasFM"], [128, 64])
    dbg("convw", convw[:], ["convw"], [128, 8, 4])

    NXN = int(os.environ.get("K_NXN", "1"))
    xnTs = [sb("xnT%d" % i, [128, 8, L], BF16) for i in range(NXN)]
    XN = [xnTs[0]]
    XT = [("xnT", 0)]
    kT = sb("kT", [128, 4, L], BF16)
    qT = sb("qT", [128, 4, L], BF16)
    ktm = sb("ktm", [128, NT, 512], BF16)
    v_aug = sb("v_aug", [128, NT, 4, 129], BF16)
    og = sb("og", [128, NT, 512], BF16)
    Rms = [sb("Rm_%d" % i, [128, 4, 128]) for i in range(2)]
    otmp = Rms[0][:].rearrange("p h t -> p (h t)")
    gate = sb("gate", [128, NT, 8])
    gsig = sb("gsig", [128, NT, 24])
    nqT = sb("nqT", [128, NT, 8, 128], BF16)
    MSET("pool", nqT[:], 0.0, ["nqT"])
    xqT = sb("xqT", [128, 4, L], BF16)
    NA = sb("NA", [68, NT, 2, 512], BF16)
    yT = sb("yT", [128, 12, L], BF16)
    MSET("dve", v_aug[:], 1.0, [("v_aug", t) for t in range(NT)])

    def fm_group(wv_, wtok, col0, M=128):
        pgx, pgxt = nextG()
        for k in range(8):
            MM(pgx[0:M, 0:L], wv_[:, k, col0:col0 + M], XN[0][:, k, :], k == 0, k == 7, [wtok, XT[0]], [pgxt])
        return pgx, pgxt

    NCV = int(os.environ.get("K_NCV", "2"))
    pretmps = [sb("pretmp%d" % i, [128, L + 3]) for i in range(NCV)]
    convaccs = [sb("convacc%d" % i, [128, L]) for i in range(NCV)]

    def conv_silu(pre, pretok, pgx, pgxt, c, brow_i, wchunk, dst, dsttok):
        ci = rot["V"] % NCV
        rot["V"] += 1
        pretmp, convacc = pretmps[ci], convaccs[ci]
        ptk, cak = ("pretmp", ci), ("convacc", ci)
        pt = (pretok, c)
        CP("dve", pretmp[:, 0:3], pre[:, c, :], [pt, ptk], [ptk])
        ACT(pretmp[:, 3:3 + L], pgx[:, 0:L], AF.Identity, [pgxt, "biasFM", ptk], [ptk], bias=biasFM[:, brow_i:brow_i + 1])
        TS("dve", convacc[:], pretmp[:, 3:3 + L], convw[:, wchunk, 0:1], ALU.mult, [ptk, "convw"], [cak])
        for j in range(1, 4):
            STT(convacc[:], pretmp[:, 3 - j:3 - j + L], convw[:, wchunk, j:j + 1], convacc[:], ALU.mult, ALU.add, [ptk, "convw", cak], [cak])
        ACT(pretmp[:, 0:L], convacc[:], AF.Tanh, [cak, ptk], [ptk], scale=0.5)
        TS("dve", pretmp[:, 0:L], pretmp[:, 0:L], 0.5, ALU.mult, [ptk], [ptk], s2=0.5, op1=ALU.add)
        TT("dve", dst[:, c, :], pretmp[:, 0:L], convacc[:], ALU.mult, [ptk, cak], [dsttok])
        CP("dve", pre[:, c, :], pretmp[:, L:L + 3], [ptk], [pt])

    l1s = [sb("l1_%d" % i, [128, 4]) for i in range(2)]
    lfms = [sb("lfm_%d" % i, [128, 4]) for i in range(2)]
    expBs = [sb("expB_%d" % i, [128, 4, 128]) for i in range(2)]
    wcols = [sb("wcol_%d" % i, [128, 4]) for i in range(2)]
    ecols = [sb("ecol_%d" % i, [128, 4]) for i in range(2)]
    DT = sb("DT", [128, 4, 128], BF16)
    PT = sb("PT", [128, 512], BF16)
    qbT = sb("qbT", [128, 4, 128], BF16)
    ves = [sb("ve_%d" % i, [128, 4, 129], BF16) for i in range(2)]
    mst = sb("mst", [128, 16])
    yml = sb("yml", [128, 512], BF16)
    LNS = float(np.log(128.0 ** -0.5))

    l1_all = [sb("l1all%d" % i, [128, NT, 4]) for i in range(2)]
    lfm_all = [sb("lfmall%d" % i, [128, NT, 4]) for i in range(2)]

    def mlstm_gates(st):
        p = st % 2
        GT = [("gate", t) for t in range(NT)]
        ACT(l1_all[p][:], gate[:, :, 4:8], AF.Exp, GT, [("l1_all", p)], scale=-1.0)
        ACT(l1_all[p][:], l1_all[p][:], AF.Ln, [("l1_all", p)], [("l1_all", p)], bias=1.0)
        TS("dve", lfm_all[p][:], l1_all[p][:], -1.0, ALU.mult, [("l1_all", p)], [("lfm_all", p)])

    def mlstm_tile(st, t, own):
        gt = st * NT + t
        par = gt % 2
        l1, lfm, Rm, expB, wcol, ecol, ve = l1s[par], lfms[par], Rms[par], expBs[par], wcols[par], ecols[par], ves[par]
        T_ = lambda n: (n, par)
        cols = slice(t * 128, (t + 1) * 128)
        gtok = ("gate", t)
        lfm = lfm_all[st % 2][:, t, :]
        LFT = ("lfm_all", st % 2)
        for h in range(4):
            TS("dve", Rm[:, h, :], Uf[:], lfm[:, h:h + 1], ALU.mult, ["Uf", LFT], [T_("Rm")])
        pB, pBt = nextG()
        MM(pB[:, 0:512], onesf[:], Rm[:].rearrange("p h t -> p (h t)"), True, True, ["onesf", T_("Rm")], [pBt])
        pb2, pb2t = nextG()
        MM(pb2[:, 0:4], Uf[:], lfm, True, True, ["Uf", LFT], [pb2t])
        ACT(expB[:].rearrange("p h t -> p (h t)"), pB[:, 0:512], AF.Exp, [pBt], [T_("expB")])
        TT("dve", wcol[:], gate[:, t, 0:4], pb2[:, 0:4], ALU.subtract, [gtok, pb2t], [T_("wcol")])
        ACT(wcol[:], wcol[:], AF.Exp, [T_("wcol")], [T_("wcol")], bias=LNS)
        TT("dve", ecol[:], wcol[:], expB[:, :, 127], ALU.mult, [T_("wcol"), T_("expB")], [T_("ecol")])
        if own:
            for h in range(4):
                STT(DT[:, h, :], expB[:, h, :], wcol[:, h:h + 1], Uf[:], ALU.mult, ALU.mult, [T_("expB"), T_("wcol"), "Uf"], ["DT"])
            pS, pSt = nextG()
            for h in range(4):
                MM(pS[:, h * 128:(h + 1) * 128], kT[:, h, cols], qT[:, h, cols], True, True, ["kT", "qT"], [pSt])
            TT("dve", PT[:], pS[:, 0:512], DT[:].rearrange("p h t -> p (h t)"), ALU.mult, [pSt, "DT"], ["PT"])
            TT("pool", qbT[:], qT[:, :, cols], expB[:], ALU.mult, ["qT", T_("expB")], ["qbT"])
            pO = []
            for hp in range(2):
                po, pot = nextC()
                pO.append((po, pot))
                for hh in range(2):
                    h = hp * 2 + hh
                    MM(po[:, hh * 129:(hh + 1) * 129], PT[:, h * 128:(h + 1) * 128], v_aug[:, t, h, :], True, False, ["PT", ("v_aug", t)], [pot])
                    MM(po[:, hh * 129:(hh + 1) * 129], qbT[:, h, :], Cb[:, h, :], False, True, ["qbT", "Cb"], [pot])
            for hp in range(2):
                po, pot = pO[hp]
                pv = po[:, 0:258].rearrange("p (h c) -> p h c", h=2)
                ACT(mst[:, hp * 2:hp * 2 + 2], pv[:, :, 128], AF.Abs, [pot, "mstA"], ["mstA"])
            TS("dve", mst[:, 0:4], mst[:, 0:4], 1.0, ALU.max, ["mstA"], ["mstA"])
            RCP(mst[:, 0:4], mst[:, 0:4], ["mstA"], ["mstA"])
            for h in range(4):
                po, pot = pO[h // 2]
                hh = h % 2
                ACT(PT[:, 0:128], po[:, hh * 129:hh * 129 + 128], AF.Square, [pot, "mstA", "mstB"], ["PT", "mstB"], scale=mst[:, h:h + 1], accum_out=mst[:, 4 + h:5 + h])
            TS("dve", mst[:, 4:8], mst[:, 4:8], 1.0 / 128, ALU.mult, ["mstB"], ["mstB"], s2=EPS, op1=ALU.add)
            TT("pool", mst[:, 4:8], mst[:, 4:8], mhalf[:, 0:4], ALU.pow, ["mstB", "mhalf"], ["mstB"])
            TT("dve", mst[:, 8:12], mst[:, 0:4], mst[:, 4:8], ALU.mult, ["mstA", "mstB"], ["mstC"])
            for h in range(4):
                po, pot = pO[h // 2]
                hh = h % 2
                STT(yml[:, h * 128:(h + 1) * 128], po[:, hh * 129:hh * 129 + 128], mst[:, 8 + h:9 + h], og[:, t, h * 128:(h + 1) * 128], ALU.mult, ALU.mult,
                    [pot, "mstC", ("og", t)], ["yml"])
            pt, ptt = nextT()
            for h in range(4):
                TR(pt[:, h * 128:(h + 1) * 128], yml[:, h * 128:(h + 1) * 128], identb[:], ["yml", "identb"], [ptt])
            CP("act", yT[:, 0:4, cols], pt.rearrange("p (k n) -> p k n", k=4), [ptt], [("yT", 0, t)])
        if gt == NTILE - 1:
            return
        TT("dve", ve[:], v_aug[:, t, :, :], ecol[:].unsqueeze(2).to_broadcast([128, 4, 129]), ALU.mult, [("v_aug", t), T_("ecol")], [T_("ve")])
        for hp in range(2):
            pa, pat = nextC()
            for hh in range(2):
                h = hp * 2 + hh
                MM(pa[:, hh * 129:(hh + 1) * 129], ktm[:, t, h * 128:(h + 1) * 128], ve[:, h, :], True, True, [("ktm", t), T_("ve")], [pat])
            for hh in range(2):
                h = hp * 2 + hh
                STT(Cst[:, h, :], Cst[:, h, :], expB[:, h, 127:128], pa[:, hh * 129:(hh + 1) * 129], ALU.mult, ALU.add, [pat, T_("expB"), "Cst"], ["Cst"])
        if gt == NTILE // 2 - 1:
            TS("dve", Cst[:], Cst[:], flag2[:, 0:1], ALU.mult, ["Cst", "flag2"], ["Cst"])
        CP("act", Cb[:], Cst[:], ["Cst"], ["Cb"])

    NPB = int(os.environ.get("K_NP", "3"))
    PcT = [sb("PcT%d" % i, [128, 512], BF16) for i in range(NPB)]

    def nextP():
        i = rot["P"] % NPB
        rot["P"] += 1
        return PcT[i], ("PcT", i)

    nst = sb("nst", [128, 32])
    impacc = sb("impacc", [128, 64])
    score = sb("score", [128, 64])
    score2 = sb("score2", [128, 64])
    mx8 = sb("mx8", [128, 16])
    negm = sb("negm", [128, 64], BF16)
    ynsa = sb("ynsa", [128, 512], BF16)
    ytmp = sb("ytmp", [128, 64])
    yxa = sb("yxa", [128, 512], BF16)

    def attn_scores(pS, pSt, mms):
        n = len(mms)
        for i, (l_ap, r_ap, rd) in enumerate(mms):
            MM(pS[:, 0:512], l_ap, r_ap, i == 0, i == n - 1, rd, [pSt])

    def run_jobs(jobs):
        issued = []

        def issue_scores(j):
            if j.get("pre") is not None:
                j["pre"]()
            pS, pSt = nextG()
            attn_scores(pS, pSt, j["mms"])
            issued.append((pS, pSt))

        if not jobs:
            return
        issue_scores(jobs[0])
        for i, j in enumerate(jobs):
            if i + 1 < len(jobs):
                issue_scores(jobs[i + 1])
            pS, pSt = issued[i]
            P, Pt = nextP()
            ACT(P[:], pS[:, 0:512], AF.Exp, [pSt], [Pt], scale=j.get("scale", 1.0))
            j["pv"](P, Pt)
            if j.get("post") is not None:
                j["post"]()

    def nsa_group(st, t, g):
        gt = st * NT + t
        ti = gt - NTILE // 2
        cols = slice(t * 128, (t + 1) * 128)
        q4 = nqT[:, t, 4 * g:4 * g + 4, :]
        na_full = NA[0:68, t, g, :]
        na_aug = NA[64:68, t, g, :]
        natok = ("NA", t, g)
        pOC, pOCt = nextC()
        pOS, pOSt = nextC()
        pOW, pOWt = nextC()
        oc3 = pOC[:, 0:512].rearrange("p (r c) -> p r c", r=4)
        os3 = pOS[:, 0:260].rearrange("p (r c) -> p r c", r=4)
        ow3 = pOW[:, 0:260].rearrange("p (r c) -> p r c", r=4)
        jobs = []
        for nt in range(2):
            mms = [(kcT[:, nt * 128:(nt + 1) * 128], q4, ["kcT", "nqT"]),
                   (CA[64:68, nt * 128:(nt + 1) * 128], na_aug, ["CA", "NAaug"]),
                   (identb[:], okc[:, nt, ti * 128:(ti + 1) * 128].unsqueeze(1).to_broadcast([128, 4, 128]), ["identb", "okc"])]

            def pv_c(P, Pt, nt=nt):
                for r_ in range(4):
                    MM(pOC[:, r_ * 128:(r_ + 1) * 128], P[:, r_ * 128:(r_ + 1) * 128], vcx[:, nt, g, :], nt == 0 and r_ == 0, nt == 1, [Pt] + VCX, [pOCt])
            jobs.append(dict(mms=mms, pv=pv_c))

        def topk_dve():
            TS("dve", nst[:, 0:4], oc3[:, :, 64], 1e-30, ALU.max, [pOCt, "nstA"], ["nstA"])
            RCP(nst[:, 0:4], nst[:, 0:4], ["nstA"], ["nstA"])
            TS("dve", impacc[:, 0:63], oc3[:, 0, 65:128], nst[:, 0:1], ALU.mult, [pOCt, "nstA"], ["impacc"])
            for r_ in range(1, 4):
                STT(impacc[:, 0:63], oc3[:, r_, 65:128], nst[:, r_:r_ + 1], impacc[:, 0:63], ALU.mult, ALU.add, [pOCt, "nstA", "impacc"], ["impacc"])
            CP("dve", score[:, 63:64], bsel[:, ti, 63:64], ["bsel", "score"], ["score"])
            TT("dve", score[:, 0:63], bsel[:, ti, 0:63], impacc[:, 0:63], ALU.add, ["bsel", "score", "impacc"], ["score"])
            S.add("dve", lambda e: e.max(out=mx8[:, 0:8], in_=score[:]), ["score", "mx8"], ["mx8"], cost=0.15)
            S.add("dve", lambda e: e.match_replace(out=score2[:], in_to_replace=mx8[:, 0:8], in_values=score[:], imm_value=-3.0e38), ["score", "mx8"], ["score2"], cost=0.25)
            S.add("dve", lambda e: e.max(out=mx8[:, 8:16], in_=score2[:]), ["score2", "mx8"], ["mx8"], cost=0.15)
            TS("dve", negm[:], score[:], mx8[:, 15:16], ALU.is_ge, ["score", "mx8"], ["negm"], s2=1.0, op1=ALU.subtract)
        jobs[-1]["post"] = topk_dve

        def mask_to_NA():
            pt, ptt = nextT()
            TR(pt[0:64, 0:128], negm[:], identb[:], ["negm", "identb"], [ptt])
            ACT(NA[0:64, t, g, :].rearrange("p (r q) -> p r q", r=4), pt[0:64, 0:128].unsqueeze(1).to_broadcast([64, 4, 128]), AF.Copy, [ptt], [natok], scale=-NEGM)

        for kt in range(gt - 4, gt + 1):
            slot = kt % 8
            mms = [(kwT[:, slot * 128:(slot + 1) * 128], q4, [("kwT", slot // NT), "nqT"]),
                   (EA[64:68, kt, :], na_aug, ["EA", "NAaug"])]
            if kt == gt:
                mms.append((identb[:], caus, ["identb", "caus"]))
            if kt == gt - 4:
                mms.append((identb[:], winm, ["identb", "winm"]))

            def pv_w(P, Pt, kt=kt, slot=slot):
                for r_ in range(4):
                    MM(pOW[:, r_ * 65:(r_ + 1) * 65], P[:, r_ * 128:(r_ + 1) * 128], vw_aug[:, slot, g, :], kt == gt - 4 and r_ == 0, kt == gt, [Pt, ("vw_aug", slot // NT)], [pOWt])
            jobs.append(dict(mms=mms, pv=pv_w))
        kt_lo = max(0, gt - 16) if g == 0 else 0
        for kt in range(kt_lo, gt + 1):
            mms = [(ksT[:, kt * 128:(kt + 1) * 128], q4, [("ksT", kt // NT), "nqT"]),
                   (EA[0:68, kt, :], na_full, ["EA", natok, "NAaug"])]
            if kt == gt:
                mms.append((identb[:], caus, ["identb", "caus"]))

            def pv_s(P, Pt, kt=kt):
                for r_ in range(4):
                    MM(pOS[:, r_ * 65:(r_ + 1) * 65], P[:, r_ * 128:(r_ + 1) * 128], vs_aug[:, kt, g, :], kt == kt_lo and r_ == 0, kt == gt, [Pt, ("vs_aug", kt // NT)], [pOSt])
            jobs.append(dict(mms=mms, pv=pv_s, pre=(mask_to_NA if kt == kt_lo else None)))
        run_jobs(jobs)
        TS("dve", nst[:, 4:8], os3[:, :, 64], 1e-30, ALU.max, [pOSt, "nstB"], ["nstB"])
        TS("dve", nst[:, 8:12], ow3[:, :, 64], 1e-30, ALU.max, [pOWt, "nstB"], ["nstB"])
        RCP(nst[:, 4:12], nst[:, 4:12], ["nstB"], ["nstB"])
        gv = gsig[:, t, g * 12:(g + 1) * 12].rearrange("p (r b) -> p r b", b=3)
        for bi in range(3):
            TT("dve", nst[:, 12 + bi * 4:16 + bi * 4], nst[:, bi * 4:bi * 4 + 4], gv[:, :, bi], ALU.mult, ["nstA", "nstB", ("gsig", t), "nstK"], ["nstK"])
        for r_ in range(4):
            hcol = (4 * g + r_) * 64
            TS("dve", ytmp[:], oc3[:, r_, 0:64], nst[:, 12 + r_:13 + r_], ALU.mult, [pOCt, "nstK"], ["ytmp"])
            STT(ytmp[:], os3[:, r_, 0:64], nst[:, 16 + r_:17 + r_], ytmp[:], ALU.mult, ALU.add, [pOSt, "ytmp", "nstK"], ["ytmp"])
            STT(ynsa[:, hcol:hcol + 64], ow3[:, r_, 0:64], nst[:, 20 + r_:21 + r_], ytmp[:], ALU.mult, ALU.add, [pOWt, "ytmp", "nstK"], ["ynsa"])

    def nsa_tile(st, t):
        cols = slice(t * 128, (t + 1) * 128)
        for g in range(2):
            nsa_group(st, t, g)
        pt, ptt = nextT()
        for h in range(4):
            TR(pt[:, h * 128:(h + 1) * 128], ynsa[:, h * 128:(h + 1) * 128], identb[:], ["ynsa", "identb"], [ptt])
        CP("act", yT[:, 4:8, cols], pt.rearrange("p (k n) -> p k n", k=4), [ptt], [("yT", 1, t)])

    def xa_tile(st, t):
        cols = slice(t * 128, (t + 1) * 128)
        pO = [nextC(), nextC()]
        jobs = []
        for mt in range(2):
            def scores_x(mt=mt):
                pass
            mms = None
            jobs.append(mt)
        held = []
        for mt in range(2):
            pS, pSt = nextG()
            for h in range(4):
                MM(pS[:, h * 128:(h + 1) * 128], mkT[:, h, mt * 128:(mt + 1) * 128], xqT[:, h, cols], True, True, ["mkT", "xqT"], [pSt])
            held.append((pS, pSt))
        for mt in range(2):
            pS, pSt = held[mt]
            P, Pt = nextP()
            ACT(P[:], pS[:, 0:512], AF.Exp, [pSt], [Pt], scale=float(128.0 ** -0.5))
            for h in range(4):
                po, pot = pO[h // 2]
                hh = h % 2
                MM(po[:, hh * 129:(hh + 1) * 129], P[:, h * 128:(h + 1) * 128], mv_aug[:, mt, h, :], mt == 0 and hh == 0, mt == 1, [Pt, "mv_aug"], [pot])
        for hp in range(2):
            po, pot = pO[hp]
            pv = po[:, 0:258].rearrange("p (h c) -> p h c", h=2)
            TS("dve", nst[:, 24 + hp * 2:26 + hp * 2], pv[:, :, 128], 1e-30, ALU.max, [pot, "nstX"], ["nstX"])
        RCP(nst[:, 24:28], nst[:, 24:28], ["nstX"], ["nstX"])
        for h in range(4):
            po, pot = pO[h // 2]
            hh = h % 2
            TS("dve", yxa[:, h * 128:(h + 1) * 128], po[:, hh * 129:hh * 129 + 128], nst[:, 24 + h:25 + h], ALU.mult, [pot, "nstX"], ["yxa"])
        pt, ptt = nextT()
        for h in range(4):
            TR(pt[:, h * 128:(h + 1) * 128], yxa[:, h * 128:(h + 1) * 128], identb[:], ["yxa", "identb"], [ptt])
        CP("act", yT[:, 8:12, cols], pt.rearrange("p (k n) -> p k n", k=4), [ptt], [("yT", 2, t)])

    gT = sb("gT", [128, 2, 2, 2, 32], BF16)
    gtmp = sb("gtmp", [128, 128])
    gtmp2 = sb("gtmp2", [128, 128])
    NB = L // 16

    def compress_st(st):
        T0 = st * L
        n0 = T0 // 16 - 1
        W1 = [wfetch(("w1", kv), donetok=(("w1done", st) if kv == 1 else None)) for kv in range(2)]
        if os.environ.get("K_CB", "1") == "1":
            for kv in range(2):
                w1v, w1t = W1[kv]
                pgx, pgxt = nextG()
                for g in range(2):
                    for half in range(2):
                        c0 = (g * 2 + half) * NB
                        for lp in range(16):
                            MM(pgx[:, c0:c0 + NB], w1v[:, lp, half * 128:(half + 1) * 128], kcin2[:, kv * 2 + g, 2 * lp + 1:2 * lp + 1 + 16 * (NB - 1) + 1:16],
                               lp == 0, lp == 15, [w1t, "kcin2"], [pgxt])
                NC4 = 4 * NB
                gx = gtmp[:, 0:NC4].rearrange("p (g h n) -> p g h n", g=2, h=2)
                px = pgx[:, 0:NC4].rearrange("p (g h n) -> p g h n", g=2, h=2)
                for half in range(2):
                    ACT(gx[:, :, half, :], px[:, :, half, :], AF.Identity, [pgxt, "hb", "gtmp"], ["gtmp"], bias=hb[:, kv, half:half + 1])
                TT("dve", gtmp2[:, 0:NC4], gtmp[:, 0:NC4], gtmp[:, 0:NC4], ALU.mult, ["gtmp"], ["gtmp2"])
                TS("dve", gtmp2[:, 0:NC4], gtmp2[:, 0:NC4], 0.044715, ALU.mult, ["gtmp2"], ["gtmp2"], s2=1.0, op1=ALU.add)
                TT("dve", gtmp2[:, 0:NC4], gtmp2[:, 0:NC4], gtmp[:, 0:NC4], ALU.mult, ["gtmp", "gtmp2"], ["gtmp2"])
                SIGT(gtmp2[:, 0:NC4], gtmp2[:, 0:NC4], ["gtmp2"], ["gtmp2"], scale=1.5957691216057308)
                TT("dve", gT[:, kv, :, :, 0:NB], gtmp2[:, 0:NC4].rearrange("p (g h n) -> p g h n", g=2, h=2), gx, ALU.mult, ["gtmp", "gtmp2"], ["gT"])
                pgy, pgyt = nextG()
                i = 0
                for g in range(2):
                    for half in range(2):
                        MM(pgy[:, 0:NB], W2p[:, kv, half, g, :], gT[:, kv, g, half, 0:NB], i == 0, i == 3, W2T + ["gT"], [pgyt])
                        i += 1
                dstc = kcT if kv == 0 else vcT
                dtok = "kcT" if kv == 0 else "vcT"
                lo = 1 if st == 0 else 0
                CP("act", dstc[:, n0 + lo:n0 + NB], pgy[:, lo:NB], [pgyt], [dtok])
        else:
            for kv in range(2):
                w1v, w1t = W1[kv]
                for g in range(2):
                    for half in range(2):
                        pgx, pgxt = nextG()
                        for lp in range(16):
                            MM(pgx[:, 0:NB], w1v[:, lp, half * 128:(half + 1) * 128], kcin2[:, kv * 2 + g, 2 * lp + 1:2 * lp + 1 + 16 * (NB - 1) + 1:16],
                               lp == 0, lp == 15, [w1t, "kcin2"], [pgxt])
                        ACT(gtmp[:, 0:NB], pgx[:, 0:NB], AF.Identity, [pgxt, "hb"], ["gtmp"], bias=hb[:, kv, half:half + 1])
                        TT("dve", gtmp2[:, 0:NB], gtmp[:, 0:NB], gtmp[:, 0:NB], ALU.mult, ["gtmp"], ["gtmp2"])
                        TS("dve", gtmp2[:, 0:NB], gtmp2[:, 0:NB], 0.044715, ALU.mult, ["gtmp2"], ["gtmp2"], s2=1.0, op1=ALU.add)
                        TT("dve", gtmp2[:, 0:NB], gtmp2[:, 0:NB], gtmp[:, 0:NB], ALU.mult, ["gtmp", "gtmp2"], ["gtmp2"])
                        SIGT(gtmp2[:, 0:NB], gtmp2[:, 0:NB], ["gtmp2"], ["gtmp2"], scale=1.5957691216057308)
                        TT("dve", gT[:, kv, g, half, 0:NB], gtmp2[:, 0:NB], gtmp[:, 0:NB], ALU.mult, ["gtmp", "gtmp2"], ["gT"])
                pgy, pgyt = nextG()
                i = 0
                for g in range(2):
                    for half in range(2):
                        MM(pgy[:, 0:NB], W2p[:, kv, half, g, :], gT[:, kv, g, half, 0:NB], i == 0, i == 3, W2T + ["gT"], [pgyt])
                        i += 1
                dstc = kcT if kv == 0 else vcT
                dtok = "kcT" if kv == 0 else "vcT"
                lo = 1 if st == 0 else 0
                CP("act", dstc[:, n0 + lo:n0 + NB], pgy[:, lo:NB], [pgyt], [dtok])
        nts = sorted(set([max(n0, 0) // 128, (n0 + NB - 1) // 128]))
        for nt in nts:
            pt, ptt = nextT()
            TR(pt[:, 0:128], vcT[:, nt * 128:(nt + 1) * 128], identb[:], ["vcT", "identb"], [ptt])
            CP("act", vcx[:, nt, :, 0:64], pt[:, 0:128].rearrange("p (g d) -> p g d", g=2), [ptt] + VCX, ["vcx"])
        CP("dve", kcin2[:, :, 0:17], kcin2[:, :, L:L + 17], ["kcin2"], ["kcin2"])

    mT = sb("mT", [128, 8, L], BF16)
    USE_HNT = os.environ.get("K_HNT", "0") == "1"
    hnT = sb("hnT", [128, 8, L], BF16) if USE_HNT else None
    gsb = [sb("gsb%d" % i, [128, L]) for i in range(2)]
    mtmp = sb("mtmp", [128, L])
    rtmp = [sb("rtmp%d" % i, [128, L], BF16) for i in range(2)]
    cnt2 = {"g": 0, "r": 0}

    def own_tail(st):
        S.tag = "H_merge"
        YT = lambda j: [("yT", j, t) for t in range(NT)]
        for j in range(3):
            wbm = [wfetch(("mg", j, hf)) for hf in range(2)]
            wbr, wbrt = wfetch(("br", j))
            for dc in range(8):
                wg, wgt = wbm[dc // 4]
                co = (dc % 4) * 128
                pg_, pgt_ = nextG()
                for k in range(8):
                    MM(pg_[:, 0:L], wg[:, k, co:co + 128], XN[0][:, k, :], k == 0, k == 7, [wgt, XT[0]], [pgt_])
                gi = cnt2["g"] % 2
                cnt2["g"] += 1
                gb, gbt = gsb[gi], ("gsb", gi)
                bi = FMROW[("mg", j * 8 + dc)]
                SIGT(gb[:], pg_[:, 0:L], [pgt_, "hbiasFM"], [gbt], hbias=hbiasFM[:, bi:bi + 1])
                pu, put = nextG()
                for k in range(4):
                    MM(pu[:, 0:L], wbr[:, k, dc * 128:(dc + 1) * 128], yT[:, j * 4 + k, :], k == 0, k == 3, [wbrt] + YT(j), [put])
                if j == 0:
                    TT("dve", macc[:, dc, :], pu[:, 0:L], gb[:], ALU.mult, [put, gbt], [("macc", dc), ("aT", 16 + 2 * dc), ("aT", 17 + 2 * dc)])
                else:
                    TT("dve", mtmp[:], pu[:, 0:L], gb[:], ALU.mult, [put, gbt], ["mtmp"])
                    if j == 1:
                        TT("pool", macc[:, dc, :], macc[:, dc, :], mtmp[:], ALU.add, ["mtmp", ("macc", dc)], [("macc", dc), ("aT", 16 + 2 * dc), ("aT", 17 + 2 * dc)])
                    else:
                        TT("pool", mT[:, dc, :], macc[:, dc, :], mtmp[:], ALU.add, ["mtmp", ("macc", dc)], [("mT", dc)])
        MT = [("mT", dc) for dc in range(8)]
        S.tag = "I_outproj"
        for t in range(NT):
            gt = st * NT + t
            DMA("sp", hres[:, t, :], xc[gt * 128:(gt + 1) * 128, :], [], [("hres", t, 0), ("hres", t, 1)], ("hresx", t))
        for dh in range(2):
            wo, wt_ = wfetch(("out", dh))
            for t in range(NT):
                ph, pht = nextG()
                for k in range(8):
                    MM(ph[:, 0:512], mT[:, k, t * 128:(t + 1) * 128], wo[:, k, :], k == 0, k == 7, [wt_] + MT, [pht])
                TT("dve", hres[:, t, dh * 512:(dh + 1) * 512], ph[:, 0:512], hres[:, t, dh * 512:(dh + 1) * 512], ALU.add, [pht, ("hres", t, dh)], [("hres", t, dh)])
        S.tag = "J_ffn"
        for t in range(NT):
            rmsnorm_to_T(hres[:, t, :], [("hres", t, 0), ("hres", t, 1)], gffn_r, "gffn_r", (hnT if USE_HNT else XN[0]), slice(t * 128, (t + 1) * 128), ("hnT" if USE_HNT else XT[0]))
        for fb in range(8):
            w1, wt_ = wfetch(("ff1", fb))
            for fcl in range(4):
                fc = fb * 4 + fcl
                pa_, pat_ = nextG()
                for k in range(8):
                    MM(pa_[:, 0:L], w1[:, k, fcl * 128:(fcl + 1) * 128], (hnT if USE_HNT else XN[0])[:, k, :], k == 0, k == 7, [wt_, ("hnT" if USE_HNT else XT[0])], [pat_])
                ri = cnt2["r"] % 2
                cnt2["r"] += 1
                rb, rbt = rtmp[ri], ("rtmp", ri)
                if os.environ.get("K_RELU", "1") == "1":
                    TS("dve", rb[:], pa_[:, 0:L], 0.0, ALU.max, [pat_], [rbt])
                else:
                    ACT(rb[:], pa_[:, 0:L], AF.Relu, [pat_], [rbt])
                TT("pool", aT[:, fc, :], rb[:], rb[:], ALU.mult, [rbt], [("aT", fc)])
        for dh in range(2):
            pf = [nextC() for t in range(NT)]
            for fg in range(4):
                w2, wt_ = wfetch(("ff2", dh, fg))
                for t in range(NT):
                    po, pot = pf[t]
                    for k in range(8):
                        fc = fg * 8 + k
                        MM(po[:, 0:512], aT[:, fc, t * 128:(t + 1) * 128], w2[:, k, :], fc == 0, fc == 31, [wt_, ("aT", fc)], [pot])
            for t in range(NT):
                po, pot = pf[t]
                TT("dve", hres[:, t, dh * 512:(dh + 1) * 512], po[:, 0:512], hres[:, t, dh * 512:(dh + 1) * 512], ALU.add, [pot, ("hres", t, dh)], [("hres", t, dh)])
        S.tag = "K_final"
        for t in range(NT):
            gt = st * NT + t
            ht = [("hres", t, 0), ("hres", t, 1)]
            col, tok = rms_scale(hres[:, t, :], ht)
            STT(hres[:, t, :], hres[:, t, :], col, gfin_r[:], ALU.mult, ALU.mult, ht + [tok, "gfin_r"], ht)
            orow = (gt - NTILE // 2) * 128
            DMA("sp", y_out[orow:orow + 128, :], hres[:, t, :], ht, [], ("yout", t))

    for st in range(NST if os.environ.get('K_NOLOOP', '0') == '0' else 0):
        XN[0] = xnTs[st % NXN]
        XT[0] = ("xnT", st % NXN)
        own = st >= NSTP
        do_q = own or st == NSTP - 1
        pfx = not own
        fcol = flag2[:, 0:1] if pfx else ones2[:, 0:1]
        fsrc2 = flag2 if pfx else ones2
        ftok = "flag2" if pfx else "ones2"
        btile = biasTMf if pfx else biasTM
        btok = "biasTMf" if pfx else "biasTM"
        S.tag = "A_norm"
        for t in range(NT):
            gt = st * NT + t
            xi = nextX()
            DMA("sp", xbuf[xi][:], xc[gt * 128:(gt + 1) * 128, :], [], [("xbuf", xi)], ("xbuf", xi))
            rmsnorm_to_T(xbuf[xi][:], [("xbuf", xi)], gmix_r, "gmix_r", XN[0], slice(t * 128, (t + 1) * 128), XT[0])
        if st == 1:
            late_tables()
        if st == 2:
            S.tag = "setup"
            mem_setup()
            dbg("mkT", mkT[:], ["mkT"], [128, 4, 256])
            dbg("mv_aug", mv_aug[:], ["mv_aug"], [128, 2, 4, 129])
            S.tag = "A_norm"
        if st == NSTP:
            dbg(XT[0], XN[0][:], [XT[0]], [128, 8, L])
        S.tag = "B_ctxproj"
        wk_, wt_ = wfetch("mlk")
        for c in range(4):
            pgx, pgxt = fm_group(wk_, wt_, c * 128)
            conv_silu(prek, "prek", pgx, pgxt, c, FMROW[("mlk", c)], 4 + c, kT, "kT")
        for t in range(NT):
            pt, ptt = nextT()
            for c in range(4):
                TR(pt[:, c * 128:(c + 1) * 128], kT[:, c, t * 128:(t + 1) * 128], identb[:], ["kT", "identb"], [ptt])
            CP("dve" if os.environ.get("K_KTM", "1") == "1" else "act", ktm[:, t, :], pt, [ptt], [("ktm", t)])
        wv_, wt_ = wfetch("mlv")
        for t in range(NT):
            pgx, pgxt = nextG()
            for k in range(8):
                MM(pgx[:, 0:512], XN[0][:, k, t * 128:(t + 1) * 128], wv_[:, k, :], k == 0, k == 7, [wt_, XT[0]], [pgxt])
            TT("dve", v_aug[:, t, :, 0:128], pgx[:, 0:512].rearrange("p (h d) -> p h d", h=4), biasTM[:, 0:512].rearrange("p (h d) -> p h d", h=4), ALU.add,
               [pgxt, "biasTM", ("v_aug", t)], [("v_aug", t)])
        wc_, wt_ = wfetch("cbuf")
        pgx, pgxt = fm_group(wc_, wt_, 0)
        bi = FMROW["ks"]
        EVB(ksT[:, st * L:(st + 1) * L], pgx[:, 0:L], biasFM[:, bi:bi + 1], [pgxt, "biasFM"], [("ksT", st)])
        need_win = st >= NSTP - (512 // L)
        if need_win:
            pgx, pgxt = fm_group(wc_, wt_, 128)
            bi = FMROW["kw"]
            ro = (st * L) % 1024
            EVB(kwT[:, ro:ro + L], pgx[:, 0:L], biasFM[:, bi:bi + 1], [pgxt, "biasFM"], [("kwT", (ro // 128) // NT)])

        def kc_evac(pgx, pgxt, kv, g):
            bi3 = FMROW[("kc", kv, g)]
            idx = kv * 2 + g
            ACT(kcin2[0:64, idx, 16:16 + L], pgx[0:64, 0:L], AF.Identity, [pgxt, "biasFM", "kcin2"], ["kcin2"], bias=biasFM[0:64, bi3:bi3 + 1])
            ACT(kcin2[64:128, idx, 17:17 + L], pgx[64:128, 0:L], AF.Identity, [pgxt, "biasFM", "kcin2"], ["kcin2"], bias=biasFM[64:128, bi3:bi3 + 1])
        for g in range(2):
            pgx, pgxt = fm_group(wc_, wt_, 256 + g * 128)
            kc_evac(pgx, pgxt, 0, g)
        wd_, wt_ = wfetch("dbuf")
        for g in range(2):
            pgx, pgxt = fm_group(wd_, wt_, g * 128)
            kc_evac(pgx, pgxt, 1, g)
        for t in range(NT):
            gt = st * NT + t
            slot = gt % 8
            vst = ("vs_aug", gt // NT)
            vwt = ("vw_aug", slot // NT)
            pgx, pgxt = nextG()
            for k in range(8):
                MM(pgx[:, 0:288], XN[0][:, k, t * 128:(t + 1) * 128], wd_[:, k, 256:544], k == 0, k == 7, [wt_, XT[0]], [pgxt])
            STT(vs_aug[:, gt, :, 0:64], pgx[:, 0:128].rearrange("p (g d) -> p g d", g=2), fcol, (btile[:, 0:128] if pfx else btile[:, 512:640]).rearrange("p (g d) -> p g d", g=2), ALU.mult, ALU.add,
                [pgxt, ftok, btok, vst], [vst])
            CP("dve", vs_aug[:, gt, :, 64], fsrc2[:, 0:2], [ftok, vst], [vst])
            if need_win:
                STT(vw_aug[:, slot, :, 0:64], pgx[:, 128:256].rearrange("p (g d) -> p g d", g=2), fcol, (btile[:, 128:256] if pfx else btile[:, 640:768]).rearrange("p (g d) -> p g d", g=2), ALU.mult, ALU.add,
                    [pgxt, ftok, btok, vwt], [vwt])
                CP("dve", vw_aug[:, slot, :, 64], fsrc2[:, 0:2], [ftok, vwt], [vwt])
            TT("dve", gate[:, t, :], pgx[:, 256:264], biasTM[:, 768:776], ALU.add, [pgxt, "biasTM"], [("gate", t)])
            if own:
                TT("dve", gsig[:, t, :], pgx[:, 264:288], biasTM[:, 776:800], ALU.add, [pgxt, "biasTM"], [("gsig", t)])
                SIGT(gsig[:, t, :], gsig[:, t, :], [("gsig", t)], [("gsig", t)])
        S.tag = "C_compress"
        compress_st(st)
        if st < NSTP - 1:
            per_st = (len(conv_rest) + NSTP - 2) // (NSTP - 1)
            for idx in conv_rest[st * per_st:(st + 1) * per_st]:
                emit_conv(idx, [("w1done", st)])
        if st == NSTP - 1:
            for idx in conv_ffn[0:8]:
                emit_conv(idx, [("w1done", st)])
        if st == (NSTP if os.environ.get("K_CS", "1") == "1" else NSTP - 1):
            for idx in conv_ffn[8:16]:
                emit_conv(idx, [("w1done", st)])
        S.tag = "D_qproj"
        if do_q:
            wq_, wt_ = wfetch("mlq")
            for c in range(4):
                pgx, pgxt = fm_group(wq_, wt_, c * 128)
                conv_silu(preq, "preq", pgx, pgxt, c, FMROW[("mlq", c)], c, qT, "qT")
        if st == NSTP - 1:
            for c in range(4):
                TS("dve", prek[:, c, :], prek[:, c, :], flag2[:, 0:1], ALU.mult, [("prek", c), "flag2"], [("prek", c)])
                TS("dve", preq[:, c, :], preq[:, c, :], flag2[:, 0:1], ALU.mult, [("preq", c), "flag2"], [("preq", c)])
        if own:
            wo_, wt_ = wfetch("mlo")
            for t in range(NT):
                pgx, pgxt = nextG()
                for k in range(8):
                    MM(pgx[:, 0:512], XN[0][:, k, t * 128:(t + 1) * 128], wo_[:, k, :], k == 0, k == 7, [wt_, XT[0]], [pgxt])
                TT("dve", otmp, pgx[:, 0:512], biasTM[:, 800:1312], ALU.add, [pgxt, "biasTM"], [("Rm", 0)])
                SIGT(otmp, otmp, [("Rm", 0)], [("Rm", 0)])
                TT("dve", og[:, t, :], otmp, gml_r[:], ALU.mult, [("Rm", 0), "gml_r"], [("og", t)])
            v4, wt_ = wfetch("nsq")
            for p_ in range(4):
                pgx, pgxt = nextG()
                for k in range(8):
                    MM(pgx[:, 0:L], v4[:, k, p_ * 128:(p_ + 1) * 128], XN[0][:, k, :], k == 0, k == 7, [wt_, XT[0]], [pgxt])
                bq = FMROW[("nsq", p_)]
                ACT(nqT[0:64, :, p_, :], pgx[0:64, 0:L].rearrange("p (t q) -> p t q", t=NT), AF.Identity, [pgxt, "biasFM", "nqT"], ["nqT"],
                    bias=biasFM[0:64, bq:bq + 1], scale=0.125)
                ACT(nqT[64:128, :, p_ + 4, :], pgx[64:128, 0:L].rearrange("p (t q) -> p t q", t=NT), AF.Identity, [pgxt, "biasFM", "nqT"], ["nqT"],
                    bias=biasFM[64:128, bq:bq + 1], scale=0.125)
            wx_, wt_ = wfetch("xaq")
            for c in range(4):
                pgx, pgxt = fm_group(wx_, wt_, c * 128)
                bx = FMROW[("xaq", c)]
                EVB(xqT[:, c, :], pgx[:, 0:L], biasFM[:, bx:bx + 1], [pgxt, "biasFM"], ["xqT"])
            oi = st - NSTP
            for t in range(NT):
                ti = oi * NT + t
                DMA("sp", NA[64:68, t, :, :], t_qaug[:, ti, :, :], [], ["NAaug"], "NAaug")
        S.tag = "E_mlstm"
        mlstm_gates(st)
        for t in range(NT):
            S.tag = "E_mlstm"
            mlstm_tile(st, t, own)
            if own:
                S.tag = "F_nsa"
                nsa_tile(st, t)
                S.tag = "G_xa"
                xa_tile(st, t)
        if own:
            if st == NSTP:
                dbg("yT", yT[:], [("yT", j, t) for j in range(3) for t in range(NT)], [128, 12, L])
                dbg("Cst", Cst[:], ["Cst"], [128, 4, 129])
                dbg("kcT", kcT[:], ["kcT"], [128, 256])
                dbg("vcT", vcT[:], ["vcT"], [128, 256])
                dbg("kT", kT[:], ["kT"], [128, 4, L])
                dbg("qT", qT[:], ["qT"], [128, 4, L])
                dbg("nqT", nqT[:], ["nqT"], [128, NT, 8, 128])
                dbg("gsig", gsig[:], [("gsig", t) for t in range(NT)], [128, NT, 24])
                dbg("vs_aug", vs_aug[:], [("vs_aug", i) for i in range(NST)], [128, NTILE, 2, 65])
                dbg("ksT", ksT[:], [("ksT", i) for i in range(NST)], [128, SEQ])
                dbg("score", score[:], ["score"], [128, 64])
                dbg("negm", negm[:], ["negm"], [128, 64])
            own_tail(st)
            if st == NSTP:
                dbg("mT", mT[:], [("mT", dc) for dc in range(8)], [128, 8, L])
                dbg("hres", hres[:], [("hres", t, dh) for t in range(NT) for dh in range(2)], [128, NT, D])
        if stop_after is not None and st == stop_after:
            break

    S.finalize(es)
    build.stats = dict(ops=len(S.ops), waits=S.nwaits, sems=S.nsems, est_us=getattr(S, "est_total", None))
    build.sched = S
    return nc, es


def _bf(a):
    return np.asarray(a, dtype=np.float32).astype(ml_dtypes.bfloat16)


def make_tables(s):
    T = {}
    T["t_flag"] = np.zeros((128, 2), np.float32) + (1.0 if s == 1 else 0.0)
    T["t_identf"] = np.eye(128, dtype=np.float32)
    ii = np.arange(128)
    T["t_U"] = (ii[:, None] <= ii[None, :]).astype(np.float32)
    first_valid_tok = 0 if s == 1 else HALF
    n = np.arange(256)
    q = HALF + np.arange(HALF)
    vis = (16 * n[:, None] + 31 <= q[None, :]) & (16 * n[:, None] >= first_valid_tok) & (n[:, None] < 255)
    okc = np.where(vis, 0.0, NEGM).astype(np.float32)
    T["t_okc"] = _bf(okc.reshape(2, 128, HALF).transpose(1, 0, 2))
    j = np.arange(64)
    cur = q // 64
    fb = first_valid_tok // 64
    ok = (j[None, :] <= cur[:, None]) & (j[None, :] >= fb)
    forced = (j[None, :] == fb) | (j[None, :] == cur[:, None]) | ((j[None, :] == cur[:, None] - 1) & (j[None, :] >= fb))
    bs = np.where(ok, np.where(forced, 1000.0, 0.0), -1e30).astype(np.float32)
    T["t_bsel"] = _bf(np.ascontiguousarray(bs.reshape(16, 128, 64).transpose(1, 0, 2)))
    EA = np.zeros((68, NTILE, 128), np.float32)
    for kt in range(NTILE):
        EA[2 * kt, kt, 0:64] = 1.0
        EA[2 * kt + 1, kt, 64:128] = 1.0
        EA[64, kt, :] = kt
        EA[65, kt, :] = ii
        EA[66, kt, :] = 1.0
        EA[67, kt, :] = 1.0
    T["t_EA"] = _bf(EA)
    CA = np.zeros((68, 256), np.float32)
    pos = 16 * n + 31
    CA[64] = pos // 128
    CA[65] = pos % 128
    CA[66] = 1.0
    CA[67] = 1.0
    T["t_CA"] = _bf(CA)
    slopes = 2.0 ** (-8.0 * np.arange(1, 9) / 8)
    qa = np.zeros((4, 16, 2, 4, 128), np.float32)
    for ti in range(16):
        tq0 = HALF + ti * 128
        for g in range(2):
            for r in range(4):
                sl = slopes[4 * g + r]
                qa[0, ti, g, r, :] = 128.0 * sl
                qa[1, ti, g, r, :] = sl
                qa[2, ti, g, r, :] = -sl * tq0
                qa[3, ti, g, r, :] = -sl * ii
    T["t_qaug"] = _bf(qa.reshape(4, 16, 2, 512))
    caus = np.where(ii[:, None] <= ii[None, :], 0.0, NEGM).astype(np.float32)
    T["t_caus"] = _bf(np.tile(caus, (1, 4)))
    win = np.where(ii[:, None] > ii[None, :], 0.0, NEGM).astype(np.float32)
    T["t_win"] = _bf(np.tile(win, (1, 4)))
    cs = n * 16
    js = np.arange(63) * 64
    ov = ((cs[:, None] < js[None, :] + 64) & (cs[:, None] + 32 > js[None, :]) & (n[:, None] < 255)).astype(np.float32)
    T["t_ovl"] = _bf(ov.reshape(2, 128, 63).transpose(1, 0, 2))
    return T


_CACHE = {}


def kernel(x, mem, g_mix, w_in, b_in, ml_conv, ml_norm_g, cmp_pe, cmp_w1, cmp_w2, g_mem, w_mem_kv,
           w_branch, w_out, g_ffn, w_ff1, w_ff2, g_final, _debug=False, _stop_after=None, _cores=8):
    f = lambda a: np.ascontiguousarray(np.asarray(a, dtype=np.float32))
    x = f(x)
    mem = f(mem)
    shared = {
        "w_in": f(w_in)[0], "b_in": f(b_in).reshape(1, D_IN), "ml_conv": f(ml_conv)[0], "ml_norm_g": f(ml_norm_g).reshape(1, 512),
        "cmp_pe": f(cmp_pe)[0], "cmp_w1": f(cmp_w1)[0], "cmp_w2": f(cmp_w2)[0],
        "g_mix": f(g_mix).reshape(1, D), "g_mem": f(g_mem).reshape(1, D), "g_ffn": f(g_ffn).reshape(1, D), "g_final": f(g_final).reshape(1, D),
        "w_mem_kv": f(w_mem_kv)[0], "w_branch": f(w_branch)[0], "w_out": f(w_out)[0], "w_ff1": f(w_ff1)[0], "w_ff2": f(w_ff2)[0],
    }
    key = (_debug, _stop_after)
    if key not in _CACHE:
        _CACHE[key] = build(debug=_debug, stop_after=_stop_after)
    nc, _es = _CACHE[key]
    tabs = [make_tables(0), make_tables(1)]
    in_maps = []
    for core in range(_cores):
        b, s = core // 2, core % 2
        if s == 1:
            xcore = x[b]
        else:
            xcore = np.concatenate([x[b, :HALF], x[b, :HALF]], axis=0)
        m = dict(shared)
        m["xc"] = np.ascontiguousarray(xcore)
        m["mem"] = mem[b]
        m.update(tabs[s])
        in_maps.append(m)
    res = run_bass_kernel_spmd(nc, in_maps, core_ids=list(range(_cores)))
    out = np.zeros((4, SEQ, D), np.float32)
    for core in range(_cores):
        b, s = core // 2, core % 2
        out[b, s * HALF:(s + 1) * HALF] = res.results[core]["y"]
    if _debug:
        kernel.last = res.results
    return out
```

```python
import os
import numpy as np
import ml_dtypes
from contextlib import ExitStack
import concourse.bass as bass
import concourse.mybir as mybir
from concourse.bass_utils import run_bass_kernel_spmd

F32 = mybir.dt.float32
BF16 = mybir.dt.bfloat16
AF = mybir.ActivationFunctionType
ALU = mybir.AluOpType

D = 1024
SEQ = 4096
HALF = 2048
NT = 2
L = 128 * NT
NST = SEQ // L
NSTP = NST // 2
NTILE = SEQ // 128
D_IN = 6944
EPS = 1e-6
NEGM = -30000.0
WB = 4352
NWBUF = int(os.environ.get("K_NW", "4"))

C_MLQ, C_MLK, C_MLV, C_MLO, C_MLI, C_MLF = 0, 512, 1024, 1536, 2048, 2052
C_NSQ, C_KC, C_VC, C_KS, C_VS, C_KW, C_VW, C_NSG, C_XAQ, C_MG = 2056, 2568, 2696, 2824, 2952, 3080, 3208, 3336, 3360, 3872

DEBUG = {}


class Sched:
    ENG = ["pe", "act", "dve", "pool", "sp"]

    def __init__(self, nc, same_engine_sync=True, reorder=True):
        self.nc = nc
        self.ops = []
        self.same = same_engine_sync
        self.reorder = reorder
        self.tag = "setup"

    def add(self, eng, fn, reads=(), writes=(), dma=None, cost=0.3, nbytes=0):
        self.ops.append(dict(eng=eng, fn=fn, reads=tuple(reads), writes=tuple(writes), dma=dma, tag=self.tag, cost=cost, nbytes=nbytes))

    def _schedule(self, ops):
        import heapq
        n = len(ops)
        succ = [[] for _ in range(n)]
        indeg = [0] * n
        for i, op in enumerate(ops):
            indeg[i] = len(op["alldeps"])
            for d in op["alldeps"]:
                succ[d].append(i)
        finish = [0.0] * n
        ready_t = [0.0] * n
        PRIO = os.environ.get("K_PRIO", "1") == "1"
        blevel = [0.0] * n
        if PRIO:
            for i in range(n - 1, -1, -1):
                c = ops[i]["cost"] if ops[i]["dma"] is None else (ops[i]["nbytes"] / 230e3 + 2.0)
                m = 0.0
                for s_ in succ[i]:
                    if blevel[s_] > m:
                        m = blevel[s_]
                blevel[i] = c + m
        eng_free = {e: 0.0 for e in self.ENG}
        future = {e: [] for e in self.ENG}
        avail = {e: [] for e in self.ENG}
        for i, op in enumerate(ops):
            if indeg[i] == 0:
                heapq.heappush(future[op["eng"]], (0.0, i))
        dma_free = 0.0
        order = []
        BW = float(os.environ.get("K_BW", "230")) * 1e3
        LAT = float(os.environ.get("K_LAT", "0.3"))
        while len(order) < n:
            best = None
            for e in self.ENG:
                f, a = future[e], avail[e]
                while f and f[0][0] <= eng_free[e]:
                    t_, i_ = heapq.heappop(f)
                    heapq.heappush(a, (-blevel[i_], i_) if PRIO else i_)
                if a:
                    cand = (eng_free[e], (a[0][1] if PRIO else a[0]), e, True)
                elif f:
                    cand = (f[0][0], f[0][1], e, False)
                else:
                    continue
                if best is None or cand[:2] < best[:2]:
                    best = cand
            start, i, e, from_avail = best
            if from_avail:
                heapq.heappop(avail[e])
            else:
                heapq.heappop(future[e])
            op = ops[i]
            if op["dma"] is not None:
                eng_free[e] = start + 0.08
                t0 = max(start, dma_free)
                dma_free = t0 + op["nbytes"] / BW
                finish[i] = dma_free + float(os.environ.get('K_DLAT', '2.0'))
            else:
                finish[i] = start + op["cost"]
                eng_free[e] = finish[i]
            order.append(i)
            for s_ in succ[i]:
                if op["dma"] is not None and ops[s_]["dma"] == op["dma"]:
                    ready_t[s_] = max(ready_t[s_], start)
                else:
                    lat = 0.0 if (ops[s_]["eng"] == e and op["dma"] is None) else LAT
                    ready_t[s_] = max(ready_t[s_], finish[i] + lat)
                indeg[s_] -= 1
                if indeg[s_] == 0:
                    heapq.heappush(future[ops[s_]["eng"]], (ready_t[s_], s_))
        self.est_total = max(finish) if n else 0.0
        return order

    def finalize(self, es):
        nc, ops = self.nc, self.ops
        last_w, readers = {}, {}
        for i, op in enumerate(ops):
            deps = set()
            for b in op["reads"]:
                if b in last_w:
                    deps.add(last_w[b])
            for b in op["writes"]:
                if b in last_w:
                    deps.add(last_w[b])
                deps.update(readers.get(b, ()))
            deps.discard(i)
            nd = set()
            for d in deps:
                dop = ops[d]
                if dop["dma"] is not None and op["dma"] is not None and dop["dma"] == op["dma"]:
                    nd |= dop["alldeps"]
                    nd.add(d) if dop["eng"] == op["eng"] else None
                else:
                    nd.add(d)
            op["alldeps"] = nd
            for b in op["reads"]:
                readers.setdefault(b, []).append(i)
            for b in op["writes"]:
                last_w[b] = i
                readers[b] = []
        order = self._schedule(ops) if (self.reorder and os.environ.get("K_REORDER", "1") == "1") else list(range(len(ops)))
        pos = {i: p for p, i in enumerate(order)}
        for i, op in enumerate(ops):
            assert all(pos[d] < pos[i] for d in op["alldeps"])
        needed = set()
        for i, op in enumerate(ops):
            nd = set()
            for d in op["alldeps"]:
                dop = ops[d]
                if dop["dma"] is None and op["dma"] is None and dop["eng"] == op["eng"]:
                    if op["eng"] == "pe" or not self.same:
                        continue
                if dop["dma"] is not None and op["dma"] is not None and dop["dma"] == op["dma"]:
                    continue
                nd.add(d)
            op["deps"] = nd
            needed |= nd
        sems, cnt = {}, {}

        def getsem(key):
            if key not in sems:
                sems[key] = es.enter_context(nc.semaphore("s%d" % len(sems)))
                cnt[key] = 0
            return sems[key]

        for i in order:
            op = ops[i]
            if op["dma"] is not None:
                key = ("d", op["dma"])
                getsem(key)
                cnt[key] += 16
                op["sig"] = (key, cnt[key], 16)
            elif i in needed:
                key = ("e", op["eng"])
                getsem(key)
                cnt[key] += 1
                op["sig"] = (key, cnt[key], 1)
            else:
                op["sig"] = None
        per = {e: [] for e in self.ENG}
        for i in order:
            per[ops[i]["eng"]].append(i)
        self.order = order
        self.per = per
        self.nwaits = 0
        self.nsems = len(sems)

        def run(eng_name, e):
            w = {}
            for i in per[eng_name]:
                op = ops[i]
                need = {}
                for d in op["deps"]:
                    key, val, _ = ops[d]["sig"]
                    if need.get(key, 0) < val:
                        need[key] = val
                for key, val in need.items():
                    if w.get(key, 0) < val:
                        e.wait_ge(sems[key], val)
                        w[key] = val
                        self.nwaits += 1
                ins = op["fn"](e)
                if op["sig"] is not None:
                    key, val, inc = op["sig"]
                    ins.then_inc(sems[key], inc)
            if eng_name == "sp":
                for key in sems:
                    if key[0] == "d":
                        e.wait_ge(sems[key], cnt[key])

        block = es.enter_context(nc.Block())

        @block.sync
        def _(e):
            run("sp", e)

        @block.scalar
        def _(e):
            run("act", e)

        @block.vector
        def _(e):
            run("dve", e)

        @block.gpsimd
        def _(e):
            run("pool", e)

        @block.tensor
        def _(e):
            run("pe", e)


def build(debug=False, stop_after=None):
    nc = bass.Bass("TRN2", target_bir_lowering=False)
    es = ExitStack()
    S = Sched(nc)

    def fsz(ap):
        n = 1
        for d in ap.shape[1:]:
            n *= d
        return n

    def MM(out, lhsT, rhs, start, stop, reads, writes):
        n = max(fsz(rhs), 64)
        c = n / 2400.0 * (4.0 if rhs.dtype == F32 else 1.0) + 0.035
        S.add("pe", lambda e: e.matmul(out, lhsT=lhsT, rhs=rhs, start=start, stop=stop), reads, writes, cost=c)

    def TR(out, in_, ident, reads, writes):
        S.add("pe", lambda e: e.transpose(out=out, in_=in_, identity=ident), reads, writes, cost=0.09)

    def ACT(out, in_, func, reads, writes, bias=None, scale=1.0, accum_out=None):
        kw = {}
        if bias is not None:
            kw["bias"] = bias
        if accum_out is not None:
            kw["accum_out"] = accum_out
        S.add("act", lambda e: e.activation(out=out, in_=in_, func=func, scale=scale, **kw), reads, writes, cost=fsz(in_) / 1200.0 + 0.2)

    def dvecost(eng, ap):
        return fsz(ap) / (900.0 if eng == "dve" else 400.0) + (0.07 if eng == "dve" else 0.3)

    def TT(eng, out, in0, in1, op, reads, writes):
        S.add(eng, lambda e: e.tensor_tensor(out=out, in0=in0, in1=in1, op=op), reads, writes, cost=dvecost(eng, in0))

    def TS(eng, out, in0, s1, op0, reads, writes, s2=None, op1=None):
        if op1 is None:
            S.add(eng, lambda e: e.tensor_scalar(out=out, in0=in0, scalar1=s1, scalar2=None, op0=op0), reads, writes, cost=dvecost(eng, in0))
        else:
            S.add(eng, lambda e: e.tensor_scalar(out=out, in0=in0, scalar1=s1, scalar2=s2, op0=op0, op1=op1), reads, writes, cost=dvecost(eng, in0))

    def STT(out, in0, scalar, in1, op0, op1, reads, writes):
        S.add("dve", lambda e: e.scalar_tensor_tensor(out=out, in0=in0, scalar=scalar, in1=in1, op0=op0, op1=op1), reads, writes, cost=dvecost("dve", in0) * 1.3)

    def CP(eng, out, in_, reads, writes):
        if eng == "act":
            S.add("act", lambda e: e.copy(out=out, in_=in_), reads, writes, cost=fsz(in_) / 1200.0 + 0.2)
        else:
            S.add(eng, lambda e: e.tensor_copy(out=out, in_=in_), reads, writes, cost=dvecost(eng, in_))

    def EVB(out, in_, bias_col, reads, writes, scale=None):
        if os.environ.get("K_EVB", "0") == "0":
            ACT(out, in_, AF.Identity, reads, writes, bias=bias_col, scale=(1.0 if scale is None else scale))
            return
        if scale is None:
            S.add("dve", lambda e: e.tensor_scalar(out=out, in0=in_, scalar1=bias_col, scalar2=None, op0=ALU.add), reads, writes, cost=dvecost("dve", in_))
        else:
            S.add("dve", lambda e: e.tensor_scalar(out=out, in0=in_, scalar1=scale, scalar2=bias_col, op0=ALU.mult, op1=ALU.add), reads, writes, cost=dvecost("dve", in_))

    def SIGT(out, in_, reads, writes, hbias=None, scale=1.0):
        ACT(out, in_, AF.Tanh, reads, writes, bias=hbias, scale=0.5 * scale)
        TS("dve", out, out, 0.5, ALU.mult, writes, writes, s2=0.5, op1=ALU.add)

    def RCP(out, in_, reads, writes):
        S.add("dve", lambda e: e.reciprocal(out=out, in_=in_), reads, writes, cost=dvecost("dve", in_) + 0.1)

    def SQRT(out, in_, reads, writes):
        S.add("act", lambda e: e.sqrt(out=out, in_=in_), reads, writes, cost=fsz(in_) / 1200.0 + 0.2)

    def MSET(eng, ap, val, writes, reads=()):
        S.add(eng, lambda e: e.memset(ap, val), reads, writes, cost=dvecost(eng, ap))

    def DMA(eng, out, in_, reads, writes, key, slow=False, nbm=1):
        nb = out.shape[0] * fsz(out) * (4 if out.dtype == F32 else 2) * nbm
        if slow:
            S.add(eng, lambda e: e.dma_start(out=out, in_=in_, allow_slow_non_contiguous=True), reads, writes, dma=key, nbytes=nb)
        else:
            S.add(eng, lambda e: e.dma_start(out=out, in_=in_), reads, writes, dma=key, nbytes=nb)

    def din(name, shape, dt=F32):
        return nc.dram_tensor(name, list(shape), dt, kind="ExternalInput").ap()

    xc = din("xc", [SEQ, D])
    mem = din("mem", [256, D])
    w_in = din("w_in", [D, D_IN])
    b_in = din("b_in", [1, D_IN])
    ml_conv = din("ml_conv", [4, 1024])
    ml_norm_g = din("ml_norm_g", [1, 512])
    cmp_pe = din("cmp_pe", [2, 32, 64])
    cmp_w1 = din("cmp_w1", [2, 2048, 256])
    cmp_w2 = din("cmp_w2", [2, 256, 64])
    g_mix = din("g_mix", [1, D])
    g_mem = din("g_mem", [1, D])
    g_ffn = din("g_ffn", [1, D])
    g_final = din("g_final", [1, D])
    w_mem_kv = din("w_mem_kv", [D, D])
    w_branch = din("w_branch", [3, 512, D])
    w_out = din("w_out", [D, D])
    w_ff1 = din("w_ff1", [D, 4096])
    w_ff2 = din("w_ff2", [4096, D])
    t_flag = din("t_flag", [128, 2])
    t_identf = din("t_identf", [128, 128])
    t_U = din("t_U", [128, 128])
    t_okc = din("t_okc", [128, 2, HALF], BF16)
    t_bsel = din("t_bsel", [128, 16, 64], BF16)
    t_EA = din("t_EA", [68, NTILE, 128], BF16)
    t_CA = din("t_CA", [68, 256], BF16)
    t_qaug = din("t_qaug", [4, 16, 2, 512], BF16)
    t_caus = din("t_caus", [128, 512], BF16)
    t_win = din("t_win", [128, 512], BF16)
    t_ovl = din("t_ovl", [128, 2, 63], BF16)
    y_out = nc.dram_tensor("y", [HALF, D], F32, kind="ExternalOutput").ap()

    def sb(name, shape, dt=F32):
        return es.enter_context(nc.sbuf_tensor(name, list(shape), dt))

    def ps(name, shape, dt=F32):
        return es.enter_context(nc.psum_tensor(name, list(shape), dt))

    def dbg(name, tile_ap, reads, shape):
        if not debug:
            return
        o = nc.dram_tensor("dbg_" + name, list(shape), F32, kind="ExternalOutput").ap()
        DEBUG[name] = tuple(shape)
        DMA("sp" if tile_ap.dtype == F32 else "pool", o, tile_ap, reads, [], ("dbg", name))

    psT = ps("psT", [128, 1024], BF16)

    def nextT():
        i = rot["T"] % 2 if os.environ.get("K_T2", "0") == "1" else 0
        rot["T"] += 1
        return psT[:, i * 512:(i + 1) * 512], ("psT", i)
    NPSG = int(os.environ.get("K_PSG", "4"))
    NPSC = int(os.environ.get("K_PSC", "3"))
    psG = [ps("psG%d" % i, [128, 512]) for i in range(NPSG)]
    psC = [ps("psC%d" % i, [128, 512]) for i in range(NPSC)]
    rot = {"G": 0, "C": 0, "P": 0, "W": 0, "X": 0, "S": 0, "T": 0, "N": 0, "W2": 0, "V": 0}

    def nextG():
        i = rot["G"] % NPSG
        rot["G"] += 1
        return psG[i], ("psG", i)

    def nextC():
        i = rot["C"] % NPSC
        rot["C"] += 1
        return psC[i], ("psC", i)

    identf = sb("identf", [128, 128])
    identb = sb("identb", [128, 128], BF16)
    Uf = sb("Uf", [128, 128])
    onesf = sb("onesf", [128, 128])
    flag2 = sb("flag2", [128, 2])
    ones2 = sb("ones2", [128, 2])
    mhalf = sb("mhalf", [128, 4])
    okc = sb("okc", [128, 2, HALF], BF16)
    bsel = sb("bsel", [128, 16, 64], BF16)
    EA = sb("EA", [68, NTILE, 128], BF16)
    CA = sb("CA", [68, 256], BF16)
    caus1 = sb("caus", [128, 128], BF16)
    winm1 = sb("winm", [128, 128], BF16)
    gmix_r = sb("gmix_r", [128, D], BF16)
    gffn_r = sb("gffn_r", [128, D], BF16)
    gfin_r = sb("gfin_r", [128, D])
    hres = sb("hres", [128, NT, D])
    gmem_r = hres[:, 0, :]
    convrow = hres[0:4, 1, :]
    gml_r = sb("gml_r", [128, 512])
    biasTM = sb("biasTM", [128, 1312])
    biasTMf = sb("biasTMf", [128, 256])
    brow = sb("brow", [64, 128])
    biasFM = sb("biasFM", [128, 64])
    hbiasFM = sb("hbiasFM", [128, 64])
    convw = sb("convw", [128, 8, 4])

    def load(out_ap, in_ap, tok, eng="sp"):
        DMA(eng, out_ap, in_ap, [], [tok], tok)

    load(identf[:], t_identf, "identf")
    load(Uf[:], t_U, "Uf")
    load(flag2[:], t_flag, "flag2")
    caus = caus1[:].unsqueeze(1).to_broadcast([128, 4, 128])
    winm = winm1[:].unsqueeze(1).to_broadcast([128, 4, 128])
    load(gmix_r[:], g_mix.partition_broadcast(128).squeeze(1), "gmix_r", eng="pool")
    HR = [("hres", t, dh) for t in range(NT) for dh in range(2)]
    DMA("sp", convrow, ml_conv, [], HR, "convrow")

    def late_tables():
        load(okc[:], t_okc, "okc")
        load(bsel[:], t_bsel, "bsel")
        load(EA[:], t_EA, "EA")
        load(CA[:], t_CA, "CA")
        load(caus1[:], t_caus[:, 0:128], "caus")
        load(winm1[:], t_win[:, 0:128], "winm")
        load(gffn_r[:], g_ffn.partition_broadcast(128).squeeze(1), "gffn_r", eng="pool")
        load(gfin_r[:], g_final.partition_broadcast(128).squeeze(1), "gfin_r")
        load(gml_r[:], ml_norm_g.partition_broadcast(128).squeeze(1), "gml_r")
    bpb = b_in.partition_broadcast(128).squeeze(1)
    load(biasTM[:, 0:512], bpb[:, C_MLV:C_MLV + 512], "biasTM")
    load(biasTM[:, 512:640], bpb[:, C_VS:C_VS + 128], "biasTM")
    load(biasTM[:, 640:768], bpb[:, C_VW:C_VW + 128], "biasTM")
    load(biasTM[:, 768:776], bpb[:, C_MLI:C_MLI + 8], "biasTM")
    load(biasTM[:, 776:800], bpb[:, C_NSG:C_NSG + 24], "biasTM")
    load(biasTM[:, 800:1312], bpb[:, C_MLO:C_MLO + 512], "biasTM")
    MSET("dve", onesf[:], 1.0, ["onesf"])
    MSET("dve", ones2[:], 1.0, ["ones2"])
    MSET("dve", mhalf[:], -0.5, ["mhalf"])
    CP("dve", identb[:], identf[:], ["identf"], ["identb"])
    TS("dve", biasTMf[:], biasTM[:, 512:768], flag2[:, 0:1], ALU.mult, ["biasTM", "flag2"], ["biasTMf"])
    MSET("dve", brow[32:40, :], 0.0, ["brow0"])
    FMROW = {}
    rows_free = [i for i in range(64) if not (32 <= i < 40)]

    def newrow():
        return rows_free.pop(0)

    def brow_load(r, c0, n, dst0=0, full=True):
        DMA("sp", brow[r:r + 1, dst0:dst0 + n], b_in[0:1, c0:c0 + n], ([] if full else ["brow0"]), ["brow"], "brow")

    for c in range(4):
        r = newrow(); FMROW[("mlk", c)] = r; brow_load(r, C_MLK + c * 128, 128)
    for c in range(4):
        r = newrow(); FMROW[("mlq", c)] = r; brow_load(r, C_MLQ + c * 128, 128)
    r = newrow(); FMROW["ks"] = r; brow_load(r, C_KS, 128)
    r = newrow(); FMROW["kw"] = r; brow_load(r, C_KW, 128)
    for kv, cbase in ((0, C_KC), (1, C_VC)):
        for g in range(2):
            r = newrow(); FMROW[("kc", kv, g)] = r
            brow_load(r, cbase + g * 64, 64, 0)
            brow_load(r, cbase + g * 64, 64, 64)
    for p_ in range(4):
        r = 32 + p_
        FMROW[("nsq", p_)] = r
        brow_load(r, C_NSQ + p_ * 64, 64, 0, full=False)
        brow_load(r, C_NSQ + (p_ + 4) * 64, 64, 64, full=False)
    for c in range(4):
        r = newrow(); FMROW[("xaq", c)] = r; brow_load(r, C_XAQ + c * 128, 128)
    for c in range(24):
        r = newrow(); FMROW[("mg", c)] = r; brow_load(r, C_MG + c * 128, 128)
    r = 0
    assert r <= 64
    pg, pgt = nextG()
    TR(pg[:, 0:64], brow[:, :], identf[0:64, 0:64], ["brow0", "brow", "identf"], [pgt])
    CP("dve", biasFM[:], pg[:, 0:64], [pgt], ["biasFM"])
    nq0 = FMROW[("nsq", 0)]
    TS("dve", biasFM[:, nq0:nq0 + 8], biasFM[:, nq0:nq0 + 8], 0.125, ALU.mult, ["biasFM"], ["biasFM"])
    TS("dve", hbiasFM[:], biasFM[:], 0.5, ALU.mult, ["biasFM"], ["hbiasFM"])
    pg2, pgt2 = nextG()
    for c in range(8):
        TR(pg2[:, c * 4:(c + 1) * 4], convrow[:, c * 128:(c + 1) * 128], identf[0:4, 0:4], HR + ["identf"], [pgt2])
    CP("dve", convw[:].rearrange("p c j -> p (c j)"), pg2[:, 0:32], [pgt2], ["convw"])

    wbufs = [sb("wbuf%d" % i, [128, WB], BF16) for i in range(NWBUF)]
    WSPEC = []
    WSLOT = {}

    def wreg(key, K, pieces, zero=False):
        WSLOT[key] = len(WSPEC)
        WSPEC.append((key, K, pieces, zero))

    for kv in range(2):
        WSLOT[("w1", kv)] = len(WSPEC)
        WSPEC.append((("w1", kv), 16, "w1", kv))
    wreg("memk", 8, [(0, 512, w_mem_kv[:, 0:512])])
    wreg("memv", 8, [(0, 512, w_mem_kv[:, 512:1024])])
    wreg("mlk", 8, [(0, 512, w_in[:, C_MLK:C_MLK + 512])])
    wreg("mlv", 8, [(0, 512, w_in[:, C_MLV:C_MLV + 512])])
    wreg("cbuf", 8, [(0, 128, w_in[:, C_KS:C_KS + 128]), (128, 128, w_in[:, C_KW:C_KW + 128]),
                     (256, 64, w_in[:, C_KC:C_KC + 64]), (320, 64, w_in[:, C_KC:C_KC + 64]),
                     (384, 64, w_in[:, C_KC + 64:C_KC + 128]), (448, 64, w_in[:, C_KC + 64:C_KC + 128])])
    wreg("dbuf", 8, [(0, 64, w_in[:, C_VC:C_VC + 64]), (64, 64, w_in[:, C_VC:C_VC + 64]),
                     (128, 64, w_in[:, C_VC + 64:C_VC + 128]), (192, 64, w_in[:, C_VC + 64:C_VC + 128]),
                     (256, 128, w_in[:, C_VS:C_VS + 128]), (384, 128, w_in[:, C_VW:C_VW + 128]),
                     (512, 8, w_in[:, C_MLI:C_MLI + 8]), (520, 24, w_in[:, C_NSG:C_NSG + 24])])
    wreg("mlq", 8, [(0, 512, w_in[:, C_MLQ:C_MLQ + 512])])
    wreg("mlo", 8, [(0, 512, w_in[:, C_MLO:C_MLO + 512])])
    pcs = []
    for p_ in range(4):
        pcs.append((p_ * 128, 64, w_in[:, C_NSQ + p_ * 64:C_NSQ + (p_ + 1) * 64]))
        pcs.append((p_ * 128 + 64, 64, w_in[:, C_NSQ + (p_ + 4) * 64:C_NSQ + (p_ + 5) * 64]))
    wreg("nsq", 8, pcs)
    wreg("xaq", 8, [(0, 512, w_in[:, C_XAQ:C_XAQ + 512])])
    for j in range(3):
        for hf in range(2):
            wreg(("mg", j, hf), 8, [(0, 512, w_in[:, C_MG + j * 1024 + hf * 512:C_MG + j * 1024 + (hf + 1) * 512])])
        wreg(("br", j), 4, [(0, 1024, w_branch[j])])
    for dh in range(2):
        wreg(("out", dh), 8, [(0, 512, w_out[:, dh * 512:(dh + 1) * 512])])
    for fb in range(8):
        wreg(("ff1", fb), 8, [(0, 512, w_ff1[:, fb * 512:(fb + 1) * 512])])
    for dh in range(2):
        for fg in range(4):
            wreg(("ff2", dh, fg), 8, [(0, 512, w_ff2[fg * 1024:(fg + 1) * 1024, dh * 512:(dh + 1) * 512])])
    NSLOT = len(WSPEC)
    wscr = nc.dram_tensor("wscr", [NSLOT, 128, WB], BF16, kind="Internal").ap()
    MSET("pool", wbufs[0][:], 0.0, [("wbuf", 0)])

    def emit_conv(idx, extra_reads=()):
        key, K, pieces, zero = WSPEC[idx]
        if key in FT_KEYS:
            return
        er = list(extra_reads)
        if pieces == "w1":
            kv = zero
            src = cmp_w1[kv].rearrange("(lp two d) h -> two d lp h", two=2, d=64)
            v = wscr[idx][:, 0:4096].rearrange("p (k w) -> p k w", k=16)
            DMA("pool", v[0:64], src[1], er, [("wscr", idx)], ("wscr", idx))
            DMA("pool", v[64:128], src[0], er, [("wscr", idx)], ("wscr", idx))
            return
        W = 512 if zero else sum(n for _, n, _ in pieces)
        v = wscr[idx][:, 0:K * W].rearrange("p (k w) -> p k w", k=K)
        rd = []
        if zero:
            DMA("pool", wscr[idx][:, 0:K * W], wbufs[0][:, 0:K * W], [("wbuf", 0)] + er, [("wscrz", idx)], ("wscrz", idx))
            rd = [("wscrz", idx)]
        for off, n, src_ap in pieces:
            DMA("pool", v[:, :, off:off + n], src_ap.rearrange("(k p) n -> p k n", p=128), rd + er, [("wscr", idx)], ("wscr", idx))

    converted = set()
    FT_KEYS = set(["mlk", "mlv", "cbuf", "dbuf", ("w1", 0), ("w1", 1)])

    def wfetch(key, donetok=None):
        idx = WSLOT[key]
        _, K, pieces, zero = WSPEC[idx]
        W = 256 if pieces == "w1" else (512 if zero else sum(n for _, n, _ in pieces))
        nhead = int(os.environ.get("K_NHEAD", "0"))
        if nhead == 0:
            i = rot["W"] % NWBUF
            rot["W"] += 1
        elif S.tag in ("H_merge", "I_outproj", "J_ffn", "K_final"):
            i = nhead + rot["W2"] % (NWBUF - nhead)
            rot["W2"] += 1
        else:
            i = rot["W"] % nhead
            rot["W"] += 1
        tok = ("wbuf", i)
        extra = [donetok] if donetok else []
        if key in FT_KEYS and idx not in converted:
            converted.add(idx)
            bv = wbufs[i][:, 0:K * W].rearrange("p (k w) -> p k w", k=K)
            if pieces == "w1":
                kv = zero
                src = cmp_w1[kv].rearrange("(lp two d) h -> two d lp h", two=2, d=64)
                DMA("pool", bv[0:64], src[1], [], [tok], tok, nbm=2)
                DMA("pool", bv[64:128], src[0], [], [tok] + extra, tok, nbm=2)
            else:
                if zero:
                    MSET("pool", wbufs[i][:, 0:K * W], 0.0, [tok])
                for pi, (off, n, src_ap) in enumerate(pieces):
                    last = pi == len(pieces) - 1
                    DMA("pool", bv[:, :, off:off + n], src_ap.rearrange("(k p) n -> p k n", p=128), [], [tok] + (extra if last else []), tok, nbm=2)
            DMA("sp", wscr[idx][:, 0:K * W], wbufs[i][:, 0:K * W], [tok], [("wscr", idx)], ("wscr", idx))
        else:
            DMA("sp", wbufs[i][:, 0:K * W], wscr[idx][:, 0:K * W], [("wscr", idx)], [tok] + extra, tok)
        return wbufs[i][:, 0:K * W].rearrange("p (k w) -> p k w", k=K), tok

    ksT = sb("ksT", [128, SEQ], BF16)
    kwT = sb("kwT", [128, 1024], BF16)
    vs_aug = sb("vs_aug", [128, NTILE, 2, 65], BF16)
    vw_aug = sb("vw_aug", [128, 8, 2, 65], BF16)
    kcT = sb("kcT", [128, 256], BF16)
    vcT = sb("vcT", [128, 256], BF16)
    vcx = sb("vcx", [128, 2, 2, 128], BF16)
    mkT = sb("mkT", [128, 4, 256], BF16)
    mv_aug = sb("mv_aug", [128, 2, 4, 129], BF16)
    W2p = sb("W2p", [128, 2, 2, 2, 128], BF16)
    pep = sb("pep", [128, 2, 16], BF16)
    hb = sb("hb", [128, 2, 2])
    kcin2 = sb("kcin2", [128, 4, L + 17], BF16)
    Cst = sb("Cst", [128, 4, 129])
    Cb = sb("Cb", [128, 4, 129], BF16)
    prek = sb("prek", [128, 4, 3])
    preq = sb("preq", [128, 4, 3])

    MSET("dve", vs_aug[:], 1.0, [("vs_aug", i) for i in range(NST)])
    MSET("pool", vw_aug[:], 1.0, [("vw_aug", i) for i in range(8 // NT)])
    MSET("dve", kcT[:], 0.0, ["kcT"])
    MSET("dve", vcT[:], 0.0, ["vcT"])
    MSET("pool", kwT[:], 0.0, [("kwT", i) for i in range(8 // NT)])
    MSET("dve", mv_aug[:], 1.0, ["mv_aug"])
    MSET("dve", kcin2[:], 0.0, ["kcin2"])
    MSET("dve", Cst[:], 0.0, ["Cst"])
    MSET("dve", Cb[:], 0.0, ["Cb"])
    MSET("dve", prek[:], 0.0, [("prek", c) for c in range(4)])
    MSET("dve", preq[:], 0.0, [("preq", c) for c in range(4)])
    MSET("pool", W2p[:], 0.0, ["W2p0"])
    MSET("dve", vcx[:], 0.0, ["vcx"])
    MSET("dve", vcx[:, :, :, 64:65], 1.0, ["vcx"], ["vcx"])
    for g in range(2):
        DMA("sp", vcx[:, :, g, 65:128], t_ovl, ["vcx"], ["vcx_ovl"], "vcx_ovl")
    VCX = ["vcx", "vcx_ovl"]
    for kv in range(2):
        srcp = cmp_pe[kv].rearrange("(lp two) d -> two d lp", two=2)
        DMA("pool", pep[0:64, kv, :], srcp[1], [], ["pep"], "pep", slow=True)
        DMA("pool", pep[64:128, kv, :], srcp[0], [], ["pep"], "pep", slow=True)
        for half in range(2):
            for g in range(2):
                DMA("pool", W2p[:, kv, half, g, g * 64:(g + 1) * 64], cmp_w2[kv, half * 128:(half + 1) * 128, :], ["W2p0"], ["W2p"], "W2p")
    W2T = ["W2p0", "W2p"]
    NCONV0 = WSLOT["mlq"]
    first = [WSLOT[k] for k in ("mlk", "mlv", "cbuf", "dbuf", ("w1", 0), ("w1", 1))]
    for idx in first:
        emit_conv(idx)
    for idx in range(NSLOT):
        if WSPEC[idx][3] is True:
            emit_conv(idx)
    conv_rest = [WSLOT["memk"], WSLOT["memv"]] + [idx for idx in range(NCONV0, WSLOT[("ff1", 0)]) if WSPEC[idx][3] is not True]
    conv_ffn = list(range(WSLOT[("ff1", 0)], NSLOT))

    for kv in range(2):
        w1v, w1t = wfetch(("w1", kv))
        for half in range(2):
            pgx, pgxt = nextG()
            for lp in range(16):
                MM(pgx[:, 0:1], w1v[:, lp, half * 128:(half + 1) * 128], pep[:, kv, lp:lp + 1], lp == 0, lp == 15, [w1t, "pep"], [pgxt])
            CP("dve", hb[:, kv, half:half + 1], pgx[:, 0:1], [pgxt], ["hb"])

    stat = sb("stat", [128, 16])
    xbuf = [sb("xbuf%d" % i, [128, D]) for i in range(2)]
    xnbs = [sb("xnb%d" % i, [128, D], BF16) for i in range(2)]

    def nextX():
        i = rot["X"] % 2
        rot["X"] += 1
        return i

    def rms_scale(src_ap, src_toks, junk=None, junktok=None):
        if junk is None:
            junk, junktok = xnbs[0], ("xnb", 0)
        si = rot["S"] % 8
        rot["S"] += 1
        tok = ("stat", si)
        col = stat[:, si:si + 1]
        ACT(junk[:], src_ap, AF.Square, src_toks, [junktok, tok], accum_out=col)
        TS("dve", col, col, 1.0 / D, ALU.mult, [tok], [tok], s2=EPS, op1=ALU.add)
        TT("pool", col, col, mhalf[:, 0:1], ALU.pow, [tok, "mhalf"], [tok])
        return col, tok

    def rmsnorm_to_T(src_ap, src_toks, grep, grep_tok, dstT, dst_cols, dst_tok):
        gt_ = list(grep_tok) if isinstance(grep_tok, list) else [grep_tok]
        dst_tok = list(dst_tok) if isinstance(dst_tok, list) else [dst_tok]
        ni = rot["N"] % 2 if os.environ.get("K_XNB2", "1") == "1" else 0
        rot["N"] += 1
        xn_, xtok = xnbs[ni], ("xnb", ni)
        col, tok = rms_scale(src_ap, src_toks, xn_, xtok)
        STT(xn_[:], src_ap, col, grep[:], ALU.mult, ALU.mult, list(src_toks) + [tok] + gt_, [xtok])
        for hf in range(2):
            pt, ptt = nextT()
            for k in range(4):
                kk = hf * 4 + k
                TR(pt[:, k * 128:(k + 1) * 128], xn_[:, kk * 128:(kk + 1) * 128], identb[:], [xtok, "identb"], [ptt])
            CP("act", dstT[:, hf * 4:(hf + 1) * 4, dst_cols], pt.rearrange("p (k n) -> p k n", k=4), [ptt], dst_tok)

    aTraw = sb("aTraw", [128, 16 * L])
    aT = aTraw[:].bitcast(BF16).rearrange("p (f l) -> p f l", f=32)
    macc = aTraw[:, 8 * L:16 * L].rearrange("p (f l) -> p f l", f=8)
    memT = aT[:, 0:8, :]
    AT8 = [("aT", fc) for fc in range(8)]

    def mem_setup():
        DMA("sp", gmem_r, g_mem.partition_broadcast(128).squeeze(1), [], HR, "gmem_r")
        for mt in range(2):
            xi = nextX()
            DMA("sp", xbuf[xi][:], mem[mt * 128:(mt + 1) * 128, :], [], [("xbuf", xi)], ("xbuf", xi))
            rmsnorm_to_T(xbuf[xi][:], [("xbuf", xi)], gmem_r, HR, memT, slice(mt * 128, (mt + 1) * 128), AT8)
        wk, wt_ = wfetch("memk")
        for h in range(4):
            pgx, pgxt = nextG()
            for k in range(8):
                MM(pgx[:, 0:256], wk[:, k, h * 128:(h + 1) * 128], memT[:, k, :], k == 0, k == 7, [wt_] + AT8, [pgxt])
            CP("act", mkT[:, h, :], pgx[:, 0:256], [pgxt], ["mkT"])
        wv, wt_ = wfetch("memv")
        for mt in range(2):
            pgx, pgxt = nextG()
            for k in range(8):
                MM(pgx[:, 0:512], memT[:, k, mt * 128:(mt + 1) * 128], wv[:, k, :], k == 0, k == 7, [wt_] + AT8, [pgxt])
            CP("act", mv_aug[:, mt, :, 0:128], pgx[:, 0:512].rearrange("p (h d) -> p h d", h=4), [pgxt, "mv_aug"], ["mv_aug"])
    dbg("hb", hb[:], ["hb"], [128, 2, 2])
    dbg("biasFM", biasFM[:], ["biasFM"], [128, 64])
    dbg("convw", convw[:], ["convw"], [128, 8, 4])

    NXN = int(os.environ.get("K_NXN", "1"))
    xnTs = [sb("xnT%d" % i, [128, 8, L], BF16) for i in range(NXN)]
    XN = [xnTs[0]]
    XT = [("xnT", 0)]
    kT = sb("kT", [128, 4, L], BF16)
    qT = sb("qT", [128, 4, L], BF16)
    ktm = sb("ktm", [128, NT, 512], BF16)
    v_aug = sb("v_aug", [128, NT, 4, 129], BF16)
    og = sb("og", [128, NT, 512], BF16)
    Rms = [sb("Rm_%d" % i, [128, 4, 128]) for i in range(2)]
    otmp = Rms[0][:].rearrange("p h t -> p (h t)")
    gate = sb("gate", [128, NT, 8])
    gsig = sb("gsig", [128, NT, 24])
    nqT = sb("nqT", [128, NT, 8, 128], BF16)
    MSET("pool", nqT[:], 0.0, ["nqT"])
    xqT = sb("xqT", [128, 4, L], BF16)
    NA = sb("NA", [68, NT, 2, 512], BF16)
    yT = sb("yT", [128, 12, L], BF16)
    MSET("dve", v_aug[:], 1.0, [("v_aug", t) for t in range(NT)])

    def fm_group(wv_, wtok, col0, M=128):
        pgx, pgxt = nextG()
        for k in range(8):
            MM(pgx[0:M, 0:L], wv_[:, k, col0:col0 + M], XN[0][:, k, :], k == 0, k == 7, [wtok, XT[0]], [pgxt])
        return pgx, pgxt

    NCV = int(os.environ.get("K_NCV", "2"))
    pretmps = [sb("pretmp%d" % i, [128, L + 3]) for i in range(NCV)]
    convaccs = [sb("convacc%d" % i, [128, L]) for i in range(NCV)]

    def conv_silu(pre, pretok, pgx, pgxt, c, brow_i, wchunk, dst, dsttok):
        ci = rot["V"] % NCV
        rot["V"] += 1
        pretmp, convacc = pretmps[ci], convaccs[ci]
        ptk, cak = ("pretmp", ci), ("convacc", ci)
        pt = (pretok, c)
        CP("dve", pretmp[:, 0:3], pre[:, c, :], [pt, ptk], [ptk])
        ACT(pretmp[:, 3:3 + L], pgx[:, 0:L], AF.Identity, [pgxt, "biasFM", ptk], [ptk], bias=biasFM[:, brow_i:brow_i + 1])
        TS("dve", convacc[:], pretmp[:, 3:3 + L], convw[:, wchunk, 0:1], ALU.mult, [ptk, "convw"], [cak])
        for j in range(1, 4):
            STT(convacc[:], pretmp[:, 3 - j:3 - j + L], convw[:, wchunk, j:j + 1], convacc[:], ALU.mult, ALU.add, [ptk, "convw", cak], [cak])
        ACT(pretmp[:, 0:L], convacc[:], AF.Tanh, [cak, ptk], [ptk], scale=0.5)
        TS("dve", pretmp[:, 0:L], pretmp[:, 0:L], 0.5, ALU.mult, [ptk], [ptk], s2=0.5, op1=ALU.add)
        TT("dve", dst[:, c, :], pretmp[:, 0:L], convacc[:], ALU.mult, [ptk, cak], [dsttok])
        CP("dve", pre[:, c, :], pretmp[:, L:L + 3], [ptk], [pt])

    l1s = [sb("l1_%d" % i, [128, 4]) for i in range(2)]
    lfms = [sb("lfm_%d" % i, [128, 4]) for i in range(2)]
    expBs = [sb("expB_%d" % i, [128, 4, 128]) for i in range(2)]
    wcols = [sb("wcol_%d" % i, [128, 4]) for i in range(2)]
    ecols = [sb("ecol_%d" % i, [128, 4]) for i in range(2)]
    DT = sb("DT", [128, 4, 128], BF16)
    PT = sb("PT", [128, 512], BF16)
    qbT = sb("qbT", [128, 4, 128], BF16)
    ves = [sb("ve_%d" % i, [128, 4, 129], BF16) for i in range(2)]
    mst = sb("mst", [128, 16])
    yml = sb("yml", [128, 512], BF16)
    LNS = float(np.log(128.0 ** -0.5))

    l1_all = [sb("l1all%d" % i, [128, NT, 4]) for i in range(2)]
    lfm_all = [sb("lfmall%d" % i, [128, NT, 4]) for i in range(2)]

    def mlstm_gates(st):
        p = st % 2
        GT = [("gate", t) for t in range(NT)]
        ACT(l1_all[p][:], gate[:, :, 4:8], AF.Exp, GT, [("l1_all", p)], scale=-1.0)
        ACT(l1_all[p][:], l1_all[p][:], AF.Ln, [("l1_all", p)], [("l1_all", p)], bias=1.0)
        TS("dve", lfm_all[p][:], l1_all[p][:], -1.0, ALU.mult, [("l1_all", p)], [("lfm_all", p)])

    def mlstm_tile(st, t, own):
        gt = st * NT + t
        par = gt % 2
        l1, lfm, Rm, expB, wcol, ecol, ve = l1s[par], lfms[par], Rms[par], expBs[par], wcols[par], ecols[par], ves[par]
        T_ = lambda n: (n, par)
        cols = slice(t * 128, (t + 1) * 128)
        gtok = ("gate", t)
        lfm = lfm_all[st % 2][:, t, :]
        LFT = ("lfm_all", st % 2)
        for h in range(4):
            TS("dve", Rm[:, h, :], Uf[:], lfm[:, h:h + 1], ALU.mult, ["Uf", LFT], [T_("Rm")])
        pB, pBt = nextG()
        MM(pB[:, 0:512], onesf[:], Rm[:].rearrange("p h t -> p (h t)"), True, True, ["onesf", T_("Rm")], [pBt])
        pb2, pb2t = nextG()
        MM(pb2[:, 0:4], Uf[:], lfm, True, True, ["Uf", LFT], [pb2t])
        ACT(expB[:].rearrange("p h t -> p (h t)"), pB[:, 0:512], AF.Exp, [pBt], [T_("expB")])
        TT("dve", wcol[:], gate[:, t, 0:4], pb2[:, 0:4], ALU.subtract, [gtok, pb2t], [T_("wcol")])
        ACT(wcol[:], wcol[:], AF.Exp, [T_("wcol")], [T_("wcol")], bias=LNS)
        TT("dve", ecol[:], wcol[:], expB[:, :, 127], ALU.mult, [T_("wcol"), T_("expB")], [T_("ecol")])
        if own:
            for h in range(4):
                STT(DT[:, h, :], expB[:, h, :], wcol[:, h:h + 1], Uf[:], ALU.mult, ALU.mult, [T_("expB"), T_("wcol"), "Uf"], ["DT"])
            pS, pSt = nextG()
            for h in range(4):
                MM(pS[:, h * 128:(h + 1) * 128], kT[:, h, cols], qT[:, h, cols], True, True, ["kT", "qT"], [pSt])
            TT("dve", PT[:], pS[:, 0:512], DT[:].rearrange("p h t -> p (h t)"), ALU.mult, [pSt, "DT"], ["PT"])
            TT("pool", qbT[:], qT[:, :, cols], expB[:], ALU.mult, ["qT", T_("expB")], ["qbT"])
            pO = []
            for hp in range(2):
                po, pot = nextC()
                pO.append((po, pot))
                for hh in range(2):
                    h = hp * 2 + hh
                    MM(po[:, hh * 129:(hh + 1) * 129], PT[:, h * 128:(h + 1) * 128], v_aug[:, t, h, :], True, False, ["PT", ("v_aug", t)], [pot])
                    MM(po[:, hh * 129:(hh + 1) * 129], qbT[:, h, :], Cb[:, h, :], False, True, ["qbT", "Cb"], [pot])
            for hp in range(2):
                po, pot = pO[hp]
                pv = po[:, 0:258].rearrange("p (h c) -> p h c", h=2)
                ACT(mst[:, hp * 2:hp * 2 + 2], pv[:, :, 128], AF.Abs, [pot, "mstA"], ["mstA"])
            TS("dve", mst[:, 0:4], mst[:, 0:4], 1.0, ALU.max, ["mstA"], ["mstA"])
            RCP(mst[:, 0:4], mst[:, 0:4], ["mstA"], ["mstA"])
            for h in range(4):
                po, pot = pO[h // 2]
                hh = h % 2
                ACT(PT[:, 0:128], po[:, hh * 129:hh * 129 + 128], AF.Square, [pot, "mstA", "mstB"], ["PT", "mstB"], scale=mst[:, h:h + 1], accum_out=mst[:, 4 + h:5 + h])
            TS("dve", mst[:, 4:8], mst[:, 4:8], 1.0 / 128, ALU.mult, ["mstB"], ["mstB"], s2=EPS, op1=ALU.add)
            TT("pool", mst[:, 4:8], mst[:, 4:8], mhalf[:, 0:4], ALU.pow, ["mstB", "mhalf"], ["mstB"])
            TT("dve", mst[:, 8:12], mst[:, 0:4], mst[:, 4:8], ALU.mult, ["mstA", "mstB"], ["mstC"])
            for h in range(4):
                po, pot = pO[h // 2]
                hh = h % 2
                STT(yml[:, h * 128:(h + 1) * 128], po[:, hh * 129:hh * 129 + 128], mst[:, 8 + h:9 + h], og[:, t, h * 128:(h + 1) * 128], ALU.mult, ALU.mult,
                    [pot, "mstC", ("og", t)], ["yml"])
            pt, ptt = nextT()
            for h in range(4):
                TR(pt[:, h * 128:(h + 1) * 128], yml[:, h * 128:(h + 1) * 128], identb[:], ["yml", "identb"], [ptt])
            CP("act", yT[:, 0:4, cols], pt.rearrange("p (k n) -> p k n", k=4), [ptt], [("yT", 0, t)])
        if gt == NTILE - 1:
            return
        TT("dve", ve[:], v_aug[:, t, :, :], ecol[:].unsqueeze(2).to_broadcast([128, 4, 129]), ALU.mult, [("v_aug", t), T_("ecol")], [T_("ve")])
        for hp in range(2):
            pa, pat = nextC()
            for hh in range(2):
                h = hp * 2 + hh
                MM(pa[:, hh * 129:(hh + 1) * 129], ktm[:, t, h * 128:(h + 1) * 128], ve[:, h, :], True, True, [("ktm", t), T_("ve")], [pat])
            for hh in range(2):
                h = hp * 2 + hh
                STT(Cst[:, h, :], Cst[:, h, :], expB[:, h, 127:128], pa[:, hh * 129:(hh + 1) * 129], ALU.mult, ALU.add, [pat, T_("expB"), "Cst"], ["Cst"])
        if gt == NTILE // 2 - 1:
            TS("dve", Cst[:], Cst[:], flag2[:, 0:1], ALU.mult, ["Cst", "flag2"], ["Cst"])
        CP("act", Cb[:], Cst[:], ["Cst"], ["Cb"])

    NPB = int(os.environ.get("K_NP", "3"))
    PcT = [sb("PcT%d" % i, [128, 512], BF16) for i in range(NPB)]

    def nextP():
        i = rot["P"] % NPB
        rot["P"] += 1
        return PcT[i], ("PcT", i)

    nst = sb("nst", [128, 32])
    impacc = sb("impacc", [128, 64])
    imptmp = sb("imptmp", [128, 4, 63])
    score = sb("score", [128, 64])
    score2 = sb("score2", [128, 64])
    mx8 = sb("mx8", [128, 16])
    negm = sb("negm", [128, 64], BF16)
    ynsa = sb("ynsa", [128, 512], BF16)
    ytmp = sb("ytmp", [128, 64])
    yxa = sb("yxa", [128, 512], BF16)

    def attn_scores(pS, pSt, mms):
        n = len(mms)
        for i, (l_ap, r_ap, rd) in enumerate(mms):
            MM(pS[:, 0:512], l_ap, r_ap, i == 0, i == n - 1, rd, [pSt])

    def run_jobs(jobs):
        issued = []

        def issue_scores(j):
            if j.get("pre") is not None:
                j["pre"]()
            pS, pSt = nextG()
            attn_scores(pS, pSt, j["mms"])
            issued.append((pS, pSt))

        if not jobs:
            return
        issue_scores(jobs[0])
        for i, j in enumerate(jobs):
            if i + 1 < len(jobs):
                issue_scores(jobs[i + 1])
            pS, pSt = issued[i]
            P, Pt = nextP()
            ACT(P[:], pS[:, 0:512], AF.Exp, [pSt], [Pt], scale=j.get("scale", 1.0))
            j["pv"](P, Pt)
            if j.get("post") is not None:
                j["post"]()

    def nsa_group(st, t, g):
        gt = st * NT + t
        ti = gt - NTILE // 2
        cols = slice(t * 128, (t + 1) * 128)
        q4 = nqT[:, t, 4 * g:4 * g + 4, :]
        na_full = NA[0:68, t, g, :]
        na_aug = NA[64:68, t, g, :]
        natok = ("NA", t, g)
        pOC, pOCt = nextC()
        pOS, pOSt = nextC()
        pOW, pOWt = nextC()
        oc3 = pOC[:, 0:512].rearrange("p (r c) -> p r c", r=4)
        os3 = pOS[:, 0:260].rearrange("p (r c) -> p r c", r=4)
        ow3 = pOW[:, 0:260].rearrange("p (r c) -> p r c", r=4)
        jobs = []
        for nt in range(2):
            mms = [(kcT[:, nt * 128:(nt + 1) * 128], q4, ["kcT", "nqT"]),
                   (CA[64:68, nt * 128:(nt + 1) * 128], na_aug, ["CA", "NAaug"]),
                   (identb[:], okc[:, nt, ti * 128:(ti + 1) * 128].unsqueeze(1).to_broadcast([128, 4, 128]), ["identb", "okc"])]

            def pv_c(P, Pt, nt=nt):
                for r_ in range(4):
                    MM(pOC[:, r_ * 128:(r_ + 1) * 128], P[:, r_ * 128:(r_ + 1) * 128], vcx[:, nt, g, :], nt == 0 and r_ == 0, nt == 1, [Pt] + VCX, [pOCt])
            jobs.append(dict(mms=mms, pv=pv_c))

        def topk_dve():
            TS("dve", nst[:, 0:4], oc3[:, :, 64], 1e-30, ALU.max, [pOCt, "nstA"], ["nstA"])
            RCP(nst[:, 0:4], nst[:, 0:4], ["nstA"], ["nstA"])
            TT("dve", imptmp[:], oc3[:, :, 65:128], nst[:, 0:4].unsqueeze(2).to_broadcast([128, 4, 63]), ALU.mult, [pOCt, "nstA"], ["imptmp"])
            S.add("dve", lambda e: e.reduce_sum(out=impacc[:, 0:63], in_=imptmp[:].transpose([0, 2, 1]), axis=mybir.AxisListType.X),
                  ["imptmp"], ["impacc"], cost=0.35)
            CP("dve", score[:, 63:64], bsel[:, ti, 63:64], ["bsel", "score"], ["score"])
            TT("dve", score[:, 0:63], bsel[:, ti, 0:63], impacc[:, 0:63], ALU.add, ["bsel", "score", "impacc"], ["score"])
            S.add("dve", lambda e: e.max(out=mx8[:, 0:8], in_=score[:]), ["score", "mx8"], ["mx8"], cost=0.15)
            S.add("dve", lambda e: e.match_replace(out=score2[:], in_to_replace=mx8[:, 0:8], in_values=score[:], imm_value=-3.0e38), ["score", "mx8"], ["score2"], cost=0.25)
            S.add("dve", lambda e: e.max(out=mx8[:, 8:16], in_=score2[:]), ["score2", "mx8"], ["mx8"], cost=0.15)
            TS("dve", negm[:], score[:], mx8[:, 15:16], ALU.is_ge, ["score", "mx8"], ["negm"], s2=1.0, op1=ALU.subtract)
        jobs[-1]["post"] = topk_dve

        def mask_to_NA():
            pt, ptt = nextT()
            TR(pt[0:64, 0:128], negm[:], identb[:], ["negm", "identb"], [ptt])
            ACT(NA[0:64, t, g, :].rearrange("p (r q) -> p r q", r=4), pt[0:64, 0:128].unsqueeze(1).to_broadcast([64, 4, 128]), AF.Copy, [ptt], [natok], scale=-NEGM)

        for kt in range(gt - 4, gt + 1):
            slot = kt % 8
            mms = [(kwT[:, slot * 128:(slot + 1) * 128], q4, [("kwT", slot // NT), "nqT"]),
                   (EA[64:68, kt, :], na_aug, ["EA", "NAaug"])]
            if kt == gt:
                mms.append((identb[:], caus, ["identb", "caus"]))
            if kt == gt - 4:
                mms.append((identb[:], winm, ["identb", "winm"]))

            def pv_w(P, Pt, kt=kt, slot=slot):
                for r_ in range(4):
                    MM(pOW[:, r_ * 65:(r_ + 1) * 65], P[:, r_ * 128:(r_ + 1) * 128], vw_aug[:, slot, g, :], kt == gt - 4 and r_ == 0, kt == gt, [Pt, ("vw_aug", slot // NT)], [pOWt])
            jobs.append(dict(mms=mms, pv=pv_w))
        kt_lo = max(0, gt - 16) if g == 0 else 0
        for kt in range(kt_lo, gt + 1):
            mms = [(ksT[:, kt * 128:(kt + 1) * 128], q4, [("ksT", kt // NT), "nqT"]),
                   (EA[0:68, kt, :], na_full, ["EA", natok, "NAaug"])]
            if kt == gt:
                mms.append((identb[:], caus, ["identb", "caus"]))

            def pv_s(P, Pt, kt=kt):
                for r_ in range(4):
                    MM(pOS[:, r_ * 65:(r_ + 1) * 65], P[:, r_ * 128:(r_ + 1) * 128], vs_aug[:, kt, g, :], kt == kt_lo and r_ == 0, kt == gt, [Pt, ("vs_aug", kt // NT)], [pOSt])
            jobs.append(dict(mms=mms, pv=pv_s, pre=(mask_to_NA if kt == kt_lo else None)))
        run_jobs(jobs)
        TS("dve", nst[:, 4:8], os3[:, :, 64], 1e-30, ALU.max, [pOSt, "nstB"], ["nstB"])
        TS("dve", nst[:, 8:12], ow3[:, :, 64], 1e-30, ALU.max, [pOWt, "nstB"], ["nstB"])
        RCP(nst[:, 4:12], nst[:, 4:12], ["nstB"], ["nstB"])
        gv = gsig[:, t, g * 12:(g + 1) * 12].rearrange("p (r b) -> p r b", b=3)
        for bi in range(3):
            TT("dve", nst[:, 12 + bi * 4:16 + bi * 4], nst[:, bi * 4:bi * 4 + 4], gv[:, :, bi], ALU.mult, ["nstA", "nstB", ("gsig", t), "nstK"], ["nstK"])
        for r_ in range(4):
            hcol = (4 * g + r_) * 64
            TS("dve", ytmp[:], oc3[:, r_, 0:64], nst[:, 12 + r_:13 + r_], ALU.mult, [pOCt, "nstK"], ["ytmp"])
            STT(ytmp[:], os3[:, r_, 0:64], nst[:, 16 + r_:17 + r_], ytmp[:], ALU.mult, ALU.add, [pOSt, "ytmp", "nstK"], ["ytmp"])
            STT(ynsa[:, hcol:hcol + 64], ow3[:, r_, 0:64], nst[:, 20 + r_:21 + r_], ytmp[:], ALU.mult, ALU.add, [pOWt, "ytmp", "nstK"], ["ynsa"])

    def nsa_tile(st, t):
        cols = slice(t * 128, (t + 1) * 128)
        for g in range(2):
            nsa_group(st, t, g)
        pt, ptt = nextT()
        for h in range(4):
            TR(pt[:, h * 128:(h + 1) * 128], ynsa[:, h * 128:(h + 1) * 128], identb[:], ["ynsa", "identb"], [ptt])
        CP("act", yT[:, 4:8, cols], pt.rearrange("p (k n) -> p k n", k=4), [ptt], [("yT", 1, t)])

    def xa_tile(st, t):
        cols = slice(t * 128, (t + 1) * 128)
        pO = [nextC(), nextC()]
        jobs = []
        for mt in range(2):
            def scores_x(mt=mt):
                pass
            mms = None
            jobs.append(mt)
        held = []
        for mt in range(2):
            pS, pSt = nextG()
            for h in range(4):
                MM(pS[:, h * 128:(h + 1) * 128], mkT[:, h, mt * 128:(mt + 1) * 128], xqT[:, h, cols], True, True, ["mkT", "xqT"], [pSt])
            held.append((pS, pSt))
        for mt in range(2):
            pS, pSt = held[mt]
            P, Pt = nextP()
            ACT(P[:], pS[:, 0:512], AF.Exp, [pSt], [Pt], scale=float(128.0 ** -0.5))
            for h in range(4):
                po, pot = pO[h // 2]
                hh = h % 2
                MM(po[:, hh * 129:(hh + 1) * 129], P[:, h * 128:(h + 1) * 128], mv_aug[:, mt, h, :], mt == 0 and hh == 0, mt == 1, [Pt, "mv_aug"], [pot])
        for hp in range(2):
            po, pot = pO[hp]
            pv = po[:, 0:258].rearrange("p (h c) -> p h c", h=2)
            TS("dve", nst[:, 24 + hp * 2:26 + hp * 2], pv[:, :, 128], 1e-30, ALU.max, [pot, "nstX"], ["nstX"])
        RCP(nst[:, 24:28], nst[:, 24:28], ["nstX"], ["nstX"])
        for h in range(4):
            po, pot = pO[h // 2]
            hh = h % 2
            TS("dve", yxa[:, h * 128:(h + 1) * 128], po[:, hh * 129:hh * 129 + 128], nst[:, 24 + h:25 + h], ALU.mult, [pot, "nstX"], ["yxa"])
        pt, ptt = nextT()
        for h in range(4):
            TR(pt[:, h * 128:(h + 1) * 128], yxa[:, h * 128:(h + 1) * 128], identb[:], ["yxa", "identb"], [ptt])
        CP("act", yT[:, 8:12, cols], pt.rearrange("p (k n) -> p k n", k=4), [ptt], [("yT", 2, t)])

    gT = sb("gT", [128, 2, 2, 2, 32], BF16)
    gtmp = sb("gtmp", [128, 128])
    gtmp2 = sb("gtmp2", [128, 128])
    NB = L // 16

    def compress_st(st):
        T0 = st * L
        n0 = T0 // 16 - 1
        W1 = [wfetch(("w1", kv), donetok=(("w1done", st) if kv == 1 else None)) for kv in range(2)]
        if os.environ.get("K_CB", "1") == "1":
            for kv in range(2):
                w1v, w1t = W1[kv]
                pgx, pgxt = nextG()
                for g in range(2):
                    for half in range(2):
                        c0 = (g * 2 + half) * NB
                        for lp in range(16):
                            MM(pgx[:, c0:c0 + NB], w1v[:, lp, half * 128:(half + 1) * 128], kcin2[:, kv * 2 + g, 2 * lp + 1:2 * lp + 1 + 16 * (NB - 1) + 1:16],
                               lp == 0, lp == 15, [w1t, "kcin2"], [pgxt])
                NC4 = 4 * NB
                gx = gtmp[:, 0:NC4].rearrange("p (g h n) -> p g h n", g=2, h=2)
                px = pgx[:, 0:NC4].rearrange("p (g h n) -> p g h n", g=2, h=2)
                for half in range(2):
                    ACT(gx[:, :, half, :], px[:, :, half, :], AF.Identity, [pgxt, "hb", "gtmp"], ["gtmp"], bias=hb[:, kv, half:half + 1])
                TT("dve", gtmp2[:, 0:NC4], gtmp[:, 0:NC4], gtmp[:, 0:NC4], ALU.mult, ["gtmp"], ["gtmp2"])
                TS("dve", gtmp2[:, 0:NC4], gtmp2[:, 0:NC4], 0.044715, ALU.mult, ["gtmp2"], ["gtmp2"], s2=1.0, op1=ALU.add)
                TT("dve", gtmp2[:, 0:NC4], gtmp2[:, 0:NC4], gtmp[:, 0:NC4], ALU.mult, ["gtmp", "gtmp2"], ["gtmp2"])
                SIGT(gtmp2[:, 0:NC4], gtmp2[:, 0:NC4], ["gtmp2"], ["gtmp2"], scale=1.5957691216057308)
                TT("dve", gT[:, kv, :, :, 0:NB], gtmp2[:, 0:NC4].rearrange("p (g h n) -> p g h n", g=2, h=2), gx, ALU.mult, ["gtmp", "gtmp2"], ["gT"])
                pgy, pgyt = nextG()
                i = 0
                for g in range(2):
                    for half in range(2):
                        MM(pgy[:, 0:NB], W2p[:, kv, half, g, :], gT[:, kv, g, half, 0:NB], i == 0, i == 3, W2T + ["gT"], [pgyt])
                        i += 1
                dstc = kcT if kv == 0 else vcT
                dtok = "kcT" if kv == 0 else "vcT"
                lo = 1 if st == 0 else 0
                CP("act", dstc[:, n0 + lo:n0 + NB], pgy[:, lo:NB], [pgyt], [dtok])
        else:
            for kv in range(2):
                w1v, w1t = W1[kv]
                for g in range(2):
                    for half in range(2):
                        pgx, pgxt = nextG()
                        for lp in range(16):
                            MM(pgx[:, 0:NB], w1v[:, lp, half * 128:(half + 1) * 128], kcin2[:, kv * 2 + g, 2 * lp + 1:2 * lp + 1 + 16 * (NB - 1) + 1:16],
                               lp == 0, lp == 15, [w1t, "kcin2"], [pgxt])
                        ACT(gtmp[:, 0:NB], pgx[:, 0:NB], AF.Identity, [pgxt, "hb"], ["gtmp"], bias=hb[:, kv, half:half + 1])
                        TT("dve", gtmp2[:, 0:NB], gtmp[:, 0:NB], gtmp[:, 0:NB], ALU.mult, ["gtmp"], ["gtmp2"])
                        TS("dve", gtmp2[:, 0:NB], gtmp2[:, 0:NB], 0.044715, ALU.mult, ["gtmp2"], ["gtmp2"], s2=1.0, op1=ALU.add)
                        TT("dve", gtmp2[:, 0:NB], gtmp2[:, 0:NB], gtmp[:, 0:NB], ALU.mult, ["gtmp", "gtmp2"], ["gtmp2"])
                        SIGT(gtmp2[:, 0:NB], gtmp2[:, 0:NB], ["gtmp2"], ["gtmp2"], scale=1.5957691216057308)
                        TT("dve", gT[:, kv, g, half, 0:NB], gtmp2[:, 0:NB], gtmp[:, 0:NB], ALU.mult, ["gtmp", "gtmp2"], ["gT"])
                pgy, pgyt = nextG()
                i = 0
                for g in range(2):
                    for half in range(2):
                        MM(pgy[:, 0:NB], W2p[:, kv, half, g, :], gT[:, kv, g, half, 0:NB], i == 0, i == 3, W2T + ["gT"], [pgyt])
                        i += 1
                dstc = kcT if kv == 0 else vcT
                dtok = "kcT" if kv == 0 else "vcT"
                lo = 1 if st == 0 else 0
                CP("act", dstc[:, n0 + lo:n0 + NB], pgy[:, lo:NB], [pgyt], [dtok])
        nts = sorted(set([max(n0, 0) // 128, (n0 + NB - 1) // 128]))
        for nt in nts:
            pt, ptt = nextT()
            TR(pt[:, 0:128], vcT[:, nt * 128:(nt + 1) * 128], identb[:], ["vcT", "identb"], [ptt])
            CP("act", vcx[:, nt, :, 0:64], pt[:, 0:128].rearrange("p (g d) -> p g d", g=2), [ptt] + VCX, ["vcx"])
        CP("dve", kcin2[:, :, 0:17], kcin2[:, :, L:L + 17], ["kcin2"], ["kcin2"])

    mT = sb("mT", [128, 8, L], BF16)
    USE_HNT = os.environ.get("K_HNT", "0") == "1"
    hnT = sb("hnT", [128, 8, L], BF16) if USE_HNT else None
    gsb = [sb("gsb%d" % i, [128, L]) for i in range(2)]
    mtmp = sb("mtmp", [128, L])
    rtmp = [sb("rtmp%d" % i, [128, L], BF16) for i in range(2)]
    cnt2 = {"g": 0, "r": 0}

    def own_tail(st):
        S.tag = "H_merge"
        YT = lambda j: [("yT", j, t) for t in range(NT)]
        for j in range(3):
            wbm = [wfetch(("mg", j, hf)) for hf in range(2)]
            wbr, wbrt = wfetch(("br", j))
            for dc in range(8):
                wg, wgt = wbm[dc // 4]
                co = (dc % 4) * 128
                pg_, pgt_ = nextG()
                for k in range(8):
                    MM(pg_[:, 0:L], wg[:, k, co:co + 128], XN[0][:, k, :], k == 0, k == 7, [wgt, XT[0]], [pgt_])
                gi = cnt2["g"] % 2
                cnt2["g"] += 1
                gb, gbt = gsb[gi], ("gsb", gi)
                bi = FMROW[("mg", j * 8 + dc)]
                SIGT(gb[:], pg_[:, 0:L], [pgt_, "hbiasFM"], [gbt], hbias=hbiasFM[:, bi:bi + 1])
                pu, put = nextG()
                for k in range(4):
                    MM(pu[:, 0:L], wbr[:, k, dc * 128:(dc + 1) * 128], yT[:, j * 4 + k, :], k == 0, k == 3, [wbrt] + YT(j), [put])
                if j == 0:
                    TT("dve", macc[:, dc, :], pu[:, 0:L], gb[:], ALU.mult, [put, gbt], [("macc", dc), ("aT", 16 + 2 * dc), ("aT", 17 + 2 * dc)])
                else:
                    TT("dve", mtmp[:], pu[:, 0:L], gb[:], ALU.mult, [put, gbt], ["mtmp"])
                    if j == 1:
                        TT("pool", macc[:, dc, :], macc[:, dc, :], mtmp[:], ALU.add, ["mtmp", ("macc", dc)], [("macc", dc), ("aT", 16 + 2 * dc), ("aT", 17 + 2 * dc)])
                    else:
                        TT("pool", mT[:, dc, :], macc[:, dc, :], mtmp[:], ALU.add, ["mtmp", ("macc", dc)], [("mT", dc)])
        MT = [("mT", dc) for dc in range(8)]
        S.tag = "I_outproj"
        for t in range(NT):
            gt = st * NT + t
            DMA("sp", hres[:, t, :], xc[gt * 128:(gt + 1) * 128, :], [], [("hres", t, 0), ("hres", t, 1)], ("hresx", t))
        for dh in range(2):
            wo, wt_ = wfetch(("out", dh))
            for t in range(NT):
                ph, pht = nextG()
                for k in range(8):
                    MM(ph[:, 0:512], mT[:, k, t * 128:(t + 1) * 128], wo[:, k, :], k == 0, k == 7, [wt_] + MT, [pht])
                TT("dve", hres[:, t, dh * 512:(dh + 1) * 512], ph[:, 0:512], hres[:, t, dh * 512:(dh + 1) * 512], ALU.add, [pht, ("hres", t, dh)], [("hres", t, dh)])
        S.tag = "J_ffn"
        for t in range(NT):
            rmsnorm_to_T(hres[:, t, :], [("hres", t, 0), ("hres", t, 1)], gffn_r, "gffn_r", (hnT if USE_HNT else XN[0]), slice(t * 128, (t + 1) * 128), ("hnT" if USE_HNT else XT[0]))
        for fb in range(8):
            w1, wt_ = wfetch(("ff1", fb))
            for fcl in range(4):
                fc = fb * 4 + fcl
                pa_, pat_ = nextG()
                for k in range(8):
                    MM(pa_[:, 0:L], w1[:, k, fcl * 128:(fcl + 1) * 128], (hnT if USE_HNT else XN[0])[:, k, :], k == 0, k == 7, [wt_, ("hnT" if USE_HNT else XT[0])], [pat_])
                ri = cnt2["r"] % 2
                cnt2["r"] += 1
                rb, rbt = rtmp[ri], ("rtmp", ri)
                if os.environ.get("K_RELU", "1") == "1":
                    TS("dve", rb[:], pa_[:, 0:L], 0.0, ALU.max, [pat_], [rbt])
                else:
                    ACT(rb[:], pa_[:, 0:L], AF.Relu, [pat_], [rbt])
                TT("pool", aT[:, fc, :], rb[:], rb[:], ALU.mult, [rbt], [("aT", fc)])
        for dh in range(2):
            pf = [nextC() for t in range(NT)]
            for fg in range(4):
                w2, wt_ = wfetch(("ff2", dh, fg))
                for t in range(NT):
                    po, pot = pf[t]
                    for k in range(8):
                        fc = fg * 8 + k
                        MM(po[:, 0:512], aT[:, fc, t * 128:(t + 1) * 128], w2[:, k, :], fc == 0, fc == 31, [wt_, ("aT", fc)], [pot])
            for t in range(NT):
                po, pot = pf[t]
                TT("dve", hres[:, t, dh * 512:(dh + 1) * 512], po[:, 0:512], hres[:, t, dh * 512:(dh + 1) * 512], ALU.add, [pot, ("hres", t, dh)], [("hres", t, dh)])
        S.tag = "K_final"
        for t in range(NT):
            gt = st * NT + t
            ht = [("hres", t, 0), ("hres", t, 1)]
            col, tok = rms_scale(hres[:, t, :], ht)
            STT(hres[:, t, :], hres[:, t, :], col, gfin_r[:], ALU.mult, ALU.mult, ht + [tok, "gfin_r"], ht)
            orow = (gt - NTILE // 2) * 128
            DMA("sp", y_out[orow:orow + 128, :], hres[:, t, :], ht, [], ("yout", t))

    for st in range(NST if os.environ.get('K_NOLOOP', '0') == '0' else 0):
        XN[0] = xnTs[st % NXN]
        XT[0] = ("xnT", st % NXN)
        own = st >= NSTP
        do_q = own or st == NSTP - 1
        pfx = not own
        fcol = flag2[:, 0:1] if pfx else ones2[:, 0:1]
        fsrc2 = flag2 if pfx else ones2
        ftok = "flag2" if pfx else "ones2"
        btile = biasTMf if pfx else biasTM
        btok = "biasTMf" if pfx else "biasTM"
        S.tag = "A_norm"
        for t in range(NT):
            gt = st * NT + t
            xi = nextX()
            DMA("sp", xbuf[xi][:], xc[gt * 128:(gt + 1) * 128, :], [], [("xbuf", xi)], ("xbuf", xi))
            rmsnorm_to_T(xbuf[xi][:], [("xbuf", xi)], gmix_r, "gmix_r", XN[0], slice(t * 128, (t + 1) * 128), XT[0])
        if st == 1:
            late_tables()
        if st == 2:
            S.tag = "setup"
            mem_setup()
            dbg("mkT", mkT[:], ["mkT"], [128, 4, 256])
            dbg("mv_aug", mv_aug[:], ["mv_aug"], [128, 2, 4, 129])
            S.tag = "A_norm"
        if st == NSTP:
            dbg(XT[0], XN[0][:], [XT[0]], [128, 8, L])
        S.tag = "B_ctxproj"
        wk_, wt_ = wfetch("mlk")
        for c in range(4):
            pgx, pgxt = fm_group(wk_, wt_, c * 128)
            conv_silu(prek, "prek", pgx, pgxt, c, FMROW[("mlk", c)], 4 + c, kT, "kT")
        for t in range(NT):
            pt, ptt = nextT()
            for c in range(4):
                TR(pt[:, c * 128:(c + 1) * 128], kT[:, c, t * 128:(t + 1) * 128], identb[:], ["kT", "identb"], [ptt])
            CP("dve" if os.environ.get("K_KTM", "1") == "1" else "act", ktm[:, t, :], pt, [ptt], [("ktm", t)])
        wv_, wt_ = wfetch("mlv")
        for t in range(NT):
            pgx, pgxt = nextG()
            for k in range(8):
                MM(pgx[:, 0:512], XN[0][:, k, t * 128:(t + 1) * 128], wv_[:, k, :], k == 0, k == 7, [wt_, XT[0]], [pgxt])
            TT("dve", v_aug[:, t, :, 0:128], pgx[:, 0:512].rearrange("p (h d) -> p h d", h=4), biasTM[:, 0:512].rearrange("p (h d) -> p h d", h=4), ALU.add,
               [pgxt, "biasTM", ("v_aug", t)], [("v_aug", t)])
        wc_, wt_ = wfetch("cbuf")
        pgx, pgxt = fm_group(wc_, wt_, 0)
        bi = FMROW["ks"]
        EVB(ksT[:, st * L:(st + 1) * L], pgx[:, 0:L], biasFM[:, bi:bi + 1], [pgxt, "biasFM"], [("ksT", st)])
        need_win = st >= NSTP - (512 // L)
        if need_win:
            pgx, pgxt = fm_group(wc_, wt_, 128)
            bi = FMROW["kw"]
            ro = (st * L) % 1024
            EVB(kwT[:, ro:ro + L], pgx[:, 0:L], biasFM[:, bi:bi + 1], [pgxt, "biasFM"], [("kwT", (ro // 128) // NT)])

        def kc_evac(pgx, pgxt, kv, g):
            bi3 = FMROW[("kc", kv, g)]
            idx = kv * 2 + g
            ACT(kcin2[0:64, idx, 16:16 + L], pgx[0:64, 0:L], AF.Identity, [pgxt, "biasFM", "kcin2"], ["kcin2"], bias=biasFM[0:64, bi3:bi3 + 1])
            ACT(kcin2[64:128, idx, 17:17 + L], pgx[64:128, 0:L], AF.Identity, [pgxt, "biasFM", "kcin2"], ["kcin2"], bias=biasFM[64:128, bi3:bi3 + 1])
        for g in range(2):
            pgx, pgxt = fm_group(wc_, wt_, 256 + g * 128)
            kc_evac(pgx, pgxt, 0, g)
        wd_, wt_ = wfetch("dbuf")
        for g in range(2):
            pgx, pgxt = fm_group(wd_, wt_, g * 128)
            kc_evac(pgx, pgxt, 1, g)
        for t in range(NT):
            gt = st * NT + t
            slot = gt % 8
            vst = ("vs_aug", gt // NT)
            vwt = ("vw_aug", slot // NT)
            pgx, pgxt = nextG()
            for k in range(8):
                MM(pgx[:, 0:288], XN[0][:, k, t * 128:(t + 1) * 128], wd_[:, k, 256:544], k == 0, k == 7, [wt_, XT[0]], [pgxt])
            STT(vs_aug[:, gt, :, 0:64], pgx[:, 0:128].rearrange("p (g d) -> p g d", g=2), fcol, (btile[:, 0:128] if pfx else btile[:, 512:640]).rearrange("p (g d) -> p g d", g=2), ALU.mult, ALU.add,
                [pgxt, ftok, btok, vst], [vst])
            CP("dve", vs_aug[:, gt, :, 64], fsrc2[:, 0:2], [ftok, vst], [vst])
            if need_win:
                STT(vw_aug[:, slot, :, 0:64], pgx[:, 128:256].rearrange("p (g d) -> p g d", g=2), fcol, (btile[:, 128:256] if pfx else btile[:, 640:768]).rearrange("p (g d) -> p g d", g=2), ALU.mult, ALU.add,
                    [pgxt, ftok, btok, vwt], [vwt])
                CP("dve", vw_aug[:, slot, :, 64], fsrc2[:, 0:2], [ftok, vwt], [vwt])
            TT("dve", gate[:, t, :], pgx[:, 256:264], biasTM[:, 768:776], ALU.add, [pgxt, "biasTM"], [("gate", t)])
            if own:
                TT("dve", gsig[:, t, :], pgx[:, 264:288], biasTM[:, 776:800], ALU.add, [pgxt, "biasTM"], [("gsig", t)])
                SIGT(gsig[:, t, :], gsig[:, t, :], [("gsig", t)], [("gsig", t)])
        S.tag = "C_compress"
        compress_st(st)
        if st < NSTP - 1:
            per_st = (len(conv_rest) + NSTP - 2) // (NSTP - 1)
            for idx in conv_rest[st * per_st:(st + 1) * per_st]:
                emit_conv(idx, [("w1done", st)])
        if st == NSTP - 1:
            for idx in conv_ffn[0:8]:
                emit_conv(idx, [("w1done", st)])
        if st == (NSTP if os.environ.get("K_CS", "1") == "1" else NSTP - 1):
            for idx in conv_ffn[8:16]:
                emit_conv(idx, [("w1done", st)])
        S.tag = "D_qproj"
        if do_q:
            wq_, wt_ = wfetch("mlq")
            for c in range(4):
                pgx, pgxt = fm_group(wq_, wt_, c * 128)
                conv_silu(preq, "preq", pgx, pgxt, c, FMROW[("mlq", c)], c, qT, "qT")
        if st == NSTP - 1:
            for c in range(4):
                TS("dve", prek[:, c, :], prek[:, c, :], flag2[:, 0:1], ALU.mult, [("prek", c), "flag2"], [("prek", c)])
                TS("dve", preq[:, c, :], preq[:, c, :], flag2[:, 0:1], ALU.mult, [("preq", c), "flag2"], [("preq", c)])
        if own:
            wo_, wt_ = wfetch("mlo")
            for t in range(NT):
                pgx, pgxt = nextG()
                for k in range(8):
                    MM(pgx[:, 0:512], XN[0][:, k, t * 128:(t + 1) * 128], wo_[:, k, :], k == 0, k == 7, [wt_, XT[0]], [pgxt])
                TT("dve", otmp, pgx[:, 0:512], biasTM[:, 800:1312], ALU.add, [pgxt, "biasTM"], [("Rm", 0)])
                SIGT(otmp, otmp, [("Rm", 0)], [("Rm", 0)])
                TT("dve", og[:, t, :], otmp, gml_r[:], ALU.mult, [("Rm", 0), "gml_r"], [("og", t)])
            v4, wt_ = wfetch("nsq")
            for p_ in range(4):
                pgx, pgxt = nextG()
                for k in range(8):
                    MM(pgx[:, 0:L], v4[:, k, p_ * 128:(p_ + 1) * 128], XN[0][:, k, :], k == 0, k == 7, [wt_, XT[0]], [pgxt])
                bq = FMROW[("nsq", p_)]
                ACT(nqT[0:64, :, p_, :], pgx[0:64, 0:L].rearrange("p (t q) -> p t q", t=NT), AF.Identity, [pgxt, "biasFM", "nqT"], ["nqT"],
                    bias=biasFM[0:64, bq:bq + 1], scale=0.125)
                ACT(nqT[64:128, :, p_ + 4, :], pgx[64:128, 0:L].rearrange("p (t q) -> p t q", t=NT), AF.Identity, [pgxt, "biasFM", "nqT"], ["nqT"],
                    bias=biasFM[64:128, bq:bq + 1], scale=0.125)
            wx_, wt_ = wfetch("xaq")
            for c in range(4):
                pgx, pgxt = fm_group(wx_, wt_, c * 128)
                bx = FMROW[("xaq", c)]
                EVB(xqT[:, c, :], pgx[:, 0:L], biasFM[:, bx:bx + 1], [pgxt, "biasFM"], ["xqT"])
            oi = st - NSTP
            for t in range(NT):
                ti = oi * NT + t
                DMA("sp", NA[64:68, t, :, :], t_qaug[:, ti, :, :], [], ["NAaug"], "NAaug")
        S.tag = "E_mlstm"
        mlstm_gates(st)
        for t in range(NT):
            S.tag = "E_mlstm"
            mlstm_tile(st, t, own)
            if own:
                S.tag = "F_nsa"
                nsa_tile(st, t)
                S.tag = "G_xa"
                xa_tile(st, t)
        if own:
            if st == NSTP:
                dbg("yT", yT[:], [("yT", j, t) for j in range(3) for t in range(NT)], [128, 12, L])
                dbg("Cst", Cst[:], ["Cst"], [128, 4, 129])
                dbg("kcT", kcT[:], ["kcT"], [128, 256])
                dbg("vcT", vcT[:], ["vcT"], [128, 256])
                dbg("kT", kT[:], ["kT"], [128, 4, L])
                dbg("qT", qT[:], ["qT"], [128, 4, L])
                dbg("nqT", nqT[:], ["nqT"], [128, NT, 8, 128])
                dbg("gsig", gsig[:], [("gsig", t) for t in range(NT)], [128, NT, 24])
                dbg("vs_aug", vs_aug[:], [("vs_aug", i) for i in range(NST)], [128, NTILE, 2, 65])
                dbg("ksT", ksT[:], [("ksT", i) for i in range(NST)], [128, SEQ])
                dbg("score", score[:], ["score"], [128, 64])
                dbg("negm", negm[:], ["negm"], [128, 64])
            own_tail(st)
            if st == NSTP:
                dbg("mT", mT[:], [("mT", dc) for dc in range(8)], [128, 8, L])
                dbg("hres", hres[:], [("hres", t, dh) for t in range(NT) for dh in range(2)], [128, NT, D])
        if stop_after is not None and st == stop_after:
            break

    S.finalize(es)
    build.stats = dict(ops=len(S.ops), waits=S.nwaits, sems=S.nsems, est_us=getattr(S, "est_total", None))
    build.sched = S
    return nc, es


def _bf(a):
    return np.asarray(a, dtype=np.float32).astype(ml_dtypes.bfloat16)


def make_tables(s):
    T = {}
    T["t_flag"] = np.zeros((128, 2), np.float32) + (1.0 if s == 1 else 0.0)
    T["t_identf"] = np.eye(128, dtype=np.float32)
    ii = np.arange(128)
    T["t_U"] = (ii[:, None] <= ii[None, :]).astype(np.float32)
    first_valid_tok = 0 if s == 1 else HALF
    n = np.arange(256)
    q = HALF + np.arange(HALF)
    vis = (16 * n[:, None] + 31 <= q[None, :]) & (16 * n[:, None] >= first_valid_tok) & (n[:, None] < 255)
    okc = np.where(vis, 0.0, NEGM).astype(np.float32)
    T["t_okc"] = _bf(okc.reshape(2, 128, HALF).transpose(1, 0, 2))
    j = np.arange(64)
    cur = q // 64
    fb = first_valid_tok // 64
    ok = (j[None, :] <= cur[:, None]) & (j[None, :] >= fb)
    forced = (j[None, :] == fb) | (j[None, :] == cur[:, None]) | ((j[None, :] == cur[:, None] - 1) & (j[None, :] >= fb))
    bs = np.where(ok, np.where(forced, 1000.0, 0.0), -1e30).astype(np.float32)
    T["t_bsel"] = _bf(np.ascontiguousarray(bs.reshape(16, 128, 64).transpose(1, 0, 2)))
    EA = np.zeros((68, NTILE, 128), np.float32)
    for kt in range(NTILE):
        EA[2 * kt, kt, 0:64] = 1.0
        EA[2 * kt + 1, kt, 64:128] = 1.0
        EA[64, kt, :] = kt
        EA[65, kt, :] = ii
        EA[66, kt, :] = 1.0
        EA[67, kt, :] = 1.0
    T["t_EA"] = _bf(EA)
    CA = np.zeros((68, 256), np.float32)
    pos = 16 * n + 31
    CA[64] = pos // 128
    CA[65] = pos % 128
    CA[66] = 1.0
    CA[67] = 1.0
    T["t_CA"] = _bf(CA)
    slopes = 2.0 ** (-8.0 * np.arange(1, 9) / 8)
    qa = np.zeros((4, 16, 2, 4, 128), np.float32)
    for ti in range(16):
        tq0 = HALF + ti * 128
        for g in range(2):
            for r in range(4):
                sl = slopes[4 * g + r]
                qa[0, ti, g, r, :] = 128.0 * sl
                qa[1, ti, g, r, :] = sl
                qa[2, ti, g, r, :] = -sl * tq0
                qa[3, ti, g, r, :] = -sl * ii
    T["t_qaug"] = _bf(qa.reshape(4, 16, 2, 512))
    caus = np.where(ii[:, None] <= ii[None, :], 0.0, NEGM).astype(np.float32)
    T["t_caus"] = _bf(np.tile(caus, (1, 4)))
    win = np.where(ii[:, None] > ii[None, :], 0.0, NEGM).astype(np.float32)
    T["t_win"] = _bf(np.tile(win, (1, 4)))
    cs = n * 16
    js = np.arange(63) * 64
    ov = ((cs[:, None] < js[None, :] + 64) & (cs[:, None] + 32 > js[None, :]) & (n[:, None] < 255)).astype(np.float32)
    T["t_ovl"] = _bf(ov.reshape(2, 128, 63).transpose(1, 0, 2))
    return T


_CACHE = {}


def kernel(x, mem, g_mix, w_in, b_in, ml_conv, ml_norm_g, cmp_pe, cmp_w1, cmp_w2, g_mem, w_mem_kv,
           w_branch, w_out, g_ffn, w_ff1, w_ff2, g_final, _debug=False, _stop_after=None, _cores=8):
    f = lambda a: np.ascontiguousarray(np.asarray(a, dtype=np.float32))
    x = f(x)
    mem = f(mem)
    shared = {
        "w_in": f(w_in)[0], "b_in": f(b_in).reshape(1, D_IN), "ml_conv": f(ml_conv)[0], "ml_norm_g": f(ml_norm_g).reshape(1, 512),
        "cmp_pe": f(cmp_pe)[0], "cmp_w1": f(cmp_w1)[0], "cmp_w2": f(cmp_w2)[0],
        "g_mix": f(g_mix).reshape(1, D), "g_mem": f(g_mem).reshape(1, D), "g_ffn": f(g_ffn).reshape(1, D), "g_final": f(g_final).reshape(1, D),
        "w_mem_kv": f(w_mem_kv)[0], "w_branch": f(w_branch)[0], "w_out": f(w_out)[0], "w_ff1": f(w_ff1)[0], "w_ff2": f(w_ff2)[0],
    }
    key = (_debug, _stop_after)
    if key not in _CACHE:
        _CACHE[key] = build(debug=_debug, stop_after=_stop_after)
    nc, _es = _CACHE[key]
    tabs = [make_tables(0), make_tables(1)]
    in_maps = []
    for core in range(_cores):
        b, s = core // 2, core % 2
        if s == 1:
            xcore = x[b]
        else:
            xcore = np.concatenate([x[b, :HALF], x[b, :HALF]], axis=0)
        m = dict(shared)
        m["xc"] = np.ascontiguousarray(xcore)
        m["mem"] = mem[b]
        m.update(tabs[s])
        in_maps.append(m)
    res = run_bass_kernel_spmd(nc, in_maps, core_ids=list(range(_cores)))
    out = np.zeros((4, SEQ, D), np.float32)
    for core in range(_cores):
        b, s = core // 2, core % 2
        out[b, s * HALF:(s + 1) * HALF] = res.results[core]["y"]
    if _debug:
        kernel.last = res.results
    return out
```
